# Optimizing a Trainium2 kernel written in Bass

```python
import math
import jax, jax.numpy as jnp
from jax import lax
import numpy as np

D_MODEL = 2048
BATCH = 2
SEQ = 8192
DEPTH = 4

HEAD_DIM = 128
N_RET_HEADS = D_MODEL // (2 * HEAD_DIM)
N_NA_HEADS = D_MODEL // (2 * HEAD_DIM)
N_DIFF_HEADS = D_MODEL // (2 * HEAD_DIM)
N_X_HEADS = 4
X_HEAD_DIM = 128
N_MEM = 256
GRID_W = 64
NA_WIN_ROWS = 8
NA_WIN_COLS = 16
RET_CHUNK = 128
ATTN_BLOCK = 128
FFN_HIDDEN = -((-8 * D_MODEL) // (3 * 256)) * 256
RMS_EPS = 1e-6

kernel_name = "hybrid_retention_na_diffattn_encoder"


def rms_norm(x, g):
    xf = x.astype(jnp.float32)
    y = xf * lax.rsqrt(jnp.mean(xf * xf, axis=-1, keepdims=True) + RMS_EPS)
    return (y * g.astype(jnp.float32)).astype(x.dtype)


def alibi_slopes(n):
    return 2.0 ** (-8.0 * jnp.arange(1, n + 1, dtype=jnp.float32) / n)


def chunk_retention(q, k, v, log_gamma, include_diag):
    b, h, s, d = q.shape
    dv = v.shape[-1]
    n = s // RET_CHUNK

    def to_chunks(t):
        return t.reshape(b, h, n, RET_CHUNK, t.shape[-1]).transpose(2, 0, 1, 3, 4)

    idx = jnp.arange(RET_CHUNK, dtype=jnp.float32)
    diff = idx[:, None] - idx[None, :]
    mask = (diff >= 0) if include_diag else (diff > 0)
    d_intra = jnp.where(mask, jnp.exp(log_gamma[:, None, None] * jnp.maximum(diff, 0.0)), 0.0)
    q_decay = jnp.exp(log_gamma[:, None] * (idx + 1.0))[:, :, None]
    k_decay = jnp.exp(log_gamma[:, None] * (RET_CHUNK - 1.0 - idx))[:, :, None]
    c_decay = jnp.exp(log_gamma * RET_CHUNK)[:, None, None]

    def step(state, qkv):
        qc, kc, vc = qkv
        scores = jnp.einsum('bhid,bhjd->bhij', qc, kc) * d_intra
        inner = jnp.einsum('bhij,bhje->bhie', scores, vc)
        cross = jnp.einsum('bhid,bhde->bhie', qc * q_decay, state)
        state = state * c_decay + jnp.einsum('bhjd,bhje->bhde', kc * k_decay, vc)
        return state, inner + cross

    state0 = jnp.zeros((b, h, d, dv), jnp.float32)
    _, out = lax.scan(step, state0, (to_chunks(q), to_chunks(k), to_chunks(v)))
    return out.transpose(1, 2, 0, 3, 4).reshape(b, h, s, dv)


def neighborhood_attention(q, k, v, rpb):
    b, h, s, d = q.shape
    rows = s // GRID_W
    wr = min(NA_WIN_ROWS, rows)
    qg = q.reshape(b, h, rows, GRID_W, d)
    kg = k.reshape(b, h, rows, GRID_W, d)
    vg = v.reshape(b, h, rows, GRID_W, d)
    col = jnp.arange(GRID_W)
    cs = jnp.clip(col - NA_WIN_COLS // 2, 0, GRID_W - NA_WIN_COLS)
    colidx = cs[:, None] + jnp.arange(NA_WIN_COLS)[None, :]
    dc_idx = colidx - col[:, None] + NA_WIN_COLS - 1
    scale = HEAD_DIM ** -0.5

    def row_fn(r):
        rs = jnp.clip(r - wr // 2, 0, rows - wr)
        krow = lax.dynamic_slice_in_dim(kg, rs, wr, axis=2)
        vrow = lax.dynamic_slice_in_dim(vg, rs, wr, axis=2)
        kw = krow[:, :, :, colidx, :]
        vw = vrow[:, :, :, colidx, :]
        qr = lax.dynamic_index_in_dim(qg, r, axis=2, keepdims=False)
        logits = jnp.einsum('bhcd,bhrcjd->bhcrj', qr, kw).astype(jnp.float32) * scale
        dr_idx = rs + jnp.arange(wr) - r + NA_WIN_ROWS - 1
        bias = rpb[:, dr_idx][:, :, dc_idx].astype(jnp.float32)
        logits = logits + bias.transpose(0, 2, 1, 3)[None]
        p = jax.nn.softmax(logits.reshape(b, h, GRID_W, wr * NA_WIN_COLS), axis=-1)
        p = p.reshape(b, h, GRID_W, wr, NA_WIN_COLS)
        return jnp.einsum('bhcrj,bhrcjd->bhcd', p, vw.astype(jnp.float32)).astype(v.dtype)

    out = lax.map(row_fn, jnp.arange(rows))
    return out.transpose(1, 2, 0, 3, 4).reshape(b, h, s, d)


def retention_na_mixer(h, w_in, dec_f, dec_b, ret_g, na_q_g, na_k_g, rpb, w_out):
    b, s, _ = h.shape
    proj = h @ w_in
    rq, rk, rv, rg, nq, nk, nv = jnp.split(proj, 7, axis=-1)

    def heads(t):
        return t.reshape(b, s, -1, HEAD_DIM).transpose(0, 2, 1, 3)

    f32 = jnp.float32
    rq = heads(rq).astype(f32)
    rk = heads(rk).astype(f32) * (HEAD_DIM ** -0.5)
    rv = heads(rv).astype(f32)
    lg_f = jax.nn.log_sigmoid(dec_f.astype(f32))
    lg_b = jax.nn.log_sigmoid(dec_b.astype(f32))
    fwd = chunk_retention(rq, rk, rv, lg_f, True)
    bwd = jnp.flip(chunk_retention(jnp.flip(rq, 2), jnp.flip(rk, 2), jnp.flip(rv, 2), lg_b, False), 2)
    ret = rms_norm(fwd + bwd, ret_g)
    ret = ret.transpose(0, 2, 1, 3).reshape(b, s, -1).astype(h.dtype) * jax.nn.silu(rg)

    na = neighborhood_attention(rms_norm(heads(nq), na_q_g), rms_norm(heads(nk), na_k_g), heads(nv), rpb)
    na = na.transpose(0, 2, 1, 3).reshape(b, s, -1)
    return jnp.concatenate([ret, na], axis=-1) @ w_out


def diff_attention_mixer(h, w_in, q_g, k_g, lq1, lk1, lq2, lk2, out_g, w_out, layer_idx):
    b, s, _ = h.shape
    f32 = jnp.float32
    proj = h @ w_in
    q, k, v = jnp.split(proj, 3, axis=-1)
    q = rms_norm(q.reshape(b, s, N_DIFF_HEADS, 2, HEAD_DIM), q_g).transpose(3, 0, 2, 1, 4)
    k = rms_norm(k.reshape(b, s, N_DIFF_HEADS, 2, HEAD_DIM), k_g).transpose(3, 0, 2, 1, 4)
    vf = v.reshape(b, s, N_DIFF_HEADS, 2 * HEAD_DIM).transpose(0, 2, 1, 3).astype(f32)
    lam_init = 0.8 - 0.6 * math.exp(-0.3 * layer_idx)
    lam = (jnp.exp(jnp.sum(lq1.astype(f32) * lk1.astype(f32)))
           - jnp.exp(jnp.sum(lq2.astype(f32) * lk2.astype(f32))) + lam_init)
    slopes = alibi_slopes(N_DIFF_HEADS)
    scale = HEAD_DIM ** -0.5
    nb = s // ATTN_BLOCK
    qb = q.reshape(2, b, N_DIFF_HEADS, nb, ATTN_BLOCK, HEAD_DIM).transpose(3, 0, 1, 2, 4, 5)
    key_pos = jnp.arange(s, dtype=f32)

    def block_fn(args):
        qblk, bi = args
        qpos = (bi * ATTN_BLOCK + jnp.arange(ATTN_BLOCK)).astype(f32)
        alibi = -slopes[:, None, None] * jnp.abs(qpos[:, None] - key_pos[None, :])
        logits = jnp.einsum('cbhqd,cbhkd->cbhqk', qblk, k).astype(f32) * scale + alibi
        p = jax.nn.softmax(logits, axis=-1)
        a = p[0] - lam * p[1]
        return jnp.einsum('bhqk,bhke->bhqe', a, vf)

    out = lax.map(block_fn, (qb, jnp.arange(nb)))
    out = out.transpose(1, 2, 0, 3, 4).reshape(b, N_DIFF_HEADS, s, 2 * HEAD_DIM)
    out = rms_norm(out, out_g) * (1.0 - lam_init)
    out = out.transpose(0, 2, 1, 3).reshape(b, s, -1).astype(h.dtype)
    return out @ w_out


def memory_cross_attention(h, m, w_q, w_kv, w_o, q_g, k_g):
    b, s, _ = h.shape
    nm = m.shape[1]
    q = rms_norm((h @ w_q).reshape(b, s, N_X_HEADS, X_HEAD_DIM), q_g)
    kv = (m @ w_kv).reshape(b, nm, 2, N_X_HEADS, X_HEAD_DIM)
    k = rms_norm(kv[:, :, 0], k_g)
    v = kv[:, :, 1]
    logits = jnp.einsum('bshd,bmhd->bhsm', q, k).astype(jnp.float32) * (X_HEAD_DIM ** -0.5)
    p = jax.nn.softmax(logits, axis=-1)
    o = jnp.einsum('bhsm,bmhd->bshd', p, v.astype(jnp.float32)).astype(h.dtype)
    return o.reshape(b, s, -1) @ w_o


def swiglu_ffn(h, w_in, w_out):
    g, u = jnp.split(h @ w_in, 2, axis=-1)
    return (jax.nn.silu(g) * u) @ w_out


def setup_inputs(seed: int = 0) -> dict:
    key = jax.random.key(seed)
    ks = list(jax.random.split(key, 32))
    n_even = (DEPTH + 1) // 2
    n_odd = DEPTH // 2
    f32 = jnp.float32

    def w(k, shape, fan_in):
        return jax.random.normal(k, shape, f32) * (fan_in ** -0.5)

    def gain(k, shape):
        return 1.0 + 0.02 * jax.random.normal(k, shape, f32)

    def small(k, shape, sc):
        return sc * jax.random.normal(k, shape, f32)

    ab_width = (N_RET_HEADS + N_NA_HEADS) * HEAD_DIM
    ab_in_cols = 4 * N_RET_HEADS * HEAD_DIM + 3 * N_NA_HEADS * HEAD_DIM
    c_width = N_DIFF_HEADS * 2 * HEAD_DIM
    x_width = N_X_HEADS * X_HEAD_DIM
    base_logit = jnp.asarray(np.log(2.0 ** (5 + np.arange(N_RET_HEADS)) - 1.0), f32)

    return {
        "x": jax.random.normal(ks[0], (BATCH, SEQ, D_MODEL), f32),
        "mem": jax.random.normal(ks[1], (BATCH, N_MEM, D_MODEL), f32),
        "norm_mix_g": gain(ks[2], (DEPTH, D_MODEL)),
        "norm_xattn_g": gain(ks[3], (DEPTH, D_MODEL)),
        "norm_mem_g": gain(ks[4], (DEPTH, D_MODEL)),
        "norm_ffn_g": gain(ks[5], (DEPTH, D_MODEL)),
        "w_in_ab": w(ks[6], (n_even, D_MODEL, ab_in_cols), D_MODEL),
        "ret_decay_fwd": base_logit[None] + small(ks[7], (n_even, N_RET_HEADS), 0.05),
        "ret_decay_bwd": base_logit[None] + small(ks[8], (n_even, N_RET_HEADS), 0.05),
        "ret_out_g": gain(ks[9], (n_even, HEAD_DIM)),
        "na_q_g": gain(ks[10], (n_even, HEAD_DIM)),
        "na_k_g": gain(ks[11], (n_even, HEAD_DIM)),
        "na_rpb": small(ks[12], (n_even, N_NA_HEADS, 2 * NA_WIN_ROWS - 1, 2 * NA_WIN_COLS - 1), 0.1),
        "w_out_ab": w(ks[13], (n_even, ab_width, D_MODEL), ab_width),
        "w_in_c": w(ks[14], (n_odd, D_MODEL, 3 * c_width), D_MODEL),
        "diff_q_g": gain(ks[15], (n_odd, HEAD_DIM)),
        "diff_k_g": gain(ks[16], (n_odd, HEAD_DIM)),
        "lambda_q1": small(ks[17], (n_odd, HEAD_DIM), 0.1),
        "lambda_k1": small(ks[18], (n_odd, HEAD_DIM), 0.1),
        "lambda_q2": small(ks[19], (n_odd, HEAD_DIM), 0.1),
        "lambda_k2": small(ks[20], (n_odd, HEAD_DIM), 0.1),
        "diff_out_g": gain(ks[21], (n_odd, 2 * HEAD_DIM)),
        "w_out_c": w(ks[22], (n_odd, c_width, D_MODEL), c_width),
        "w_xq": w(ks[23], (DEPTH, D_MODEL, x_width), D_MODEL),
        "w_xkv": w(ks[24], (DEPTH, D_MODEL, 2 * x_width), D_MODEL),
        "w_xo": w(ks[25], (DEPTH, x_width, D_MODEL), x_width),
        "xq_g": gain(ks[26], (DEPTH, X_HEAD_DIM)),
        "xk_g": gain(ks[27], (DEPTH, X_HEAD_DIM)),
        "w_ffn_in": w(ks[28], (DEPTH, D_MODEL, 2 * FFN_HIDDEN), D_MODEL),
        "w_ffn_out": w(ks[29], (DEPTH, FFN_HIDDEN, D_MODEL), FFN_HIDDEN),
    }


def reference(x, mem, norm_mix_g, norm_xattn_g, norm_mem_g, norm_ffn_g,
              w_in_ab, ret_decay_fwd, ret_decay_bwd, ret_out_g, na_q_g, na_k_g, na_rpb, w_out_ab,
              w_in_c, diff_q_g, diff_k_g, lambda_q1, lambda_k1, lambda_q2, lambda_k2, diff_out_g, w_out_c,
              w_xq, w_xkv, w_xo, xq_g, xk_g, w_ffn_in, w_ffn_out):
    for i in range(DEPTH):
        j = i // 2
        h = rms_norm(x, norm_mix_g[i])
        if i % 2 == 0:
            x = x + retention_na_mixer(h, w_in_ab[j], ret_decay_fwd[j], ret_decay_bwd[j], ret_out_g[j],
                                       na_q_g[j], na_k_g[j], na_rpb[j], w_out_ab[j])
        else:
            x = x + diff_attention_mixer(h, w_in_c[j], diff_q_g[j], diff_k_g[j], lambda_q1[j], lambda_k1[j],
                                         lambda_q2[j], lambda_k2[j], diff_out_g[j], w_out_c[j], i)
        m = rms_norm(mem, norm_mem_g[i])
        x = x + memory_cross_attention(rms_norm(x, norm_xattn_g[i]), m, w_xq[i], w_xkv[i], w_xo[i], xq_g[i], xk_g[i])
        x = x + swiglu_ffn(rms_norm(x, norm_ffn_g[i]), w_ffn_in[i], w_ffn_out[i])
    return x
```

```python
import numpy as np
import ml_dtypes
from contextlib import ExitStack
import concourse.bass as bass
import concourse.mybir as mybir
from concourse.bass_utils import run_bass_kernel_spmd

F32 = mybir.dt.float32
BF16 = mybir.dt.bfloat16
AF = mybir.ActivationFunctionType
ALU = mybir.AluOpType
NPBF = ml_dtypes.bfloat16

NCORES = 8
D = 2048
KC = 16
S = 8192
T = 2048
TB = 512
NTB = T // TB
FH = 5632
FC = 44
EPS = 1e-6
SAME_ENGINE_SYNC = True


class Buf:
    __slots__ = ("name", "writers", "readers", "war")

    def __init__(self, name):
        self.name = name
        self.writers = []
        self.readers = []
        self.war = []


class Op:
    __slots__ = ("eng", "idx", "fn", "deps", "dma", "semkey", "count", "signal")


class KB:
    ENGS = ("pe", "act", "dve", "pool", "sync")

    def __init__(self):
        self.nc = bass.Bass("TRN2", target_bir_lowering=False)
        self.ops = {e: [] for e in self.ENGS}
        self.stack = ExitStack()
        self.dma_count = {}
        self.nbuf = 0

    def dram(self, name, shape, dtype, kind):
        return self.nc.dram_tensor(name, list(shape), dtype, kind=kind).ap()

    def sb(self, name, shape, dtype):
        return self.stack.enter_context(self.nc.sbuf_tensor(name, list(shape), dtype))

    def ps(self, name, shape, dtype=F32):
        return self.stack.enter_context(self.nc.psum_tensor(name, list(shape), dtype))

    def buf(self, name=None):
        self.nbuf += 1
        return Buf(name or f"b{self.nbuf}")

    def op(self, eng, fn, reads=(), writes=(), dma=False):
        o = Op()
        o.eng = eng
        o.fn = fn
        o.dma = dma
        o.signal = False
        o.count = None
        o.semkey = None
        deps = []
        for b in reads:
            deps.extend(b.writers)
        for b in writes:
            if b.readers:
                b.war = b.readers
                b.readers = []
                b.writers = []
            deps.extend(b.war)
        best = {}
        dmas = []
        for d in deps:
            if d.dma:
                dmas.append(d)
            else:
                if d.eng == eng and not dma and (eng == "pe" or not SAME_ENGINE_SYNC):
                    continue
                if d.eng not in best or d.idx > best[d.eng].idx:
                    best[d.eng] = d
        o.deps = list(best.values()) + list({id(d): d for d in dmas}.values())
        for d in o.deps:
            d.signal = True
        o.idx = len(self.ops[eng])
        self.ops[eng].append(o)
        if dma:
            key = writes[0]
            o.semkey = key
            self.dma_count[id(key)] = self.dma_count.get(id(key), 0) + 16
            o.count = self.dma_count[id(key)]
            o.signal = True
        for b in reads:
            b.readers.append(o)
        for b in writes:
            b.writers.append(o)
        return o

    def dma(self, out, in_, reads=(), writes=(), eng="sync"):
        return self.op(eng, lambda e: e.dma_start(out=out, in_=in_), reads, writes, dma=True)

    def emit(self, final_bufs):
        nc = self.nc
        st = self.stack
        eng_sem = {e: st.enter_context(nc.semaphore("s_" + e)) for e in ("pe", "act", "dve", "pool")}
        dma_sem = {}
        keys = {}
        for e in self.ENGS:
            for o in self.ops[e]:
                if o.dma and id(o.semkey) not in dma_sem:
                    dma_sem[id(o.semkey)] = st.enter_context(nc.semaphore("d_%d" % len(dma_sem)))
                    keys[id(o.semkey)] = o.semkey
        for e in ("pe", "act", "dve", "pool"):
            c = 0
            for o in self.ops[e]:
                if not o.dma and o.signal:
                    c += 1
                    o.count = c
        engobj = {"pe": None, "act": None, "dve": None, "pool": None, "sync": None}

        def run(e, eng):
            waited = {}
            for o in self.ops[e]:
                for d in o.deps:
                    if d.dma:
                        sem = dma_sem[id(d.semkey)]
                        k = ("d", id(d.semkey))
                    else:
                        sem = eng_sem[d.eng]
                        k = ("e", d.eng)
                    if waited.get(k, 0) >= d.count:
                        continue
                    waited[k] = d.count
                    eng.wait_ge(sem, d.count)
                ins = o.fn(eng)
                if o.dma:
                    ins.then_inc(dma_sem[id(o.semkey)], 16)
                elif o.signal:
                    ins.then_inc(eng_sem[e], 1)
            if e == "sync":
                for b in final_bufs:
                    eng.wait_ge(dma_sem[id(b)], self.dma_count[id(b)])

        with nc.Block() as block:
            @block.tensor
            def _(eng):
                run("pe", eng)

            @block.scalar
            def _(eng):
                run("act", eng)

            @block.vector
            def _(eng):
                run("dve", eng)

            @block.gpsimd
            def _(eng):
                run("pool", eng)

            @block.sync
            def _(eng):
                run("sync", eng)
        st.close()
        return nc


def build_cast(F):
    kb = KB()
    CH = 4096
    nch = F // CH
    assert F % CH == 0
    src = kb.dram("src", [128, F], F32, "ExternalInput")
    dst = kb.dram("dst", [128, F], BF16, "ExternalOutput")
    a = [kb.sb("a%d" % i, [128, CH], F32) for i in range(3)]
    b = [kb.sb("b%d" % i, [128, CH], BF16) for i in range(3)]
    ab = [kb.buf() for _ in range(3)]
    bb = [kb.buf() for _ in range(3)]
    ob = kb.buf("out")
    for i in range(nch):
        s = i % 3
        kb.dma(a[s][:], src[:, i * CH:(i + 1) * CH], writes=[ab[s]], eng="sync")
        e = ("dve", "act", "pool")[i % 3]
        if e == "act":
            kb.op("act", lambda eng, s=s: eng.copy(out=b[s][:], in_=a[s][:]), [ab[s]], [bb[s]])
        else:
            kb.op(e, lambda eng, s=s: eng.tensor_copy(out=b[s][:], in_=a[s][:]), [ab[s]], [bb[s]])
        kb.dma(dst[:, i * CH:(i + 1) * CH], b[s][:], reads=[bb[s]], writes=[ob], eng="sync")
    return kb.emit([ob])


_CACHE = {}


def run_cast(flat_list):
    F = flat_list[0].shape[1]
    key = ("cast", F)
    if key not in _CACHE:
        _CACHE[key] = build_cast(F)
    res = run_bass_kernel_spmd(_CACHE[key], [{"src": f} for f in flat_list], core_ids=list(range(NCORES)))
    return [r["dst"] for r in res.results]


class Ctx:
    def __init__(self, kb, npsum=6):
        self.kb = kb
        self.pbank = [kb.ps("pg%d" % i, [128, 512]) for i in range(npsum)]
        self.pbuf = [kb.buf("pg%d" % i) for i in range(npsum)]
        self.pi = 0
        self.pst = kb.ps("pstat", [128, 512])
        self.pstb = kb.buf("pstat")
        self.ones = kb.sb("ones_bf", [128, 128], BF16)
        self.onesb = kb.buf("ones")
        self.epst = kb.sb("eps_t", [128, 1], F32)
        self.epsb = kb.buf("eps")
        kb.op("pool", lambda e: e.memset(self.ones[:], 1.0), [], [self.onesb])
        kb.op("pool", lambda e: e.memset(self.epst[:], EPS), [], [self.epsb])

    def bank(self, exclude=()):
        while True:
            i = self.pi
            self.pi = (self.pi + 1) % len(self.pbank)
            if not any(self.pbuf[i] is x for x in exclude):
                return self.pbank[i], self.pbuf[i]


def emit_rstd(cx, sq_aps, sq_bufs, nfree, dim, rstd, rstdb):
    kb = cx.kb
    n = len(sq_aps)
    for i, a in enumerate(sq_aps):
        kb.op("pe", lambda e, a=a, i=i: e.matmul(cx.pst[:, 0:nfree], cx.ones[:], a, start=(i == 0), stop=(i == n - 1)),
              [cx.onesb] + list(sq_bufs), [cx.pstb])
    kb.op("act", lambda e: e.activation(out=rstd, in_=cx.pst[:, 0:nfree], func=AF.Sqrt, bias=cx.epst[:, 0:1], scale=1.0 / dim),
          [cx.pstb, cx.epsb], [rstdb])
    kb.op("dve", lambda e: e.reciprocal(out=rstd, in_=rstd), [rstdb], [rstdb])


def emit_norm_block(cx, xb, xbb, g, gb, hT, hTb, sq, sqb, rstd, rstdb, nfree=TB, nk=KC):
    kb = cx.kb
    kb.op("act", lambda e: e.activation(out=sq[:, 0:nk, 0:nfree], in_=xb[:, 0:nk, 0:nfree], func=AF.Square), [xbb], [sqb])
    emit_rstd(cx, [sq[:, k, 0:nfree] for k in range(nk)], [sqb], nfree, nk * 128, rstd[:, 0:nfree], rstdb)
    for k in range(nk):
        kb.op("dve", lambda e, k=k: e.scalar_tensor_tensor(out=hT[:, k, 0:nfree], in0=xb[:, k, 0:nfree], scalar=g[:, k:k + 1],
                                                            in1=rstd[:, 0:nfree], op0=ALU.mult, op1=ALU.mult),
              [xbb, gb, rstdb], [hTb])


def emit_headnorm(cx, pbank, pbuf, gcol, gb, out_ap, outb, sq1, sq1b, rstd, rstdb, nfree):
    kb = cx.kb
    kb.op("act", lambda e: e.activation(out=sq1[:, 0:nfree], in_=pbank[:, 0:nfree], func=AF.Square), [pbuf], [sq1b])
    emit_rstd(cx, [sq1[:, 0:nfree]], [sq1b], nfree, 128, rstd[:, 0:nfree], rstdb)
    kb.op("dve", lambda e: e.scalar_tensor_tensor(out=out_ap, in0=pbank[:, 0:nfree], scalar=gcol, in1=rstd[:, 0:nfree],
                                                  op0=ALU.mult, op1=ALU.mult), [pbuf, gb, rstdb], [outb])


class WPool:
    def __init__(self, kb, name, shape, n):
        self.kb = kb
        self.t = [kb.sb("%s%d" % (name, i), shape, BF16) for i in range(n)]
        self.b = [kb.buf("%s%d" % (name, i)) for i in range(n)]
        self.i = 0

    def load(self, src, eng="sync"):
        i = self.i
        self.i = (self.i + 1) % len(self.t)
        self.kb.dma(self.t[i][:], src, writes=[self.b[i]], eng=eng)
        return self.t[i], self.b[i]


def emit_linear_fm(cx, w, wb, nk, rhs_fn, rhs_bufs, nfree=TB):
    kb = cx.kb
    pb, pbb = cx.bank()
    for k in range(nk):
        kb.op("pe", lambda e, k=k: e.matmul(pb[:, 0:nfree], w[:, k, :], rhs_fn(k), start=(k == 0), stop=(k == nk - 1)),
              [wb] + list(rhs_bufs), [pbb])
    return pb, pbb


NM = 256
XH = 4


def build_C():
    kb = KB()
    xT = kb.dram("xT", [KC, 128, T], F32, "ExternalInput")
    mixT = kb.dram("mixT", [KC, 128, T], BF16, "ExternalInput")
    memT = kb.dram("memT", [KC, 128, NM], F32, "ExternalInput")
    w_out = kb.dram("w_out", [KC, 128, KC, 128], BF16, "ExternalInput")
    w_xq = kb.dram("w_xq", [XH, 128, KC, 128], BF16, "ExternalInput")
    w_xk = kb.dram("w_xk", [XH, 128, KC, 128], BF16, "ExternalInput")
    w_xv = kb.dram("w_xv", [128, KC, 512], BF16, "ExternalInput")
    w_xo = kb.dram("w_xo", [KC, 128, XH, 128], BF16, "ExternalInput")
    w_f1 = kb.dram("w_f1", [FC, 128, KC, 2, 128], BF16, "ExternalInput")
    w_f2 = kb.dram("w_f2", [KC, 128, FC, 128], BF16, "ExternalInput")
    gains = kb.dram("gains", [128, 3 * KC + 2], F32, "ExternalInput")
    xo = kb.dram("xTo", [KC, 128, T], F32, "ExternalOutput")

    cx = Ctx(kb)
    xb = kb.sb("xb", [128, KC, TB], F32); xbb = kb.buf("xb")
    mb = kb.sb("mb", [128, KC, TB], BF16); mbb = kb.buf("mb")
    hT = kb.sb("hT", [128, KC, TB], BF16); hTb = kb.buf("hT")
    aT = kb.sb("aT", [128, FC, TB], BF16); aTb = kb.buf("aT")
    rstd = kb.sb("rstd", [128, TB], F32); rstdb = kb.buf("rstd")
    sq1 = kb.sb("sq1", [128, TB], BF16); sq1b = kb.buf("sq1")
    gt = kb.sb("gt", [128, 3 * KC + 2], F32); gtb = kb.buf("gt")
    gq = kb.sb("gq", [128, 1], F32); gqb = kb.buf("gq")
    kx = kb.sb("kx", [128, XH, NM], BF16); kxb = kb.buf("kx")
    vx = kb.sb("vx", [128, 2, XH * 128], BF16); vxb = kb.buf("vx")
    qx = kb.sb("qx", [128, XH, TB], BF16); qxb = kb.buf("qx")
    E = kb.sb("E", [128, 2, TB], BF16); Eb = kb.buf("E")
    oT = kb.sb("oT", [128, XH, TB], BF16); oTb = kb.buf("oT")
    sg = [kb.sb("sg%d" % i, [128, TB], F32) for i in range(2)]; sgb = [kb.buf() for _ in range(2)]
    wA = WPool(kb, "wA", [128, KC, 128], 2)
    wO = WPool(kb, "wO", [128, XH, 128], 2)
    wF1 = WPool(kb, "wF1", [128, KC, 2, 128], 2)
    wF2 = WPool(kb, "wF2", [128, FC, 128], 2)
    outb = kb.buf("out")

    kb.dma(gt[:], gains[:, :], writes=[gtb], eng="pool")
    kb.op("dve", lambda e: e.tensor_scalar(out=gq[:], in0=gt[:, 3 * KC:3 * KC + 1], scalar1=float(128 ** -0.5), scalar2=None, op0=ALU.mult),
          [gtb], [gqb])

    mf = xb
    for k in range(KC):
        pass
    kb.dma(xb[:, :, 0:NM], memT.rearrange("k p t -> p k t"), writes=[xbb], eng="sync")
    emit_norm_block(cx, xb, xbb, gt[:, 2 * KC:3 * KC], gtb, hT, hTb, aT, aTb, rstd, rstdb, nfree=NM)
    for h in range(XH):
        w, wb = wA.load(w_xk[h])
        pb, pbb = emit_linear_fm(cx, w, wb, KC, lambda k: hT[:, k, 0:NM], [hTb], nfree=NM)
        emit_headnorm(cx, pb, pbb, gt[:, 3 * KC + 1:3 * KC + 2], gtb, kx[:, h, :], kxb, sq1, sq1b, rstd, rstdb, NM)
    wv = aT[:, 16:32, :]
    kb.dma(wv, w_xv[:, :, :], reads=[], writes=[aTb], eng="sync")
    for mt in range(2):
        pb, pbb = cx.bank()
        for k in range(KC):
            kb.op("pe", lambda e, k=k, mt=mt, pb=pb: e.matmul(pb[:, :], hT[:, k, mt * 128:(mt + 1) * 128], aT[:, 16 + k, :],
                                                         start=(k == 0), stop=(k == KC - 1)), [hTb, aTb], [pbb])
        kb.op("act", lambda e, mt=mt, pb=pb: e.copy(out=vx[:, mt, :], in_=pb[:, :]), [pbb], [vxb])

    for tb in range(NTB):
        ts = slice(tb * TB, (tb + 1) * TB)
        kb.dma(xb[:], xT[:, :, ts].rearrange("k p t -> p k t"), writes=[xbb], eng="sync")
        kb.dma(mb[:], mixT[:, :, ts].rearrange("k p t -> p k t"), writes=[mbb], eng="pool")
        for i in range(KC):
            w, wb = wA.load(w_out[i])
            pb, pbb = emit_linear_fm(cx, w, wb, KC, lambda k: mb[:, k, :], [mbb])
            kb.op("dve", lambda e, i=i, pb=pb: e.tensor_tensor(out=xb[:, i, :], in0=xb[:, i, :], in1=pb[:, :], op=ALU.add), [pbb, xbb], [xbb])
        emit_norm_block(cx, xb, xbb, gt[:, 0:KC], gtb, hT, hTb, aT, aTb, rstd, rstdb)
        for h in range(XH):
            w, wb = wA.load(w_xq[h])
            pb, pbb = emit_linear_fm(cx, w, wb, KC, lambda k: hT[:, k, :], [hTb])
            emit_headnorm(cx, pb, pbb, gq[:, 0:1], gqb, qx[:, h, :], qxb, sq1, sq1b, rstd, rstdb, TB)
        for h in range(XH):
            for mt in range(2):
                pb, pbb = cx.bank()
                kb.op("pe", lambda e, h=h, mt=mt, pb=pb: e.matmul(pb[:, :], kx[:, h, mt * 128:(mt + 1) * 128], qx[:, h, :], start=True, stop=True),
                      [kxb, qxb], [pbb])
                kb.op("act", lambda e, mt=mt, pb=pb: e.activation(out=E[:, mt, :], in_=pb[:, :], func=AF.Exp), [pbb], [Eb])
            po, pob = cx.bank()
            pd, pdb = cx.bank()
            for mt in range(2):
                kb.op("pe", lambda e, h=h, mt=mt, po=po: e.matmul(po[:, :], vx[:, mt, h * 128:(h + 1) * 128], E[:, mt, :], start=(mt == 0), stop=(mt == 1)),
                      [vxb, Eb], [pob])
            for mt in range(2):
                kb.op("pe", lambda e, mt=mt, pd=pd: e.matmul(pd[:, :], cx.ones[:], E[:, mt, :], start=(mt == 0), stop=(mt == 1)),
                      [cx.onesb, Eb], [pdb])
            kb.op("dve", lambda e, pd=pd: e.reciprocal(out=rstd[:, :], in_=pd[:, :]), [pdb], [rstdb])
            kb.op("dve", lambda e, h=h, po=po: e.tensor_tensor(out=oT[:, h, :], in0=po[:, :], in1=rstd[:, :], op=ALU.mult), [pob, rstdb], [oTb])
        for i in range(KC):
            w, wb = wO.load(w_xo[i])
            pb, pbb = emit_linear_fm(cx, w, wb, XH, lambda k: oT[:, k, :], [oTb])
            kb.op("dve", lambda e, i=i, pb=pb: e.tensor_tensor(out=xb[:, i, :], in0=xb[:, i, :], in1=pb[:, :], op=ALU.add), [pbb, xbb], [xbb])
        emit_norm_block(cx, xb, xbb, gt[:, KC:2 * KC], gtb, hT, hTb, aT, aTb, rstd, rstdb)
        for j in range(FC):
            w, wb = wF1.load(w_f1[j])
            pg, pgb = cx.bank()
            pu, pub = cx.bank()
            for k in range(KC):
                kb.op("pe", lambda e, k=k, w=w, pg=pg: e.matmul(pg[:, :], w[:, k, 0, :], hT[:, k, :], start=(k == 0), stop=(k == KC - 1)), [wb, hTb], [pgb])
            for k in range(KC):
                kb.op("pe", lambda e, k=k, w=w, pu=pu: e.matmul(pu[:, :], w[:, k, 1, :], hT[:, k, :], start=(k == 0), stop=(k == KC - 1)), [wb, hTb], [pub])
            s = j % 2
            kb.op("act", lambda e, s=s, pg=pg: e.activation(out=sg[s][:], in_=pg[:, :], func=AF.Silu), [pgb], [sgb[s]])
            kb.op("dve", lambda e, s=s, j=j, pu=pu: e.tensor_tensor(out=aT[:, j, :], in0=sg[s][:], in1=pu[:, :], op=ALU.mult), [sgb[s], pub], [aTb])
        for i in range(KC):
            w, wb = wF2.load(w_f2[i])
            pb, pbb = emit_linear_fm(cx, w, wb, FC, lambda k: aT[:, k, :], [aTb])
            kb.op("dve", lambda e, i=i, pb=pb: e.tensor_tensor(out=xb[:, i, :], in0=xb[:, i, :], in1=pb[:, :], op=ALU.add), [pbb, xbb], [xbb])
        kb.dma(xo[:, :, ts].rearrange("k p t -> p k t"), xb[:], reads=[xbb], writes=[outb], eng="sync")
    return kb.emit([outb])


def lay_lhsT(W, mchunk=128):
    Kd, M = W.shape
    return np.ascontiguousarray(W.reshape(Kd // 128, 128, M // 128, 128).transpose(2, 1, 0, 3))


def lay_rhs(W):
    Kd, N = W.shape
    return np.ascontiguousarray(W.reshape(Kd // 128, 128, N).transpose(1, 0, 2))


def lay_gain(g):
    return np.ascontiguousarray(g.reshape(-1, 128).T)


def build_P(fm_epi, tm_scales):
    NF = len(fm_epi)
    NG = len(tm_scales)
    kb = KB()
    xT = kb.dram("xT", [KC, 128, T], F32, "ExternalInput")
    w_fm = kb.dram("w_fm", [NF, 128, KC, 128], BF16, "ExternalInput")
    w_tm = kb.dram("w_tm", [NG, 128, KC, 512], BF16, "ExternalInput")
    gains = kb.dram("gains", [128, KC + 2], F32, "ExternalInput")
    fmo = kb.dram("fmT", [NF, 128, T], BF16, "ExternalOutput")
    tmo = kb.dram("tm", [T, NG * 512], BF16, "ExternalOutput")

    cx = Ctx(kb)
    xb = kb.sb("xb", [128, KC, TB], F32); xbb = kb.buf("xb")
    hT = kb.sb("hT", [128, KC, TB], BF16); hTb = kb.buf("hT")
    sq = kb.sb("sq", [128, KC, TB], BF16); sqb = kb.buf("sq")
    rstd = kb.sb("rstd", [128, TB], F32); rstdb = kb.buf("rstd")
    sq1 = kb.sb("sq1", [128, TB], BF16); sq1b = kb.buf("sq1")
    gt = kb.sb("gt", [128, KC + 2], F32); gtb = kb.buf("gt")
    gs = kb.sb("gs", [128, 2], F32); gsb = kb.buf("gs")
    wF = WPool(kb, "wF", [128, KC, 128], 3)
    wT = WPool(kb, "wT", [128, KC, 512], 2)
    GRP = 8
    of = [kb.sb("of%d" % i, [128, GRP, TB], BF16) for i in range(2)]; ofb = [kb.buf() for _ in range(2)]
    ot = [kb.sb("ot%d" % i, [128, 4, 512], BF16) for i in range(2)]; otb = [kb.buf() for _ in range(2)]
    outb = kb.buf("out")
    outb2 = kb.buf("out2")

    kb.dma(gt[:], gains[:, :], writes=[gtb], eng="pool")
    norm_c = {}
    for ep in fm_epi:
        if ep[0] == "norm":
            norm_c[ep[1]] = ep[2]
    for col, c in norm_c.items():
        kb.op("dve", lambda e, col=col, c=c: e.tensor_scalar(out=gs[:, col:col + 1], in0=gt[:, KC + col:KC + col + 1], scalar1=float(c),
                                                              scalar2=None, op0=ALU.mult), [gtb], [gsb])
    for tb in range(NTB):
        ts = slice(tb * TB, (tb + 1) * TB)
        kb.dma(xb[:], xT[:, :, ts].rearrange("k p t -> p k t"), writes=[xbb], eng="sync")
        emit_norm_block(cx, xb, xbb, gt[:, 0:KC], gtb, hT, hTb, sq, sqb, rstd, rstdb)
        for j in range(NF):
            gi = (j // GRP) % 2
            w, wb = wF.load(w_fm[j])
            pb, pbb = emit_linear_fm(cx, w, wb, KC, lambda k: hT[:, k, :], [hTb])
            dst = of[gi][:, j % GRP, :]
            ep = fm_epi[j]
            if ep[0] == "copy":
                kb.op("act", lambda e, dst=dst, pb=pb: e.copy(out=dst, in_=pb[:, :]), [pbb], [ofb[gi]])
            elif ep[0] == "scale":
                kb.op("act", lambda e, dst=dst, pb=pb, c=ep[1]: e.mul(out=dst, in_=pb[:, :], mul=float(c)), [pbb], [ofb[gi]])
            elif ep[0] == "silu":
                kb.op("act", lambda e, dst=dst, pb=pb: e.activation(out=dst, in_=pb[:, :], func=AF.Silu), [pbb], [ofb[gi]])
            else:
                emit_headnorm(cx, pb, pbb, gs[:, ep[1]:ep[1] + 1], gsb, dst, ofb[gi], sq1, sq1b, rstd, rstdb, TB)
            if j % GRP == GRP - 1 or j == NF - 1:
                j0 = (j // GRP) * GRP
                n = j - j0 + 1
                kb.dma(fmo[j0:j0 + n, :, ts].rearrange("k p t -> p k t"), of[gi][:, 0:n, :], reads=[ofb[gi]], writes=[outb], eng="pool")
        for g in range(NG):
            w, wb = wT.load(w_tm[g])
            oi = g % 2
            for tt in range(4):
                pb, pbb = cx.bank()
                for k in range(KC):
                    kb.op("pe", lambda e, k=k, tt=tt, pb=pb, w=w: e.matmul(pb[:, :], hT[:, k, tt * 128:(tt + 1) * 128], w[:, k, :],
                                                                      start=(k == 0), stop=(k == KC - 1)), [hTb, wb], [pbb])
                kb.op("act", lambda e, tt=tt, pb=pb, oi=oi, c=tm_scales[g]: e.mul(out=ot[oi][:, tt, :], in_=pb[:, :], mul=float(c)), [pbb], [otb[oi]])
            kb.dma(tmo[tb * TB:(tb + 1) * TB, g * 512:(g + 1) * 512].rearrange("(a p) c -> p a c", p=128), ot[oi][:], reads=[otb[oi]],
                   writes=[outb2], eng="pool")
    return kb.emit([outb, outb2])


AX = mybir.AxisListType
NU = 2
QB = 512
NQB = S // QB
NKT = S // 128


def build_Modd():
    kb = KB()
    qT = kb.dram("qT", [NU, 2, 128, S], BF16, "ExternalInput")
    kT = kb.dram("kT", [NU, 2, 128, S], BF16, "ExternalInput")
    v = kb.dram("v", [NU, S, 256], BF16, "ExternalInput")
    atab = kb.dram("atab", [NU, 128, 140], F32, "ExternalInput")
    btile = kb.dram("btile", [NU, 128, 3, 128], BF16, "ExternalInput")
    ident_d = kb.dram("ident", [128, 128], BF16, "ExternalInput")
    lam4 = kb.dram("lam4", [128, 4, 128], F32, "ExternalInput")
    gout = kb.dram("gout", [128, 256], F32, "ExternalInput")
    consts = kb.dram("consts", [128, 2], F32, "ExternalInput")
    dout = kb.dram("d_tm", [NU, S, 256], BF16, "ExternalOutput")

    sps = [kb.ps("sps%d" % i, [128, 512]) for i in range(3)]; spsb = [kb.buf() for _ in range(3)]
    acc = [kb.ps("acc%d" % i, [128, 512]) for i in range(4)]; accb = [kb.buf() for _ in range(4)]
    kTs = kb.sb("kTs", [128, 2, S], BF16); kTb = kb.buf("kTs")
    Vx = kb.sb("Vx", [128, NKT, 257], BF16); Vxb = kb.buf("Vx")
    qbl = [kb.sb("qbl%d" % i, [128, 2, QB], BF16) for i in range(2)]; qblb = [kb.buf() for _ in range(2)]
    Es = [kb.sb("E%d" % i, [128, QB], BF16) for i in range(3)]; Esb = [kb.buf() for _ in range(3)]
    O = [kb.sb("O%d" % i, [128, 4, 257], F32) for i in range(2)]; Ob = [kb.buf() for _ in range(2)]
    at = kb.sb("at", [128, 140], F32); atb = kb.buf("at")
    bt = kb.sb("bt", [128, 3, 128], BF16); btb = kb.buf("bt")
    ident = kb.sb("ident_sb", [128, 128], BF16); identb = kb.buf("ident")
    l4 = kb.sb("l4", [128, 4, 128], F32); l4b = kb.buf("l4")
    gfin = kb.sb("gfin", [128, 256], F32); gfinb = kb.buf("gfin")
    cst = kb.sb("cst", [128, 2], F32); cstb = kb.buf("cst")
    epst = kb.sb("epst", [128, 1], F32); epsb = kb.buf("eps")
    sm = kb.sb("sm", [128, 8], F32); smb = kb.buf("sm")
    y = kb.sb("y", [128, 256], F32); yb = kb.buf("y")
    y2 = kb.sb("y2", [128, 256], F32); y2b = kb.buf("y2")
    rr = kb.sb("rr", [128, 4], F32); rrb = kb.buf("rr")
    ost = [kb.sb("ost%d" % i, [128, 4, 256], BF16) for i in range(2)]; ostb = [kb.buf() for _ in range(2)]
    outb = kb.buf("out")

    kb.dma(ident[:], ident_d[:, :], writes=[identb], eng="pool")
    kb.dma(l4[:], lam4[:, :, :], writes=[l4b], eng="pool")
    kb.dma(gfin[:], gout[:, :], writes=[gfinb], eng="pool")
    kb.dma(cst[:], consts[:, :], writes=[cstb], eng="pool")
    kb.op("pool", lambda e: e.memset(epst[:], EPS), [], [epsb])
    kb.op("pool", lambda e: e.memset(Vx[:, :, 256:257], 1.0), [], [Vxb])
    kb.op("dve", lambda e: e.tensor_tensor(out=y[:, 0:128], in0=l4[:, 0, :], in1=l4[:, 1, :], op=ALU.mult), [l4b], [yb])
    kb.op("dve", lambda e: e.tensor_reduce(out=sm[:, 0:1], in_=y[:, 0:128], axis=AX.X, op=ALU.add), [yb], [smb])
    kb.op("dve", lambda e: e.tensor_tensor(out=y[:, 0:128], in0=l4[:, 2, :], in1=l4[:, 3, :], op=ALU.mult), [l4b, smb], [yb])
    kb.op("dve", lambda e: e.tensor_reduce(out=sm[:, 1:2], in_=y[:, 0:128], axis=AX.X, op=ALU.add), [yb], [smb])
    kb.op("act", lambda e: e.activation(out=sm[:, 4:6], in_=sm[:, 0:2], func=AF.Exp), [smb], [smb])
    kb.op("dve", lambda e: e.tensor_tensor(out=sm[:, 2:3], in0=sm[:, 4:5], in1=sm[:, 5:6], op=ALU.subtract), [smb], [smb])
    kb.op("dve", lambda e: e.tensor_tensor(out=sm[:, 2:3], in0=sm[:, 2:3], in1=cst[:, 0:1], op=ALU.add), [smb, cstb], [smb])
    kb.op("dve", lambda e: e.tensor_scalar(out=sm[:, 3:4], in0=sm[:, 2:3], scalar1=-1.0, scalar2=None, op0=ALU.mult), [smb], [smb])
    kb.op("dve", lambda e: e.tensor_scalar(out=gfin[:], in0=gfin[:], scalar1=cst[:, 1:2], scalar2=None, op0=ALU.mult), [gfinb, cstb], [gfinb])

    si = 0
    ei = 0
    for u in range(NU):
        kb.dma(kTs[:, 0, :], kT[u, 0], writes=[kTb], eng="sync")
        kb.dma(kTs[:, 1, :], kT[u, 1], writes=[kTb], eng="sync")
        for h4 in range(4):
            kb.dma(Vx[:, h4 * 16:(h4 + 1) * 16, 0:256], v[u, h4 * 2048:(h4 + 1) * 2048, :].rearrange("(t p) e -> p t e", p=128),
                   writes=[Vxb], eng="sync")
        kb.dma(at[:], atab[u], writes=[atb], eng="pool")
        kb.dma(bt[:], btile[u], writes=[btb], eng="pool")
        for qb in range(NQB):
            qs = qb % 2
            kb.dma(qbl[qs][:], qT[u, :, :, qb * QB:(qb + 1) * QB].rearrange("c p t -> p c t"), writes=[qblb[qs]], eng="sync")
            for c in range(2):
                for kt in range(NKT):
                    kq = kt // 4
                    sp, spb = sps[si % 3], spsb[si % 3]
                    si += 1
                    E, Eb = Es[ei % 3], Esb[ei % 3]
                    ei += 1
                    diag = (kq == qb)
                    kb.op("pe", lambda e, sp=sp, c=c, kt=kt, qs=qs, diag=diag: e.matmul(sp[:, :], kTs[:, c, kt * 128:(kt + 1) * 128], qbl[qs][:, c, :],
                                                                                   start=True, stop=(not diag)), [kTb, qblb[qs]], [spb])
                    if diag:
                        sbp = kt % 4
                        for sb in range(4):
                            dl = sb - sbp
                            ti = 2 if dl == 0 else (0 if dl > 0 else 1)
                            kb.op("pe", lambda e, sp=sp, sb=sb, ti=ti: e.matmul(sp[:, sb * 128:(sb + 1) * 128], ident[:], bt[:, ti, :], start=False, stop=True,
                                                                          skip_group_check=True), [identb, btb], [spb])
                        for sb in range(4):
                            dl = abs(sb - sbp)
                            kb.op("act", lambda e, sp=sp, sb=sb, dl=dl, E=E: e.activation(out=E[:, sb * 128:(sb + 1) * 128], in_=sp[:, sb * 128:(sb + 1) * 128],
                                                                                     func=AF.Exp, bias=at[:, 136 + dl:137 + dl], scale=1.0), [spb, atb], [Eb])
                    else:
                        col = (qb * 4 - kt) if kq < qb else (64 + kt - qb * 4)
                        kb.op("act", lambda e, sp=sp, col=col, E=E: e.activation(out=E[:, :], in_=sp[:, :], func=AF.Exp, bias=at[:, col:col + 1], scale=1.0),
                              [spb, atb], [Eb])
                    if kq < qb:
                        first, last, ph = (kt == 0), (kt == qb * 4 - 1), 0
                    elif diag:
                        first, last, ph = (kt == qb * 4), (kt == qb * 4 + 3), 1
                    else:
                        first, last, ph = (kt == qb * 4 + 4), (kt == NKT - 1), 2
                    for sb in range(4):
                        kb.op("pe", lambda e, sb=sb, E=E, kt=kt, first=first, last=last: e.matmul(acc[sb][:, 0:257], E[:, sb * 128:(sb + 1) * 128], Vx[:, kt, :],
                                                                                          start=first, stop=last), [Eb, Vxb], [accb[sb]])
                    if last:
                        for sb in range(4):
                            if ph == 0:
                                kb.op("dve", lambda e, sb=sb, c=c: e.tensor_scalar(out=O[c][:, sb, :], in0=acc[sb][:, 0:257], scalar1=at[:, 128 + sb:129 + sb],
                                                                                 scalar2=None, op0=ALU.mult), [accb[sb], atb], [Ob[c]])
                            elif ph == 1:
                                if qb == 0:
                                    kb.op("dve", lambda e, sb=sb, c=c: e.tensor_copy(out=O[c][:, sb, :], in_=acc[sb][:, 0:257]), [accb[sb]], [Ob[c]])
                                else:
                                    kb.op("dve", lambda e, sb=sb, c=c: e.tensor_tensor(out=O[c][:, sb, :], in0=O[c][:, sb, :], in1=acc[sb][:, 0:257], op=ALU.add),
                                          [accb[sb], Ob[c]], [Ob[c]])
                            else:
                                kb.op("dve", lambda e, sb=sb, c=c: e.scalar_tensor_tensor(out=O[c][:, sb, :], in0=acc[sb][:, 0:257], scalar=at[:, 132 + sb:133 + sb],
                                                                                        in1=O[c][:, sb, :], op0=ALU.mult, op1=ALU.add), [accb[sb], atb, Ob[c]], [Ob[c]])
            os_ = qb % 2
            for sb in range(4):
                kb.op("dve", lambda e, sb=sb: e.reciprocal(out=rr[:, 0:1], in_=O[0][:, sb, 256:257]), [Ob[0], rrb], [rrb])
                kb.op("dve", lambda e, sb=sb: e.reciprocal(out=rr[:, 1:2], in_=O[1][:, sb, 256:257]), [Ob[1], rrb], [rrb])
                kb.op("dve", lambda e: e.tensor_tensor(out=rr[:, 1:2], in0=rr[:, 1:2], in1=sm[:, 3:4], op=ALU.mult), [rrb, smb], [rrb])
                kb.op("dve", lambda e, sb=sb: e.tensor_scalar(out=y[:], in0=O[0][:, sb, 0:256], scalar1=rr[:, 0:1], scalar2=None, op0=ALU.mult),
                      [Ob[0], rrb, yb], [yb])
                kb.op("dve", lambda e, sb=sb: e.scalar_tensor_tensor(out=y[:], in0=O[1][:, sb, 0:256], scalar=rr[:, 1:2], in1=y[:], op0=ALU.mult, op1=ALU.add),
                      [Ob[1], rrb, yb], [yb])
                kb.op("dve", lambda e: e.tensor_tensor(out=y2[:], in0=y[:], in1=y[:], op=ALU.mult), [yb, y2b], [y2b])
                kb.op("dve", lambda e: e.tensor_reduce(out=rr[:, 2:3], in_=y2[:], axis=AX.X, op=ALU.add), [y2b, rrb], [rrb])
                kb.op("act", lambda e: e.activation(out=rr[:, 3:4], in_=rr[:, 2:3], func=AF.Sqrt, bias=epst[:, 0:1], scale=1.0 / 256), [rrb, epsb], [rrb])
                kb.op("dve", lambda e: e.reciprocal(out=rr[:, 3:4], in_=rr[:, 3:4]), [rrb], [rrb])
                kb.op("dve", lambda e, sb=sb, os_=os_: e.scalar_tensor_tensor(out=ost[os_][:, sb, :], in0=y[:], scalar=rr[:, 3:4], in1=gfin[:], op0=ALU.mult, op1=ALU.mult),
                      [yb, rrb, gfinb], [ostb[os_]])
            kb.dma(dout[u, qb * QB:(qb + 1) * QB, :].rearrange("(s p) e -> p s e", p=128), ost[os_][:], reads=[ostb[os_]], writes=[outb], eng="pool")
    return kb.emit([outb])


def alibi_tables(slope):
    p = np.arange(128, dtype=np.float64)
    at = np.zeros((128, 140), np.float64)
    for dist in range(64):
        at[:, dist] = slope * (p - 128.0 * dist)
        at[:, 64 + dist] = -slope * (p + 128.0 * dist - 511.0)
    for sb in range(4):
        at[:, 128 + sb] = np.exp(-slope * (sb * 128 + p))
        at[:, 132 + sb] = np.exp(-slope * (511 - sb * 128 - p))
        at[:, 136 + sb] = -slope * sb * 128.0
    pk = p[:, None]; pq = p[None, :]
    G = -slope * (pq - pk)
    bt = np.stack([G, -G, -slope * np.abs(pq - pk)], 1)
    return at.astype(np.float32), bt.astype(NPBF)


NCH = S // 128


def build_Meven():
    kb = KB()
    fm5 = kb.dram("fm5", [NU, 5, 128, S], BF16, "ExternalInput")
    tm3 = kb.dram("tm3", [NU, 3, S, 128], BF16, "ExternalInput")
    dec = kb.dram("dec", [128, NU, 2], F32, "ExternalInput")
    itab = kb.dram("itab", [128, 4 * 128 + 2], F32, "ExternalInput")
    retg = kb.dram("retg", [128, 1], F32, "ExternalInput")
    nab = kb.dram("nab", [NU, 128, 5, 5, 128], F32, "ExternalInput")
    outT = kb.dram("mixT", [NU, 2, 128, S], BF16, "ExternalOutput")

    cx = Ctx(kb)
    A0 = kb.sb("A0", [128, S], BF16); A0b = kb.buf("A0")
    A1 = kb.sb("A1", [128, S], BF16); A1b = kb.buf("A1")
    A2 = kb.sb("A2", [128, S], BF16); A2b = kb.buf("A2")
    A3 = kb.sb("A3", [128, NCH, 128], BF16); A3b = kb.buf("A3")
    A4 = kb.sb("A4", [128, NCH, 128], BF16); A4b = kb.buf("A4")
    SfB = kb.sb("SfB", [128, NCH, 128], BF16); SfBb = kb.buf("SfB")
    SbB = kb.sb("SbB", [128, NCH, 128], BF16); SbBb = kb.buf("SbB")
    Sst = kb.sb("Sst", [128, 2, 128], F32); Sstb = [kb.buf("Sf"), kb.buf("Sb")]
    it = kb.sb("it", [128, 4 * 128 + 2], F32); itb = kb.buf("it")
    dc = kb.sb("dc", [128, NU, 2], F32); dcb = kb.buf("dc")
    rg_ = kb.sb("rg_", [128, 1], F32); rgb = kb.buf("rg")
    lgw = kb.sb("lgw", [128, 8], F32); lgwb = kb.buf("lgw")
    sc = kb.sb("sc", [128, 4], F32); scb = kb.buf("sc")
    MT4 = kb.sb("MT4", [128, 4, 128], F32); MT4b = kb.buf("MT4")
    qd4 = kb.sb("qd4", [128, 2, 4, 128], F32); qd4b = kb.buf("qd4")
    tmpm = kb.sb("tmpm", [128, 128], F32); tmpmb = kb.buf("tmpm")
    ksc = [kb.sb("ksc%d" % i, [128, 128], BF16) for i in range(4)]; kscb = [kb.buf() for _ in range(4)]
    PT = [kb.sb("PT%d" % i, [128, 4, 128], BF16) for i in range(2)]; PTb = [kb.buf() for _ in range(2)]
    qfb = [kb.sb("qfb%d" % i, [128, 2, 512], BF16) for i in range(2)]; qfbb = [kb.buf() for _ in range(2)]
    rstd = kb.sb("rstd", [128, 512], F32); rstdb = kb.buf("rstd")
    sq1 = kb.sb("sq1", [128, 512], BF16); sq1b = kb.buf("sq1")
    tmpo = kb.sb("tmpo", [128, 512], F32); tmpob = kb.buf("tmpo")
    ost = [kb.sb("ost%d" % i, [128, 512], BF16) for i in range(2)]; ostb = [kb.buf() for _ in range(2)]
    nbias = kb.sb("nbias", [128, 5, 5, 128], F32); nbiasb = kb.buf("nbias")
    Lb = [kb.sb("Lb%d" % i, [128, 5, 128], F32) for i in range(2)]; Lbb = [kb.buf() for _ in range(2)]
    En = [kb.sb("En%d" % i, [128, 5, 128], BF16) for i in range(2)]; Enb = [kb.buf() for _ in range(2)]
    outb = kb.buf("out")

    kb.dma(it[:], itab[:, :], writes=[itb], eng="pool")
    kb.dma(dc[:], dec[:, :, :], writes=[dcb], eng="pool")
    kb.dma(rg_[:], retg[:, :], writes=[rgb], eng="pool")
    oi = 0
    for u in range(NU):
        kb.dma(A0[:], fm5[u, 0], writes=[A0b], eng="sync")
        kb.dma(A1[:], fm5[u, 1], writes=[A1b], eng="sync")
        kb.dma(A2[:], fm5[u, 2], writes=[A2b], eng="sync")
        kb.dma(A3[:], tm3[u, 0].rearrange("(n p) d -> p n d", p=128), writes=[A3b], eng="sync")
        kb.dma(A4[:], tm3[u, 1].rearrange("(n p) d -> p n d", p=128), writes=[A4b], eng="sync")
        kb.op("act", lambda e, u=u: e.activation(out=lgw[:, 0:2], in_=dc[:, u, :], func=AF.Exp, scale=-1.0), [dcb], [lgwb])
        kb.op("dve", lambda e: e.tensor_scalar(out=lgw[:, 2:4], in0=lgw[:, 0:2], scalar1=-1.0 / 8, scalar2=1.0 / 7, op0=ALU.mult, op1=ALU.add), [lgwb], [lgwb])
        for cc in (6, 5, 4, 3, 2, 1):
            kb.op("dve", lambda e: e.tensor_tensor(out=lgw[:, 2:4], in0=lgw[:, 2:4], in1=lgw[:, 0:2], op=ALU.mult), [lgwb], [lgwb])
            kb.op("dve", lambda e, cc=cc: e.tensor_scalar(out=lgw[:, 2:4], in0=lgw[:, 2:4], scalar1=-1.0, scalar2=1.0 / cc, op0=ALU.mult, op1=ALU.add), [lgwb], [lgwb])
        kb.op("dve", lambda e: e.tensor_tensor(out=lgw[:, 2:4], in0=lgw[:, 2:4], in1=lgw[:, 0:2], op=ALU.mult), [lgwb], [lgwb])
        kb.op("dve", lambda e: e.tensor_scalar(out=lgw[:, 4:6], in0=lgw[:, 2:4], scalar1=-1.0, scalar2=None, op0=ALU.mult), [lgwb], [lgwb])
        lgf = lgw[:, 4:5]
        lgb = lgw[:, 5:6]
        kb.op("dve", lambda e: e.tensor_scalar(out=tmpm[:], in0=it[:, 0:128], scalar1=lgf, scalar2=None, op0=ALU.mult), [itb, lgwb], [tmpmb])
        kb.op("dve", lambda e: e.scalar_tensor_tensor(out=tmpm[:], in0=it[:, 128:256], scalar=lgb, in1=tmpm[:], op0=ALU.mult, op1=ALU.add), [itb, lgwb, tmpmb], [tmpmb])
        for r in range(4):
            kb.op("act", lambda e, r=r: e.activation(out=MT4[:, r, :], in_=tmpm[:], func=AF.Exp), [tmpmb], [MT4b])
            kb.op("act", lambda e, r=r: e.activation(out=qd4[:, 0, r, :], in_=it[:, 256:384], func=AF.Exp, scale=lgf), [itb, lgwb], [qd4b])
            kb.op("act", lambda e, r=r: e.activation(out=qd4[:, 1, r, :], in_=it[:, 384:512], func=AF.Exp, scale=lgb), [itb, lgwb], [qd4b])
        kb.op("act", lambda e: e.activation(out=sc[:, 0:1], in_=it[:, 512:513], func=AF.Exp, scale=lgf), [itb, lgwb], [scb])
        kb.op("act", lambda e: e.activation(out=sc[:, 1:2], in_=it[:, 513:514], func=AF.Exp, scale=lgb), [itb, lgwb], [scb])
        kb.op("act", lambda e: e.activation(out=sc[:, 2:4], in_=lgw[:, 4:6], func=AF.Exp, scale=128.0), [lgwb], [scb])
        ki = 0
        for d_, (SB_, SBb_) in enumerate(((SfB, SfBb), (SbB, SbBb))):
            kb.op("dve", lambda e, d_=d_: e.memset(Sst[:, d_, :], 0.0), [], [Sstb[d_]])
            order = range(NCH) if d_ == 0 else range(NCH - 1, -1, -1)
            pb = pbb = None
            for cnt, n in enumerate(order):
                if cnt % 4 == 0:
                    pb, pbb = cx.bank()
                ks, ksb = ksc[ki % 4], kscb[ki % 4]
                ki += 1
                kb.op("pool", lambda e, n=n, ks=ks, d_=d_: e.tensor_scalar(out=ks[:], in0=A3[:, n, :], scalar1=sc[:, d_:d_ + 1], scalar2=None, op0=ALU.mult),
                      [A3b, scb], [ksb])
                col = (cnt % 4) * 128
                kb.op("pe", lambda e, n=n, ks=ks, pb=pb, col=col: e.matmul(pb[:, col:col + 128], ks[:], A4[:, n, :], start=True, stop=True), [ksb, A4b], [pbb])
                kb.op("dve", lambda e, n=n, d_=d_, SB_=SB_: e.tensor_copy(out=SB_[:, n, :], in_=Sst[:, d_, :]), [Sstb[d_]], [SBb_])
                kb.op("dve", lambda e, d_=d_, pb=pb, col=col: e.scalar_tensor_tensor(out=Sst[:, d_, :], in0=Sst[:, d_, :], scalar=sc[:, 2 + d_:3 + d_],
                                                                                 in1=pb[:, col:col + 128], op0=ALU.mult, op1=ALU.add),
                      [Sstb[d_], scb, pbb], [Sstb[d_]])
        for g in range(NCH // 4):
            gs = slice(g * 512, (g + 1) * 512)
            pS, pSb = cx.bank()
            for r in range(4):
                n = g * 4 + r
                kb.op("pe", lambda e, n=n, r=r, pS=pS: e.matmul(pS[:, r * 128:(r + 1) * 128], A1[:, n * 128:(n + 1) * 128], A0[:, n * 128:(n + 1) * 128],
                                                           start=True, stop=True), [A0b, A1b], [pSb])
            pt, ptb = PT[g % 2], PTb[g % 2]
            kb.op("dve", lambda e, pS=pS, pt=pt: e.tensor_tensor(out=pt[:], in0=pS[:, :].rearrange("p (r i) -> p r i", r=4), in1=MT4[:], op=ALU.mult),
                  [pSb, MT4b], [ptb])
            qf, qfb_ = qfb[g % 2], qfbb[g % 2]
            for d_ in range(2):
                kb.op("pool", lambda e, d_=d_, qf=qf, gs=gs: e.tensor_tensor(out=qf[:, d_, :], in0=A0[:, gs], in1=qd4[:, d_].rearrange("p r i -> p (r i)"), op=ALU.mult),
                      [A0b, qd4b], [qfb_])
            pO, pOb = cx.bank()
            for r in range(4):
                n = g * 4 + r
                cs = slice(r * 128, (r + 1) * 128)
                kb.op("pe", lambda e, n=n, r=r, cs=cs, pO=pO, pt=pt: e.matmul(pO[:, cs], A4[:, n, :], pt[:, r, :], start=True, stop=False), [A4b, ptb], [pOb])
                kb.op("pe", lambda e, n=n, cs=cs, pO=pO, qf=qf: e.matmul(pO[:, cs], SfB[:, n, :], qf[:, 0, cs], start=False, stop=False), [SfBb, qfb_], [pOb])
                kb.op("pe", lambda e, n=n, cs=cs, pO=pO, qf=qf: e.matmul(pO[:, cs], SbB[:, n, :], qf[:, 1, cs], start=False, stop=True), [SbBb, qfb_], [pOb])
            emit_headnorm(cx, pO, pOb, rg_[:, 0:1], rgb, tmpo[:], tmpob, sq1, sq1b, rstd, rstdb, 512)
            os_, osb_ = ost[oi % 2], ostb[oi % 2]
            oi += 1
            kb.op("dve", lambda e, os_=os_, gs=gs: e.tensor_tensor(out=os_[:], in0=tmpo[:], in1=A2[:, gs], op=ALU.mult), [tmpob, A2b], [osb_])
            kb.dma(outT[u, 0, :, gs], os_[:], reads=[osb_], writes=[outb], eng="pool")
        kb.dma(A0[:], fm5[u, 3], writes=[A0b], eng="sync")
        kb.dma(A1[:], fm5[u, 4], writes=[A1b], eng="sync")
        kb.dma(A4[:], tm3[u, 2].rearrange("(n p) d -> p n d", p=128), writes=[A4b], eng="sync")
        kb.dma(nbias[:], nab[u], writes=[nbiasb], eng="sync")
        li = 0
        for pg in range(NCH // 4):
            gs = slice(pg * 512, (pg + 1) * 512)
            pO, pOb = cx.bank()
            pD, pDb = cx.bank()
            for r in range(4):
                p = pg * 4 + r
                kt0 = min(max(p - 2, 0), 59)
                var = {0: 0, 1: 1, 62: 3, 63: 4}.get(p, 2)
                pSa, pSab = cx.bank(exclude=(pOb, pDb))
                pSc, pScb = cx.bank(exclude=(pOb, pDb))
                for a in range(5):
                    tgt = pSa[:, a * 128:(a + 1) * 128] if a < 4 else pSc[:, 0:128]
                    tb_ = pSab if a < 4 else pScb
                    kb.op("pe", lambda e, tgt=tgt, a=a, kt0=kt0, p=p: e.matmul(tgt, A1[:, (kt0 + a) * 128:(kt0 + a + 1) * 128], A0[:, p * 128:(p + 1) * 128],
                                                                         start=True, stop=True), [A0b, A1b], [tb_])
                L, Lb_ = Lb[li % 2], Lbb[li % 2]
                E_, Eb_ = En[li % 2], Enb[li % 2]
                li += 1
                kb.op("dve", lambda e, L=L, pSa=pSa, var=var: e.tensor_tensor(out=L[:, 0:4, :], in0=pSa[:, :].rearrange("p (a q) -> p a q", a=4),
                                                                           in1=nbias[:, var, 0:4, :], op=ALU.add), [pSab, nbiasb], [Lb_])
                kb.op("dve", lambda e, L=L, pSc=pSc, var=var: e.tensor_tensor(out=L[:, 4, :], in0=pSc[:, 0:128], in1=nbias[:, var, 4, :], op=ALU.add),
                      [pScb, nbiasb], [Lb_])
                kb.op("act", lambda e, L=L, E_=E_: e.activation(out=E_[:], in_=L[:], func=AF.Exp), [Lb_], [Eb_])
                cs = slice(r * 128, (r + 1) * 128)
                for a in range(5):
                    kb.op("pe", lambda e, a=a, kt0=kt0, cs=cs, pO=pO, E_=E_: e.matmul(pO[:, cs], A4[:, kt0 + a, :], E_[:, a, :], start=(a == 0), stop=(a == 4)),
                          [A4b, Eb_], [pOb])
                for a in range(5):
                    kb.op("pe", lambda e, a=a, cs=cs, pD=pD, E_=E_: e.matmul(pD[:, cs], cx.ones[:], E_[:, a, :], start=(a == 0), stop=(a == 4)),
                          [cx.onesb, Eb_], [pDb])
            kb.op("dve", lambda e, pD=pD: e.reciprocal(out=rstd[:, :], in_=pD[:, :]), [pDb], [rstdb])
            os_, osb_ = ost[oi % 2], ostb[oi % 2]
            oi += 1
            kb.op("dve", lambda e, os_=os_, pO=pO: e.tensor_tensor(out=os_[:], in0=pO[:, :], in1=rstd[:, :], op=ALU.mult), [pOb, rstdb], [osb_])
            kb.dma(outT[u, 1, :, gs], os_[:], reads=[osb_], writes=[outb], eng="pool")
    return kb.emit([outb])


def retention_itab():
    j = np.arange(128, dtype=np.float32)[:, None]
    i = np.arange(128, dtype=np.float32)[None, :]
    A = np.maximum(i - j, 0.0)
    B = np.maximum(j - i, 0.0)
    r1 = np.broadcast_to(i + 1.0, (128, 128))
    r2 = np.broadcast_to(128.0 - i, (128, 128))
    return np.ascontiguousarray(np.concatenate([A, B, r1, r2, 127.0 - j, j], 1).astype(np.float32))


def na_bias_tables(rpb_h):
    out = np.full((5, 5, 128, 128), -30000.0, np.float32)
    kk = np.arange(128)
    for vi, p in enumerate((0, 1, 30, 62, 63)):
        kt0 = min(max(p - 2, 0), 59)
        rq = 2 * p + kk // 64
        cq = kk % 64
        rs = np.clip(rq - 4, 0, 120)
        cs_ = np.clip(cq - 8, 0, 48)
        for a in range(5):
            rk = 2 * (kt0 + a) + kk // 64
            ck = kk % 64
            okr = (rk[:, None] >= rs[None, :]) & (rk[:, None] < rs[None, :] + 8)
            okc = (ck[:, None] >= cs_[None, :]) & (ck[:, None] < cs_[None, :] + 16)
            dr = np.clip(rk[:, None] - rq[None, :] + 7, 0, 14)
            dcx = np.clip(ck[:, None] - cq[None, :] + 15, 0, 30)
            vals = rpb_h[dr, dcx]
            out[vi, a] = np.where(okr & okc, vals, np.float32(-30000.0))
    return np.ascontiguousarray(out.transpose(2, 0, 1, 3))


SC = float(128 ** -0.5)
EPI_EVEN = [("copy",)] * 8 + [("scale", SC)] * 8 + [("silu",)] * 8 + [("norm", 0, SC)] * 8 + [("norm", 1, 1.0)] * 8
TMS_EVEN = [SC, SC, 1.0, 1.0, 1.0, 1.0]
EPI_ODD = [("norm", 0, SC)] * 16 + [("norm", 1, 1.0)] * 16
TMS_ODD = [1.0] * 4


def _prog(key, fn):
    if key not in _CACHE:
        _CACHE[key] = fn()
    return _CACHE[key]


def _run(nc, in_maps):
    res = run_bass_kernel_spmd(nc, in_maps, core_ids=list(range(NCORES)))
    return res.results


def _lay_f1(W):
    g = W[:, :FH].reshape(KC, 128, FC, 128)
    u = W[:, FH:].reshape(KC, 128, FC, 128)
    return np.ascontiguousarray(np.stack([g, u], 0).transpose(3, 2, 1, 0, 4))


def _lay_tm(W):
    G = W.shape[1] // 512
    return np.ascontiguousarray(W.reshape(KC, 128, G, 512).transpose(2, 1, 0, 3))


def _bc(a, shape):
    return np.ascontiguousarray(np.broadcast_to(a, shape)).astype(np.float32)


def kernel(x, mem, norm_mix_g, norm_xattn_g, norm_mem_g, norm_ffn_g,
           w_in_ab, ret_decay_fwd, ret_decay_bwd, ret_out_g, na_q_g, na_k_g, na_rpb, w_out_ab,
           w_in_c, diff_q_g, diff_k_g, lambda_q1, lambda_k1, lambda_q2, lambda_k2, diff_out_g, w_out_c,
           w_xq, w_xkv, w_xo, xq_g, xk_g, w_ffn_in, w_ffn_out):
    f = lambda a: np.asarray(a, dtype=np.float32)
    x = f(x); mem = f(mem)
    DEPTH = 4
    lay = {}
    for i in range(DEPTH):
        j = i // 2
        if i % 2 == 0:
            W = f(w_in_ab[j])
            lay[("fm", i)] = lay_lhsT(np.concatenate([W[:, 0:1024], W[:, 1024:2048], W[:, 3072:4096], W[:, 4096:5120], W[:, 5120:6144]], 1))
            lay[("tm", i)] = _lay_tm(np.concatenate([W[:, 1024:2048], W[:, 2048:3072], W[:, 6144:7168]], 1))
            lay[("out", i)] = lay_lhsT(f(w_out_ab[j]))
        else:
            W = f(w_in_c[j])
            lay[("fm", i)] = lay_lhsT(W[:, 0:4096])
            lay[("tm", i)] = _lay_tm(W[:, 4096:6144])
            lay[("out", i)] = lay_lhsT(f(w_out_c[j]))
        lay[("xq", i)] = lay_lhsT(f(w_xq[i]))
        lay[("xk", i)] = lay_lhsT(f(w_xkv[i])[:, :512])
        lay[("xv", i)] = lay_rhs(f(w_xkv[i])[:, 512:])
        lay[("xo", i)] = lay_lhsT(f(w_xo[i]))
        lay[("f1", i)] = _lay_f1(f(w_ffn_in[i]))
        lay[("f2", i)] = lay_lhsT(f(w_ffn_out[i]))
    keys = list(lay.keys())
    sizes = [lay[k].size for k in keys]
    total = sum(sizes)
    unit = NCORES * 128 * 4096
    tot_pad = ((total + unit - 1) // unit) * unit
    flat = np.zeros(tot_pad, np.float32)
    off = 0
    for k, n in zip(keys, sizes):
        flat[off:off + n] = lay[k].ravel()
        off += n
    shards = flat.reshape(NCORES, 128, -1)
    outs = run_cast([shards[c] for c in range(NCORES)])
    flat_b = np.concatenate([np.asarray(o).reshape(-1) for o in outs])
    wb = {}
    off = 0
    for k, n in zip(keys, sizes):
        wb[k] = flat_b[off:off + n].reshape(lay[k].shape)
        off += n
    del flat, lay

    xs = x.reshape(NCORES, T, D)
    xT = [np.ascontiguousarray(xs[c].T.reshape(KC, 128, T)) for c in range(NCORES)]
    memT = [np.ascontiguousarray(mem[b].T.reshape(KC, 128, NM)) for b in range(2)]
    ident = np.eye(128, dtype=np.float32).astype(NPBF)
    itab = retention_itab()

    for i in range(DEPTH):
        j = i // 2
        even = (i % 2 == 0)
        if even:
            nc = _prog("P_even", lambda: build_P(EPI_EVEN, TMS_EVEN))
            gains = np.concatenate([lay_gain(f(norm_mix_g[i])), f(na_q_g[j])[:, None], f(na_k_g[j])[:, None]], 1)
        else:
            nc = _prog("P_odd", lambda: build_P(EPI_ODD, TMS_ODD))
            gains = np.concatenate([lay_gain(f(norm_mix_g[i])), f(diff_q_g[j])[:, None], f(diff_k_g[j])[:, None]], 1)
        gains = np.ascontiguousarray(gains.astype(np.float32))
        res = _run(nc, [{"xT": xT[c], "w_fm": wb[("fm", i)], "w_tm": wb[("tm", i)], "gains": gains} for c in range(NCORES)])
        fmT = [np.asarray(r["fmT"]) for r in res]
        tm = [np.asarray(r["tm"]) for r in res]
        mixT = [np.empty((KC, 128, T), NPBF) for _ in range(NCORES)]
        if even:
            nc = _prog("M_even", build_Meven)
            maps = []
            for c in range(NCORES):
                b = c // 4
                fm5 = np.empty((NU, 5, 128, S), NPBF)
                tm3 = np.empty((NU, 3, S, 128), NPBF)
                for u in range(NU):
                    h = (c % 4) * 2 + u
                    for q in range(4):
                        src = b * 4 + q
                        for t_ in range(5):
                            fm5[u, t_, :, q * T:(q + 1) * T] = fmT[src][t_ * 8 + h]
                        for t_ in range(3):
                            tm3[u, t_, q * T:(q + 1) * T, :] = tm[src][:, t_ * 1024 + h * 128:t_ * 1024 + (h + 1) * 128]
                hs = [(c % 4) * 2, (c % 4) * 2 + 1]
                dec = _bc(np.stack([f(ret_decay_fwd[j])[hs], f(ret_decay_bwd[j])[hs]], 1)[None], (128, NU, 2))
                nab = np.stack([na_bias_tables(f(na_rpb[j])[h]) for h in hs])
                maps.append({"fm5": fm5, "tm3": tm3, "dec": dec, "itab": itab, "retg": np.ascontiguousarray(f(ret_out_g[j])[:, None]), "nab": nab})
            res = _run(nc, maps)
            for c in range(NCORES):
                b, q = c // 4, c % 4
                for h in range(8):
                    o = np.asarray(res[b * 4 + h // 2]["mixT"])
                    mixT[c][h] = o[h % 2, 0, :, q * T:(q + 1) * T]
                    mixT[c][8 + h] = o[h % 2, 1, :, q * T:(q + 1) * T]
        else:
            nc = _prog("M_odd", build_Modd)
            lam_init = 0.8 - 0.6 * float(np.exp(-0.3 * i))
            lam4 = _bc(np.stack([f(lambda_q1[j]), f(lambda_k1[j]), f(lambda_q2[j]), f(lambda_k2[j])])[None], (128, 4, 128))
            gout = _bc(f(diff_out_g[j])[None], (128, 256))
            consts = _bc(np.array([lam_init, 1.0 - lam_init], np.float32)[None], (128, 2))
            maps = []
            for c in range(NCORES):
                b = c // 4
                qTa = np.empty((NU, 2, 128, S), NPBF)
                kTa = np.empty((NU, 2, 128, S), NPBF)
                va = np.empty((NU, S, 256), NPBF)
                ats, bts = [], []
                for u in range(NU):
                    h = (c % 4) * 2 + u
                    a_, b_ = alibi_tables(2.0 ** (-(h + 1)))
                    ats.append(a_); bts.append(b_)
                    for q in range(4):
                        src = b * 4 + q
                        for cc in range(2):
                            qTa[u, cc, :, q * T:(q + 1) * T] = fmT[src][h * 2 + cc]
                            kTa[u, cc, :, q * T:(q + 1) * T] = fmT[src][16 + h * 2 + cc]
                        va[u, q * T:(q + 1) * T, :] = tm[src][:, h * 256:(h + 1) * 256]
                maps.append({"qT": qTa, "kT": kTa, "v": va, "atab": np.stack(ats), "btile": np.stack(bts), "ident": ident,
                             "lam4": lam4, "gout": gout, "consts": consts})
            res = _run(nc, maps)
            for c in range(NCORES):
                b, q = c // 4, c % 4
                for h in range(8):
                    o = np.asarray(res[b * 4 + h // 2]["d_tm"])[h % 2, q * T:(q + 1) * T, :]
                    mixT[c][2 * h] = o[:, 0:128].T
                    mixT[c][2 * h + 1] = o[:, 128:256].T
        nc = _prog("C", build_C)
        gains = np.ascontiguousarray(np.concatenate([lay_gain(f(norm_xattn_g[i])), lay_gain(f(norm_ffn_g[i])), lay_gain(f(norm_mem_g[i])),
                                                     f(xq_g[i])[:, None], f(xk_g[i])[:, None]], 1).astype(np.float32))
        res = _run(nc, [{"xT": xT[c], "mixT": mixT[c], "memT": memT[c // 4], "w_out": wb[("out", i)], "w_xq": wb[("xq", i)],
                         "w_xk": wb[("xk", i)], "w_xv": wb[("xv", i)], "w_xo": wb[("xo", i)], "w_f1": wb[("f1", i)],
                         "w_f2": wb[("f2", i)], "gains": gains} for c in range(NCORES)])
        xT = [np.asarray(r["xTo"]) for r in res]
    out = np.stack([xT[c].reshape(D, T).T for c in range(NCORES)]).reshape(2, S, D)
    return np.ascontiguousarray(out.astype(np.float32))
```

```python
import numpy as np
import ml_dtypes
from contextlib import ExitStack
import concourse.bass as bass
import concourse.mybir as mybir
from concourse.bass_utils import run_bass_kernel_spmd

F32 = mybir.dt.float32
BF16 = mybir.dt.bfloat16
AF = mybir.ActivationFunctionType
ALU = mybir.AluOpType
NPBF = ml_dtypes.bfloat16

NCORES = 8
D = 2048
KC = 16
S = 8192
T = 2048
TB = 512
NTB = T // TB
FH = 5632
FC = 44
EPS = 1e-6
SAME_ENGINE_SYNC = True


class Buf:
    __slots__ = ("name", "writers", "readers", "war")

    def __init__(self, name):
        self.name = name
        self.writers = []
        self.readers = []
        self.war = []


class Op:
    __slots__ = ("eng", "idx", "fn", "deps", "dma", "semkey", "count", "signal")


class KB:
    ENGS = ("pe", "act", "dve", "pool", "sync")
    SEMID = 0

    def __init__(self, nc=None):
        self.fused = nc is not None
        self.nc = nc if nc is not None else bass.Bass("TRN2", target_bir_lowering=False)
        if self.fused:
            self.cm = self.nc.cleanup_on_exit()
            self.cm.__enter__()
        self.ops = {e: [] for e in self.ENGS}
        self.stack = ExitStack()
        self.dma_count = {}
        self.nbuf = 0

    def dram(self, name, shape, dtype, kind):
        return self.nc.dram_tensor(name, list(shape), dtype, kind=kind).ap()

    def sb(self, name, shape, dtype):
        return self.stack.enter_context(self.nc.sbuf_tensor(name, list(shape), dtype))

    def ps(self, name, shape, dtype=F32):
        return self.stack.enter_context(self.nc.psum_tensor(name, list(shape), dtype))

    def buf(self, name=None):
        self.nbuf += 1
        return Buf(name or f"b{self.nbuf}")

    def op(self, eng, fn, reads=(), writes=(), dma=False):
        o = Op()
        o.eng = eng
        o.fn = fn
        o.dma = dma
        o.signal = False
        o.count = None
        o.semkey = None
        deps = []
        for b in reads:
            deps.extend(b.writers)
        for b in writes:
            if b.readers:
                b.war = b.readers
                b.readers = []
                b.writers = []
            deps.extend(b.war)
        best = {}
        dmas = []
        for d in deps:
            if d.dma:
                dmas.append(d)
            else:
                if d.eng == eng and not dma and (eng == "pe" or not SAME_ENGINE_SYNC):
                    continue
                if d.eng not in best or d.idx > best[d.eng].idx:
                    best[d.eng] = d
        o.deps = list(best.values()) + list({id(d): d for d in dmas}.values())
        for d in o.deps:
            d.signal = True
        o.idx = len(self.ops[eng])
        self.ops[eng].append(o)
        if dma:
            key = writes[0]
            o.semkey = key
            self.dma_count[id(key)] = self.dma_count.get(id(key), 0) + 16
            o.count = self.dma_count[id(key)]
            o.signal = True
        for b in reads:
            b.readers.append(o)
        for b in writes:
            b.writers.append(o)
        return o

    def dma(self, out, in_, reads=(), writes=(), eng="sync"):
        return self.op(eng, lambda e: e.dma_start(out=out, in_=in_), reads, writes, dma=True)

    def emit(self, final_bufs):
        nc = self.nc
        st = self.stack
        def mksem(name):
            if self.fused:
                KB.SEMID += 1
                return nc.alloc_semaphore(name="%s_%d" % (name, KB.SEMID))
            return st.enter_context(nc.semaphore(name))
        eng_sem = {e: mksem("s_" + e) for e in ("pe", "act", "dve", "pool")}
        dma_sem = {}
        keys = {}
        for e in self.ENGS:
            for o in self.ops[e]:
                if o.dma and id(o.semkey) not in dma_sem:
                    dma_sem[id(o.semkey)] = mksem("d_%d" % len(dma_sem))
                    keys[id(o.semkey)] = o.semkey
        for e in ("pe", "act", "dve", "pool"):
            c = 0
            for o in self.ops[e]:
                if not o.dma and o.signal:
                    c += 1
                    o.count = c
        engobj = {"pe": None, "act": None, "dve": None, "pool": None, "sync": None}

        def run(e, eng):
            waited = {}
            for o in self.ops[e]:
                for d in o.deps:
                    if d.dma:
                        sem = dma_sem[id(d.semkey)]
                        k = ("d", id(d.semkey))
                    else:
                        sem = eng_sem[d.eng]
                        k = ("e", d.eng)
                    if waited.get(k, 0) >= d.count:
                        continue
                    waited[k] = d.count
                    eng.wait_ge(sem, d.count)
                ins = o.fn(eng)
                if o.dma:
                    ins.then_inc(dma_sem[id(o.semkey)], 16)
                elif o.signal:
                    ins.then_inc(eng_sem[e], 1)
            if e == "sync":
                for b in final_bufs:
                    eng.wait_ge(dma_sem[id(b)], self.dma_count[id(b)])

        with nc.Block() as block:
            @block.tensor
            def _(eng):
                run("pe", eng)

            @block.scalar
            def _(eng):
                run("act", eng)

            @block.vector
            def _(eng):
                run("dve", eng)

            @block.gpsimd
            def _(eng):
                run("pool", eng)

            @block.sync
            def _(eng):
                run("sync", eng)
        st.close()
        if self.fused:
            nc.all_engine_barrier()
            self.cm.__exit__(None, None, None)
        return nc


def build_cast(F):
    kb = KB()
    CH = 4096
    nch = F // CH
    assert F % CH == 0
    src = kb.dram("src", [128, F], F32, "ExternalInput")
    dst = kb.dram("dst", [128, F], BF16, "ExternalOutput")
    a = [kb.sb("a%d" % i, [128, CH], F32) for i in range(3)]
    b = [kb.sb("b%d" % i, [128, CH], BF16) for i in range(3)]
    ab = [kb.buf() for _ in range(3)]
    bb = [kb.buf() for _ in range(3)]
    ob = kb.buf("out")
    for i in range(nch):
        s = i % 3
        kb.dma(a[s][:], src[:, i * CH:(i + 1) * CH], writes=[ab[s]], eng="sync")
        e = ("dve", "act", "pool")[i % 3]
        if e == "act":
            kb.op("act", lambda eng, s=s: eng.copy(out=b[s][:], in_=a[s][:]), [ab[s]], [bb[s]])
        else:
            kb.op(e, lambda eng, s=s: eng.tensor_copy(out=b[s][:], in_=a[s][:]), [ab[s]], [bb[s]])
        kb.dma(dst[:, i * CH:(i + 1) * CH], b[s][:], reads=[bb[s]], writes=[ob], eng="sync")
    return kb.emit([ob])


_CACHE = {}


def run_cast(flat_list):
    F = flat_list[0].shape[1]
    key = ("cast", F)
    if key not in _CACHE:
        _CACHE[key] = build_cast(F)
    res = run_bass_kernel_spmd(_CACHE[key], [{"src": f} for f in flat_list], core_ids=list(range(NCORES)))
    return [r["dst"] for r in res.results]


class Ctx:
    def __init__(self, kb, npsum=6):
        self.kb = kb
        self.pbank = [kb.ps("pg%d" % i, [128, 512]) for i in range(npsum)]
        self.pbuf = [kb.buf("pg%d" % i) for i in range(npsum)]
        self.pi = 0
        self.pst = kb.ps("pstat", [128, 512])
        self.pstb = kb.buf("pstat")
        self.ones = kb.sb("ones_bf", [128, 128], BF16)
        self.onesb = kb.buf("ones")
        self.epst = kb.sb("eps_t", [128, 1], F32)
        self.epsb = kb.buf("eps")
        kb.op("pool", lambda e: e.memset(self.ones[:], 1.0), [], [self.onesb])
        kb.op("pool", lambda e: e.memset(self.epst[:], EPS), [], [self.epsb])

    def bank(self, exclude=()):
        while True:
            i = self.pi
            self.pi = (self.pi + 1) % len(self.pbank)
            if not any(self.pbuf[i] is x for x in exclude):
                return self.pbank[i], self.pbuf[i]


def emit_rstd(cx, sq_aps, sq_bufs, nfree, dim, rstd, rstdb):
    kb = cx.kb
    n = len(sq_aps)
    for i, a in enumerate(sq_aps):
        kb.op("pe", lambda e, a=a, i=i: e.matmul(cx.pst[:, 0:nfree], cx.ones[:], a, start=(i == 0), stop=(i == n - 1)),
              [cx.onesb] + list(sq_bufs), [cx.pstb])
    kb.op("act", lambda e: e.activation(out=rstd, in_=cx.pst[:, 0:nfree], func=AF.Sqrt, bias=cx.epst[:, 0:1], scale=1.0 / dim),
          [cx.pstb, cx.epsb], [rstdb])
    kb.op("dve", lambda e: e.reciprocal(out=rstd, in_=rstd), [rstdb], [rstdb])


def emit_norm_block(cx, xb, xbb, g, gb, hT, hTb, sq, sqb, rstd, rstdb, nfree=TB, nk=KC):
    kb = cx.kb
    kb.op("act", lambda e: e.activation(out=sq[:, 0:nk, 0:nfree], in_=xb[:, 0:nk, 0:nfree], func=AF.Square), [xbb], [sqb])
    emit_rstd(cx, [sq[:, k, 0:nfree] for k in range(nk)], [sqb], nfree, nk * 128, rstd[:, 0:nfree], rstdb)
    for k in range(nk):
        kb.op("dve", lambda e, k=k: e.scalar_tensor_tensor(out=hT[:, k, 0:nfree], in0=xb[:, k, 0:nfree], scalar=g[:, k:k + 1],
                                                            in1=rstd[:, 0:nfree], op0=ALU.mult, op1=ALU.mult),
              [xbb, gb, rstdb], [hTb])


def emit_headnorm(cx, pbank, pbuf, gcol, gb, out_ap, outb, sq1, sq1b, rstd, rstdb, nfree):
    kb = cx.kb
    kb.op("act", lambda e: e.activation(out=sq1[:, 0:nfree], in_=pbank[:, 0:nfree], func=AF.Square), [pbuf], [sq1b])
    emit_rstd(cx, [sq1[:, 0:nfree]], [sq1b], nfree, 128, rstd[:, 0:nfree], rstdb)
    kb.op("dve", lambda e: e.scalar_tensor_tensor(out=out_ap, in0=pbank[:, 0:nfree], scalar=gcol, in1=rstd[:, 0:nfree],
                                                  op0=ALU.mult, op1=ALU.mult), [pbuf, gb, rstdb], [outb])


class WPool:
    def __init__(self, kb, name, shape, n):
        self.kb = kb
        self.t = [kb.sb("%s%d" % (name, i), shape, BF16) for i in range(n)]
        self.b = [kb.buf("%s%d" % (name, i)) for i in range(n)]
        self.i = 0

    def load(self, src, eng="sync"):
        i = self.i
        self.i = (self.i + 1) % len(self.t)
        self.kb.dma(self.t[i][:], src, writes=[self.b[i]], eng=eng)
        return self.t[i], self.b[i]


def emit_linear_fm(cx, w, wb, nk, rhs_fn, rhs_bufs, nfree=TB):
    kb = cx.kb
    pb, pbb = cx.bank()
    for k in range(nk):
        kb.op("pe", lambda e, k=k: e.matmul(pb[:, 0:nfree], w[:, k, :], rhs_fn(k), start=(k == 0), stop=(k == nk - 1)),
              [wb] + list(rhs_bufs), [pbb])
    return pb, pbb


NM = 256
XH = 4


def build_C():
    kb = KB()
    xT = kb.dram("xT", [KC, 128, T], F32, "ExternalInput")
    mixT = kb.dram("mixT", [KC, 128, T], BF16, "ExternalInput")
    memT = kb.dram("memT", [KC, 128, NM], F32, "ExternalInput")
    w_out = kb.dram("w_out", [KC, 128, KC, 128], BF16, "ExternalInput")
    w_xq = kb.dram("w_xq", [XH, 128, KC, 128], BF16, "ExternalInput")
    w_xk = kb.dram("w_xk", [XH, 128, KC, 128], BF16, "ExternalInput")
    w_xv = kb.dram("w_xv", [128, KC, 512], BF16, "ExternalInput")
    w_xo = kb.dram("w_xo", [KC, 128, XH, 128], BF16, "ExternalInput")
    w_f1 = kb.dram("w_f1", [FC, 128, KC, 2, 128], BF16, "ExternalInput")
    w_f2 = kb.dram("w_f2", [KC, 128, FC, 128], BF16, "ExternalInput")
    gains = kb.dram("gains", [128, 3 * KC + 2], F32, "ExternalInput")
    xo = kb.dram("xTo", [KC, 128, T], F32, "ExternalOutput")

    cx = Ctx(kb)
    xb = kb.sb("xb", [128, KC, TB], F32); xbb = kb.buf("xb")
    mb = kb.sb("mb", [128, KC, TB], BF16); mbb = kb.buf("mb")
    hT = kb.sb("hT", [128, KC, TB], BF16); hTb = kb.buf("hT")
    aT = kb.sb("aT", [128, FC, TB], BF16); aTb = kb.buf("aT")
    rstd = kb.sb("rstd", [128, TB], F32); rstdb = kb.buf("rstd")
    sq1 = kb.sb("sq1", [128, TB], BF16); sq1b = kb.buf("sq1")
    gt = kb.sb("gt", [128, 3 * KC + 2], F32); gtb = kb.buf("gt")
    gq = kb.sb("gq", [128, 1], F32); gqb = kb.buf("gq")
    kx = kb.sb("kx", [128, XH, NM], BF16); kxb = kb.buf("kx")
    vx = kb.sb("vx", [128, 2, XH * 128], BF16); vxb = kb.buf("vx")
    qx = kb.sb("qx", [128, XH, TB], BF16); qxb = kb.buf("qx")
    E = kb.sb("E", [128, 2, TB], BF16); Eb = kb.buf("E")
    oT = kb.sb("oT", [128, XH, TB], BF16); oTb = kb.buf("oT")
    sg = [kb.sb("sg%d" % i, [128, TB], F32) for i in range(2)]; sgb = [kb.buf() for _ in range(2)]
    wA = WPool(kb, "wA", [128, KC, 128], 2)
    wO = WPool(kb, "wO", [128, XH, 128], 2)
    wF1 = WPool(kb, "wF1", [128, KC, 2, 128], 2)
    wF2 = WPool(kb, "wF2", [128, FC, 128], 2)
    outb = kb.buf("out")

    kb.dma(gt[:], gains[:, :], writes=[gtb], eng="pool")
    kb.op("dve", lambda e: e.tensor_scalar(out=gq[:], in0=gt[:, 3 * KC:3 * KC + 1], scalar1=float(128 ** -0.5), scalar2=None, op0=ALU.mult),
          [gtb], [gqb])

    mf = xb
    for k in range(KC):
        pass
    kb.dma(xb[:, :, 0:NM], memT.rearrange("k p t -> p k t"), writes=[xbb], eng="sync")
    emit_norm_block(cx, xb, xbb, gt[:, 2 * KC:3 * KC], gtb, hT, hTb, aT, aTb, rstd, rstdb, nfree=NM)
    for h in range(XH):
        w, wb = wA.load(w_xk[h])
        pb, pbb = emit_linear_fm(cx, w, wb, KC, lambda k: hT[:, k, 0:NM], [hTb], nfree=NM)
        emit_headnorm(cx, pb, pbb, gt[:, 3 * KC + 1:3 * KC + 2], gtb, kx[:, h, :], kxb, sq1, sq1b, rstd, rstdb, NM)
    wv = aT[:, 16:32, :]
    kb.dma(wv, w_xv[:, :, :], reads=[], writes=[aTb], eng="sync")
    for mt in range(2):
        pb, pbb = cx.bank()
        for k in range(KC):
            kb.op("pe", lambda e, k=k, mt=mt, pb=pb: e.matmul(pb[:, :], hT[:, k, mt * 128:(mt + 1) * 128], aT[:, 16 + k, :],
                                                         start=(k == 0), stop=(k == KC - 1)), [hTb, aTb], [pbb])
        kb.op("act", lambda e, mt=mt, pb=pb: e.copy(out=vx[:, mt, :], in_=pb[:, :]), [pbb], [vxb])

    for tb in range(NTB):
        ts = slice(tb * TB, (tb + 1) * TB)
        kb.dma(xb[:], xT[:, :, ts].rearrange("k p t -> p k t"), writes=[xbb], eng="sync")
        kb.dma(mb[:], mixT[:, :, ts].rearrange("k p t -> p k t"), writes=[mbb], eng="pool")
        for i in range(KC):
            w, wb = wA.load(w_out[i])
            pb, pbb = emit_linear_fm(cx, w, wb, KC, lambda k: mb[:, k, :], [mbb])
            kb.op("dve", lambda e, i=i, pb=pb: e.tensor_tensor(out=xb[:, i, :], in0=xb[:, i, :], in1=pb[:, :], op=ALU.add), [pbb, xbb], [xbb])
        emit_norm_block(cx, xb, xbb, gt[:, 0:KC], gtb, hT, hTb, aT, aTb, rstd, rstdb)
        for h in range(XH):
            w, wb = wA.load(w_xq[h])
            pb, pbb = emit_linear_fm(cx, w, wb, KC, lambda k: hT[:, k, :], [hTb])
            emit_headnorm(cx, pb, pbb, gq[:, 0:1], gqb, qx[:, h, :], qxb, sq1, sq1b, rstd, rstdb, TB)
        for h in range(XH):
            for mt in range(2):
                pb, pbb = cx.bank()
                kb.op("pe", lambda e, h=h, mt=mt, pb=pb: e.matmul(pb[:, :], kx[:, h, mt * 128:(mt + 1) * 128], qx[:, h, :], start=True, stop=True),
                      [kxb, qxb], [pbb])
                kb.op("act", lambda e, mt=mt, pb=pb: e.activation(out=E[:, mt, :], in_=pb[:, :], func=AF.Exp), [pbb], [Eb])
            po, pob = cx.bank()
            pd, pdb = cx.bank()
            for mt in range(2):
                kb.op("pe", lambda e, h=h, mt=mt, po=po: e.matmul(po[:, :], vx[:, mt, h * 128:(h + 1) * 128], E[:, mt, :], start=(mt == 0), stop=(mt == 1)),
                      [vxb, Eb], [pob])
            for mt in range(2):
                kb.op("pe", lambda e, mt=mt, pd=pd: e.matmul(pd[:, :], cx.ones[:], E[:, mt, :], start=(mt == 0), stop=(mt == 1)),
                      [cx.onesb, Eb], [pdb])
            kb.op("dve", lambda e, pd=pd: e.reciprocal(out=rstd[:, :], in_=pd[:, :]), [pdb], [rstdb])
            kb.op("dve", lambda e, h=h, po=po: e.tensor_tensor(out=oT[:, h, :], in0=po[:, :], in1=rstd[:, :], op=ALU.mult), [pob, rstdb], [oTb])
        for i in range(KC):
            w, wb = wO.load(w_xo[i])
            pb, pbb = emit_linear_fm(cx, w, wb, XH, lambda k: oT[:, k, :], [oTb])
            kb.op("dve", lambda e, i=i, pb=pb: e.tensor_tensor(out=xb[:, i, :], in0=xb[:, i, :], in1=pb[:, :], op=ALU.add), [pbb, xbb], [xbb])
        emit_norm_block(cx, xb, xbb, gt[:, KC:2 * KC], gtb, hT, hTb, aT, aTb, rstd, rstdb)
        for j in range(FC):
            w, wb = wF1.load(w_f1[j])
            pg, pgb = cx.bank()
            pu, pub = cx.bank()
            for k in range(KC):
                kb.op("pe", lambda e, k=k, w=w, pg=pg: e.matmul(pg[:, :], w[:, k, 0, :], hT[:, k, :], start=(k == 0), stop=(k == KC - 1)), [wb, hTb], [pgb])
            for k in range(KC):
                kb.op("pe", lambda e, k=k, w=w, pu=pu: e.matmul(pu[:, :], w[:, k, 1, :], hT[:, k, :], start=(k == 0), stop=(k == KC - 1)), [wb, hTb], [pub])
            s = j % 2
            kb.op("act", lambda e, s=s, pg=pg: e.activation(out=sg[s][:], in_=pg[:, :], func=AF.Silu), [pgb], [sgb[s]])
            kb.op("dve", lambda e, s=s, j=j, pu=pu: e.tensor_tensor(out=aT[:, j, :], in0=sg[s][:], in1=pu[:, :], op=ALU.mult), [sgb[s], pub], [aTb])
        for i in range(KC):
            w, wb = wF2.load(w_f2[i])
            pb, pbb = emit_linear_fm(cx, w, wb, FC, lambda k: aT[:, k, :], [aTb])
            kb.op("dve", lambda e, i=i, pb=pb: e.tensor_tensor(out=xb[:, i, :], in0=xb[:, i, :], in1=pb[:, :], op=ALU.add), [pbb, xbb], [xbb])
        kb.dma(xo[:, :, ts].rearrange("k p t -> p k t"), xb[:], reads=[xbb], writes=[outb], eng="sync")
    return kb.emit([outb])


def lay_lhsT(W, mchunk=128):
    Kd, M = W.shape
    return np.ascontiguousarray(W.reshape(Kd // 128, 128, M // 128, 128).transpose(2, 1, 0, 3))


def lay_rhs(W):
    Kd, N = W.shape
    return np.ascontiguousarray(W.reshape(Kd // 128, 128, N).transpose(1, 0, 2))


def lay_gain(g):
    return np.ascontiguousarray(g.reshape(-1, 128).T)


def build_P(fm_epi, tm_scales):
    NF = len(fm_epi)
    NG = len(tm_scales)
    kb = KB()
    xT = kb.dram("xT", [KC, 128, T], F32, "ExternalInput")
    w_fm = kb.dram("w_fm", [NF, 128, KC, 128], BF16, "ExternalInput")
    w_tm = kb.dram("w_tm", [NG, 128, KC, 512], BF16, "ExternalInput")
    gains = kb.dram("gains", [128, KC + 2], F32, "ExternalInput")
    fmo = kb.dram("fmT", [NF, 128, T], BF16, "ExternalOutput")
    tmo = kb.dram("tm", [T, NG * 512], BF16, "ExternalOutput")

    cx = Ctx(kb)
    xb = kb.sb("xb", [128, KC, TB], F32); xbb = kb.buf("xb")
    hT = kb.sb("hT", [128, KC, TB], BF16); hTb = kb.buf("hT")
    sq = kb.sb("sq", [128, KC, TB], BF16); sqb = kb.buf("sq")
    rstd = kb.sb("rstd", [128, TB], F32); rstdb = kb.buf("rstd")
    sq1 = kb.sb("sq1", [128, TB], BF16); sq1b = kb.buf("sq1")
    gt = kb.sb("gt", [128, KC + 2], F32); gtb = kb.buf("gt")
    gs = kb.sb("gs", [128, 2], F32); gsb = kb.buf("gs")
    wF = WPool(kb, "wF", [128, KC, 128], 3)
    wT = WPool(kb, "wT", [128, KC, 512], 2)
    GRP = 8
    of = [kb.sb("of%d" % i, [128, GRP, TB], BF16) for i in range(2)]; ofb = [kb.buf() for _ in range(2)]
    ot = [kb.sb("ot%d" % i, [128, 4, 512], BF16) for i in range(2)]; otb = [kb.buf() for _ in range(2)]
    outb = kb.buf("out")
    outb2 = kb.buf("out2")

    kb.dma(gt[:], gains[:, :], writes=[gtb], eng="pool")
    norm_c = {}
    for ep in fm_epi:
        if ep[0] == "norm":
            norm_c[ep[1]] = ep[2]
    for col, c in norm_c.items():
        kb.op("dve", lambda e, col=col, c=c: e.tensor_scalar(out=gs[:, col:col + 1], in0=gt[:, KC + col:KC + col + 1], scalar1=float(c),
                                                              scalar2=None, op0=ALU.mult), [gtb], [gsb])
    for tb in range(NTB):
        ts = slice(tb * TB, (tb + 1) * TB)
        kb.dma(xb[:], xT[:, :, ts].rearrange("k p t -> p k t"), writes=[xbb], eng="sync")
        emit_norm_block(cx, xb, xbb, gt[:, 0:KC], gtb, hT, hTb, sq, sqb, rstd, rstdb)
        for j in range(NF):
            gi = (j // GRP) % 2
            w, wb = wF.load(w_fm[j])
            pb, pbb = emit_linear_fm(cx, w, wb, KC, lambda k: hT[:, k, :], [hTb])
            dst = of[gi][:, j % GRP, :]
            ep = fm_epi[j]
            if ep[0] == "copy":
                kb.op("act", lambda e, dst=dst, pb=pb: e.copy(out=dst, in_=pb[:, :]), [pbb], [ofb[gi]])
            elif ep[0] == "scale":
                kb.op("act", lambda e, dst=dst, pb=pb, c=ep[1]: e.mul(out=dst, in_=pb[:, :], mul=float(c)), [pbb], [ofb[gi]])
            elif ep[0] == "silu":
                kb.op("act", lambda e, dst=dst, pb=pb: e.activation(out=dst, in_=pb[:, :], func=AF.Silu), [pbb], [ofb[gi]])
            else:
                emit_headnorm(cx, pb, pbb, gs[:, ep[1]:ep[1] + 1], gsb, dst, ofb[gi], sq1, sq1b, rstd, rstdb, TB)
            if j % GRP == GRP - 1 or j == NF - 1:
                j0 = (j // GRP) * GRP
                n = j - j0 + 1
                kb.dma(fmo[j0:j0 + n, :, ts].rearrange("k p t -> p k t"), of[gi][:, 0:n, :], reads=[ofb[gi]], writes=[outb], eng="pool")
        for g in range(NG):
            w, wb = wT.load(w_tm[g])
            oi = g % 2
            for tt in range(4):
                pb, pbb = cx.bank()
                for k in range(KC):
                    kb.op("pe", lambda e, k=k, tt=tt, pb=pb, w=w: e.matmul(pb[:, :], hT[:, k, tt * 128:(tt + 1) * 128], w[:, k, :],
                                                                      start=(k == 0), stop=(k == KC - 1)), [hTb, wb], [pbb])
                kb.op("act", lambda e, tt=tt, pb=pb, oi=oi, c=tm_scales[g]: e.mul(out=ot[oi][:, tt, :], in_=pb[:, :], mul=float(c)), [pbb], [otb[oi]])
            kb.dma(tmo[tb * TB:(tb + 1) * TB, g * 512:(g + 1) * 512].rearrange("(a p) c -> p a c", p=128), ot[oi][:], reads=[otb[oi]],
                   writes=[outb2], eng="pool")
    return kb.emit([outb, outb2])


AX = mybir.AxisListType
NU = 2
QB = 512
NQB = S // QB
NKT = S // 128
MODD_WIN = (9, None)


def build_Modd():
    kb = KB()
    qT = kb.dram("qT", [NU, 2, 128, S], BF16, "ExternalInput")
    kT = kb.dram("kT", [NU, 2, 128, S], BF16, "ExternalInput")
    v = kb.dram("v", [NU, S, 256], BF16, "ExternalInput")
    atab = kb.dram("atab", [NU, 128, 140], F32, "ExternalInput")
    btile = kb.dram("btile", [NU, 128, 3, 128], BF16, "ExternalInput")
    ident_d = kb.dram("ident", [128, 128], BF16, "ExternalInput")
    lam4 = kb.dram("lam4", [128, 4, 128], F32, "ExternalInput")
    gout = kb.dram("gout", [128, 256], F32, "ExternalInput")
    consts = kb.dram("consts", [128, 2], F32, "ExternalInput")
    dout = kb.dram("d_tm", [NU, S, 256], BF16, "ExternalOutput")

    sps = [kb.ps("sps%d" % i, [128, 512]) for i in range(3)]; spsb = [kb.buf() for _ in range(3)]
    acc = [kb.ps("acc%d" % i, [128, 512]) for i in range(4)]; accb = [kb.buf() for _ in range(4)]
    kTs = kb.sb("kTs", [128, 2, S], BF16); kTb = kb.buf("kTs")
    Vx = kb.sb("Vx", [128, NKT, 257], BF16); Vxb = kb.buf("Vx")
    qbl = [kb.sb("qbl%d" % i, [128, 2, QB], BF16) for i in range(2)]; qblb = [kb.buf() for _ in range(2)]
    Es = [kb.sb("E%d" % i, [128, QB], BF16) for i in range(3)]; Esb = [kb.buf() for _ in range(3)]
    O = [kb.sb("O%d" % i, [128, 4, 257], F32) for i in range(2)]; Ob = [kb.buf() for _ in range(2)]
    at = kb.sb("at", [128, 140], F32); atb = kb.buf("at")
    bt = kb.sb("bt", [128, 3, 128], BF16); btb = kb.buf("bt")
    ident = kb.sb("ident_sb", [128, 128], BF16); identb = kb.buf("ident")
    l4 = kb.sb("l4", [128, 4, 128], F32); l4b = kb.buf("l4")
    gfin = kb.sb("gfin", [128, 256], F32); gfinb = kb.buf("gfin")
    cst = kb.sb("cst", [128, 2], F32); cstb = kb.buf("cst")
    epst = kb.sb("epst", [128, 1], F32); epsb = kb.buf("eps")
    sm = kb.sb("sm", [128, 8], F32); smb = kb.buf("sm")
    y = kb.sb("y", [128, 256], F32); yb = kb.buf("y")
    y2 = kb.sb("y2", [128, 256], F32); y2b = kb.buf("y2")
    rr = kb.sb("rr", [128, 4], F32); rrb = kb.buf("rr")
    ost = [kb.sb("ost%d" % i, [128, 4, 256], BF16) for i in range(2)]; ostb = [kb.buf() for _ in range(2)]
    outb = kb.buf("out")

    kb.dma(ident[:], ident_d[:, :], writes=[identb], eng="pool")
    kb.dma(l4[:], lam4[:, :, :], writes=[l4b], eng="pool")
    kb.dma(gfin[:], gout[:, :], writes=[gfinb], eng="pool")
    kb.dma(cst[:], consts[:, :], writes=[cstb], eng="pool")
    kb.op("pool", lambda e: e.memset(epst[:], EPS), [], [epsb])
    kb.op("pool", lambda e: e.memset(Vx[:, :, 256:257], 1.0), [], [Vxb])
    kb.op("dve", lambda e: e.tensor_tensor(out=y[:, 0:128], in0=l4[:, 0, :], in1=l4[:, 1, :], op=ALU.mult), [l4b], [yb])
    kb.op("dve", lambda e: e.tensor_reduce(out=sm[:, 0:1], in_=y[:, 0:128], axis=AX.X, op=ALU.add), [yb], [smb])
    kb.op("dve", lambda e: e.tensor_tensor(out=y[:, 0:128], in0=l4[:, 2, :], in1=l4[:, 3, :], op=ALU.mult), [l4b, smb], [yb])
    kb.op("dve", lambda e: e.tensor_reduce(out=sm[:, 1:2], in_=y[:, 0:128], axis=AX.X, op=ALU.add), [yb], [smb])
    kb.op("act", lambda e: e.activation(out=sm[:, 4:6], in_=sm[:, 0:2], func=AF.Exp), [smb], [smb])
    kb.op("dve", lambda e: e.tensor_tensor(out=sm[:, 2:3], in0=sm[:, 4:5], in1=sm[:, 5:6], op=ALU.subtract), [smb], [smb])
    kb.op("dve", lambda e: e.tensor_tensor(out=sm[:, 2:3], in0=sm[:, 2:3], in1=cst[:, 0:1], op=ALU.add), [smb, cstb], [smb])
    kb.op("dve", lambda e: e.tensor_scalar(out=sm[:, 3:4], in0=sm[:, 2:3], scalar1=-1.0, scalar2=None, op0=ALU.mult), [smb], [smb])
    kb.op("dve", lambda e: e.tensor_scalar(out=gfin[:], in0=gfin[:], scalar1=cst[:, 1:2], scalar2=None, op0=ALU.mult), [gfinb, cstb], [gfinb])

    LOOK = 2
    for u in range(NU):
        kb.dma(kTs[:, 0, :], kT[u, 0], writes=[kTb], eng="sync")
        kb.dma(kTs[:, 1, :], kT[u, 1], writes=[kTb], eng="sync")
        for h4 in range(4):
            kb.dma(Vx[:, h4 * 16:(h4 + 1) * 16, 0:256], v[u, h4 * 2048:(h4 + 1) * 2048, :].rearrange("(t p) e -> p t e", p=128),
                   writes=[Vxb], eng="sync")
        kb.dma(at[:], atab[u], writes=[atb], eng="pool")
        kb.dma(bt[:], btile[u], writes=[btb], eng="pool")
        W = MODD_WIN[u]
        def krange(qb):
            if W is None:
                return 0, NKT
            return max(0, qb * 4 - W), min(NKT, qb * 4 + 4 + W)
        iters = [(qb, c, kt) for qb in range(NQB) for c in range(2) for kt in range(*krange(qb))]
        N = len(iters)

        def s_part(idx):
            qb, c, kt = iters[idx]
            qs = qb % 2
            if c == 0 and kt == krange(qb)[0]:
                kb.dma(qbl[qs][:], qT[u, :, :, qb * QB:(qb + 1) * QB].rearrange("c p t -> p c t"), writes=[qblb[qs]], eng="sync")
            kq = kt // 4
            sp, spb = sps[idx % 3], spsb[idx % 3]
            E, Eb = Es[idx % 3], Esb[idx % 3]
            diag = (kq == qb)
            kb.op("pe", lambda e: e.matmul(sp[:, :], kTs[:, c, kt * 128:(kt + 1) * 128], qbl[qs][:, c, :], start=True, stop=(not diag)),
                  [kTb, qblb[qs]], [spb])
            if diag:
                sbp = kt % 4
                for sb in range(4):
                    dl = sb - sbp
                    ti = 2 if dl == 0 else (0 if dl > 0 else 1)
                    kb.op("pe", lambda e, sb=sb, ti=ti: e.matmul(sp[:, sb * 128:(sb + 1) * 128], ident[:], bt[:, ti, :], start=False, stop=True,
                                                                  skip_group_check=True), [identb, btb], [spb])
                for sb in range(4):
                    dl = abs(sb - sbp)
                    kb.op("act", lambda e, sb=sb, dl=dl: e.activation(out=E[:, sb * 128:(sb + 1) * 128], in_=sp[:, sb * 128:(sb + 1) * 128],
                                                                      func=AF.Exp, bias=at[:, 136 + dl:137 + dl], scale=1.0), [spb, atb], [Eb])
            else:
                col = (qb * 4 - kt) if kq < qb else (64 + kt - qb * 4)
                kb.op("act", lambda e: e.activation(out=E[:, :], in_=sp[:, :], func=AF.Exp, bias=at[:, col:col + 1], scale=1.0), [spb, atb], [Eb])

        def pv_part(idx):
            qb, c, kt = iters[idx]
            kq = kt // 4
            E, Eb = Es[idx % 3], Esb[idx % 3]
            diag = (kq == qb)
            lo, hi = krange(qb)
            if kq < qb:
                first, last, ph = (kt == lo), (kt == qb * 4 - 1), 0
            elif diag:
                first, last, ph = (kt == qb * 4), (kt == qb * 4 + 3), 1
            else:
                first, last, ph = (kt == qb * 4 + 4), (kt == hi - 1), 2
            for sb in range(4):
                kb.op("pe", lambda e, sb=sb: e.matmul(acc[sb][:, 0:257], E[:, sb * 128:(sb + 1) * 128], Vx[:, kt, :], start=first, stop=last),
                      [Eb, Vxb], [accb[sb]])
            if last:
                for sb in range(4):
                    if ph == 0:
                        kb.op("dve", lambda e, sb=sb: e.tensor_scalar(out=O[c][:, sb, :], in0=acc[sb][:, 0:257], scalar1=at[:, 128 + sb:129 + sb],
                                                                     scalar2=None, op0=ALU.mult), [accb[sb], atb], [Ob[c]])
                    elif ph == 1:
                        if lo >= qb * 4:
                            kb.op("dve", lambda e, sb=sb: e.tensor_copy(out=O[c][:, sb, :], in_=acc[sb][:, 0:257]), [accb[sb]], [Ob[c]])
                        else:
                            kb.op("dve", lambda e, sb=sb: e.tensor_tensor(out=O[c][:, sb, :], in0=O[c][:, sb, :], in1=acc[sb][:, 0:257], op=ALU.add),
                                  [accb[sb], Ob[c]], [Ob[c]])
                    else:
                        kb.op("dve", lambda e, sb=sb: e.scalar_tensor_tensor(out=O[c][:, sb, :], in0=acc[sb][:, 0:257], scalar=at[:, 132 + sb:133 + sb],
                                                                            in1=O[c][:, sb, :], op0=ALU.mult, op1=ALU.add), [accb[sb], atb, Ob[c]], [Ob[c]])
            final = (c == 1) and (kt == hi - 1)
            if final:
                combine(qb)

        def combine(qb):
            os_ = qb % 2
            for sb in range(4):
                kb.op("dve", lambda e, sb=sb: e.reciprocal(out=rr[:, 0:1], in_=O[0][:, sb, 256:257]), [Ob[0], rrb], [rrb])
                kb.op("dve", lambda e, sb=sb: e.reciprocal(out=rr[:, 1:2], in_=O[1][:, sb, 256:257]), [Ob[1], rrb], [rrb])
                kb.op("dve", lambda e: e.tensor_tensor(out=rr[:, 1:2], in0=rr[:, 1:2], in1=sm[:, 3:4], op=ALU.mult), [rrb, smb], [rrb])
                kb.op("dve", lambda e, sb=sb: e.tensor_scalar(out=y[:], in0=O[0][:, sb, 0:256], scalar1=rr[:, 0:1], scalar2=None, op0=ALU.mult),
                      [Ob[0], rrb, yb], [yb])
                kb.op("dve", lambda e, sb=sb: e.scalar_tensor_tensor(out=y[:], in0=O[1][:, sb, 0:256], scalar=rr[:, 1:2], in1=y[:], op0=ALU.mult, op1=ALU.add),
                      [Ob[1], rrb, yb], [yb])
                kb.op("dve", lambda e: e.tensor_tensor(out=y2[:], in0=y[:], in1=y[:], op=ALU.mult), [yb, y2b], [y2b])
                kb.op("dve", lambda e: e.tensor_reduce(out=rr[:, 2:3], in_=y2[:], axis=AX.X, op=ALU.add), [y2b, rrb], [rrb])
                kb.op("act", lambda e: e.activation(out=rr[:, 3:4], in_=rr[:, 2:3], func=AF.Sqrt, bias=epst[:, 0:1], scale=1.0 / 256), [rrb, epsb], [rrb])
                kb.op("dve", lambda e: e.reciprocal(out=rr[:, 3:4], in_=rr[:, 3:4]), [rrb], [rrb])
                kb.op("dve", lambda e, sb=sb: e.scalar_tensor_tensor(out=ost[os_][:, sb, :], in0=y[:], scalar=rr[:, 3:4], in1=gfin[:], op0=ALU.mult, op1=ALU.mult),
                      [yb, rrb, gfinb], [ostb[os_]])
            kb.dma(dout[u, qb * QB:(qb + 1) * QB, :].rearrange("(s p) e -> p s e", p=128), ost[os_][:], reads=[ostb[os_]], writes=[outb], eng="pool")

        for idx in range(N + LOOK):
            if idx < N:
                s_part(idx)
            if idx >= LOOK:
                pv_part(idx - LOOK)
    return kb.emit([outb])


def alibi_tables(slope):
    p = np.arange(128, dtype=np.float64)
    at = np.zeros((128, 140), np.float64)
    for dist in range(64):
        at[:, dist] = slope * (p - 128.0 * dist)
        at[:, 64 + dist] = -slope * (p + 128.0 * dist - 511.0)
    for sb in range(4):
        at[:, 128 + sb] = np.exp(-slope * (sb * 128 + p))
        at[:, 132 + sb] = np.exp(-slope * (511 - sb * 128 - p))
        at[:, 136 + sb] = -slope * sb * 128.0
    pk = p[:, None]; pq = p[None, :]
    G = -slope * (pq - pk)
    bt = np.stack([G, -G, -slope * np.abs(pq - pk)], 1)
    return at.astype(np.float32), bt.astype(NPBF)


NCH = S // 128


def build_Meven():
    kb = KB()
    fm5 = kb.dram("fm5", [NU, 5, 128, S], BF16, "ExternalInput")
    tm3 = kb.dram("tm3", [NU, 3, S, 128], BF16, "ExternalInput")
    dec = kb.dram("dec", [128, NU, 2], F32, "ExternalInput")
    itab = kb.dram("itab", [128, 4 * 128 + 2], F32, "ExternalInput")
    retg = kb.dram("retg", [128, 1], F32, "ExternalInput")
    nab = kb.dram("nab", [NU, 128, 5, 5, 128], F32, "ExternalInput")
    outT = kb.dram("mixT", [NU, 2, 128, S], BF16, "ExternalOutput")

    cx = Ctx(kb)
    A0 = kb.sb("A0", [128, S], BF16); A0b = kb.buf("A0")
    A1 = kb.sb("A1", [128, S], BF16); A1b = kb.buf("A1")
    A2 = kb.sb("A2", [128, S], BF16); A2b = kb.buf("A2")
    A3 = kb.sb("A3", [128, NCH, 128], BF16); A3b = kb.buf("A3")
    A4 = kb.sb("A4", [128, NCH, 128], BF16); A4b = kb.buf("A4")
    SfB = kb.sb("SfB", [128, NCH, 128], BF16); SfBb = kb.buf("SfB")
    SbB = kb.sb("SbB", [128, NCH, 128], BF16); SbBb = kb.buf("SbB")
    Sst = kb.sb("Sst", [128, 2, 128], F32); Sstb = [kb.buf("Sf"), kb.buf("Sb")]
    it = kb.sb("it", [128, 4 * 128 + 2], F32); itb = kb.buf("it")
    dc = kb.sb("dc", [128, NU, 2], F32); dcb = kb.buf("dc")
    rg_ = kb.sb("rg_", [128, 1], F32); rgb = kb.buf("rg")
    lgw = kb.sb("lgw", [128, 8], F32); lgwb = kb.buf("lgw")
    sc = kb.sb("sc", [128, 4], F32); scb = kb.buf("sc")
    MT4 = kb.sb("MT4", [128, 4, 128], F32); MT4b = kb.buf("MT4")
    qd4 = kb.sb("qd4", [128, 2, 4, 128], F32); qd4b = kb.buf("qd4")
    tmpm = kb.sb("tmpm", [128, 128], F32); tmpmb = kb.buf("tmpm")
    ksc = [kb.sb("ksc%d" % i, [128, 128], BF16) for i in range(4)]; kscb = [kb.buf() for _ in range(4)]
    PT = [kb.sb("PT%d" % i, [128, 4, 128], BF16) for i in range(2)]; PTb = [kb.buf() for _ in range(2)]
    qfb = [kb.sb("qfb%d" % i, [128, 2, 512], BF16) for i in range(2)]; qfbb = [kb.buf() for _ in range(2)]
    rstd = kb.sb("rstd", [128, 512], F32); rstdb = kb.buf("rstd")
    sq1 = kb.sb("sq1", [128, 512], BF16); sq1b = kb.buf("sq1")
    tmpo = kb.sb("tmpo", [128, 512], F32); tmpob = kb.buf("tmpo")
    ost = [kb.sb("ost%d" % i, [128, 512], BF16) for i in range(2)]; ostb = [kb.buf() for _ in range(2)]
    nbias = kb.sb("nbias", [128, 5, 5, 128], F32); nbiasb = kb.buf("nbias")
    Lb = [kb.sb("Lb%d" % i, [128, 5, 128], F32) for i in range(2)]; Lbb = [kb.buf() for _ in range(2)]
    En = [kb.sb("En%d" % i, [128, 5, 128], BF16) for i in range(2)]; Enb = [kb.buf() for _ in range(2)]
    outb = kb.buf("out")

    kb.dma(it[:], itab[:, :], writes=[itb], eng="pool")
    kb.dma(dc[:], dec[:, :, :], writes=[dcb], eng="pool")
    kb.dma(rg_[:], retg[:, :], writes=[rgb], eng="pool")
    oi = 0
    for u in range(NU):
        kb.dma(A0[:], fm5[u, 0], writes=[A0b], eng="sync")
        kb.dma(A1[:], fm5[u, 1], writes=[A1b], eng="sync")
        kb.dma(A2[:], fm5[u, 2], writes=[A2b], eng="sync")
        kb.dma(A3[:], tm3[u, 0].rearrange("(n p) d -> p n d", p=128), writes=[A3b], eng="sync")
        kb.dma(A4[:], tm3[u, 1].rearrange("(n p) d -> p n d", p=128), writes=[A4b], eng="sync")
        kb.op("act", lambda e, u=u: e.activation(out=lgw[:, 0:2], in_=dc[:, u, :], func=AF.Exp, scale=-1.0), [dcb], [lgwb])
        kb.op("dve", lambda e: e.tensor_scalar(out=lgw[:, 2:4], in0=lgw[:, 0:2], scalar1=-1.0 / 8, scalar2=1.0 / 7, op0=ALU.mult, op1=ALU.add), [lgwb], [lgwb])
        for cc in (6, 5, 4, 3, 2, 1):
            kb.op("dve", lambda e: e.tensor_tensor(out=lgw[:, 2:4], in0=lgw[:, 2:4], in1=lgw[:, 0:2], op=ALU.mult), [lgwb], [lgwb])
            kb.op("dve", lambda e, cc=cc: e.tensor_scalar(out=lgw[:, 2:4], in0=lgw[:, 2:4], scalar1=-1.0, scalar2=1.0 / cc, op0=ALU.mult, op1=ALU.add), [lgwb], [lgwb])
        kb.op("dve", lambda e: e.tensor_tensor(out=lgw[:, 2:4], in0=lgw[:, 2:4], in1=lgw[:, 0:2], op=ALU.mult), [lgwb], [lgwb])
        kb.op("dve", lambda e: e.tensor_scalar(out=lgw[:, 4:6], in0=lgw[:, 2:4], scalar1=-1.0, scalar2=None, op0=ALU.mult), [lgwb], [lgwb])
        lgf = lgw[:, 4:5]
        lgb = lgw[:, 5:6]
        kb.op("dve", lambda e: e.tensor_scalar(out=tmpm[:], in0=it[:, 0:128], scalar1=lgf, scalar2=None, op0=ALU.mult), [itb, lgwb], [tmpmb])
        kb.op("dve", lambda e: e.scalar_tensor_tensor(out=tmpm[:], in0=it[:, 128:256], scalar=lgb, in1=tmpm[:], op0=ALU.mult, op1=ALU.add), [itb, lgwb, tmpmb], [tmpmb])
        for r in range(4):
            kb.op("act", lambda e, r=r: e.activation(out=MT4[:, r, :], in_=tmpm[:], func=AF.Exp), [tmpmb], [MT4b])
            kb.op("act", lambda e, r=r: e.activation(out=qd4[:, 0, r, :], in_=it[:, 256:384], func=AF.Exp, scale=lgf), [itb, lgwb], [qd4b])
            kb.op("act", lambda e, r=r: e.activation(out=qd4[:, 1, r, :], in_=it[:, 384:512], func=AF.Exp, scale=lgb), [itb, lgwb], [qd4b])
        kb.op("act", lambda e: e.activation(out=sc[:, 0:1], in_=it[:, 512:513], func=AF.Exp, scale=lgf), [itb, lgwb], [scb])
        kb.op("act", lambda e: e.activation(out=sc[:, 1:2], in_=it[:, 513:514], func=AF.Exp, scale=lgb), [itb, lgwb], [scb])
        kb.op("act", lambda e: e.activation(out=sc[:, 2:4], in_=lgw[:, 4:6], func=AF.Exp, scale=128.0), [lgwb], [scb])
        ki = 0
        for d_, (SB_, SBb_) in enumerate(((SfB, SfBb), (SbB, SbBb))):
            kb.op("dve", lambda e, d_=d_: e.memset(Sst[:, d_, :], 0.0), [], [Sstb[d_]])
            order = range(NCH) if d_ == 0 else range(NCH - 1, -1, -1)
            pb = pbb = None
            for cnt, n in enumerate(order):
                if cnt % 4 == 0:
                    pb, pbb = cx.bank()
                ks, ksb = ksc[ki % 4], kscb[ki % 4]
                ki += 1
                kb.op("pool", lambda e, n=n, ks=ks, d_=d_: e.tensor_scalar(out=ks[:], in0=A3[:, n, :], scalar1=sc[:, d_:d_ + 1], scalar2=None, op0=ALU.mult),
                      [A3b, scb], [ksb])
                col = (cnt % 4) * 128
                kb.op("pe", lambda e, n=n, ks=ks, pb=pb, col=col: e.matmul(pb[:, col:col + 128], ks[:], A4[:, n, :], start=True, stop=True), [ksb, A4b], [pbb])
                kb.op("dve", lambda e, n=n, d_=d_, SB_=SB_: e.tensor_copy(out=SB_[:, n, :], in_=Sst[:, d_, :]), [Sstb[d_]], [SBb_])
                kb.op("dve", lambda e, d_=d_, pb=pb, col=col: e.scalar_tensor_tensor(out=Sst[:, d_, :], in0=Sst[:, d_, :], scalar=sc[:, 2 + d_:3 + d_],
                                                                                 in1=pb[:, col:col + 128], op0=ALU.mult, op1=ALU.add),
                      [Sstb[d_], scb, pbb], [Sstb[d_]])
        for g in range(NCH // 4):
            gs = slice(g * 512, (g + 1) * 512)
            pS, pSb = cx.bank()
            for r in range(4):
                n = g * 4 + r
                kb.op("pe", lambda e, n=n, r=r, pS=pS: e.matmul(pS[:, r * 128:(r + 1) * 128], A1[:, n * 128:(n + 1) * 128], A0[:, n * 128:(n + 1) * 128],
                                                           start=True, stop=True), [A0b, A1b], [pSb])
            pt, ptb = PT[g % 2], PTb[g % 2]
            kb.op("dve", lambda e, pS=pS, pt=pt: e.tensor_tensor(out=pt[:], in0=pS[:, :].rearrange("p (r i) -> p r i", r=4), in1=MT4[:], op=ALU.mult),
                  [pSb, MT4b], [ptb])
            qf, qfb_ = qfb[g % 2], qfbb[g % 2]
            for d_ in range(2):
                kb.op("pool", lambda e, d_=d_, qf=qf, gs=gs: e.tensor_tensor(out=qf[:, d_, :], in0=A0[:, gs], in1=qd4[:, d_].rearrange("p r i -> p (r i)"), op=ALU.mult),
                      [A0b, qd4b], [qfb_])
            pO, pOb = cx.bank()
            for r in range(4):
                n = g * 4 + r
                cs = slice(r * 128, (r + 1) * 128)
                kb.op("pe", lambda e, n=n, r=r, cs=cs, pO=pO, pt=pt: e.matmul(pO[:, cs], A4[:, n, :], pt[:, r, :], start=True, stop=False), [A4b, ptb], [pOb])
                kb.op("pe", lambda e, n=n, cs=cs, pO=pO, qf=qf: e.matmul(pO[:, cs], SfB[:, n, :], qf[:, 0, cs], start=False, stop=False), [SfBb, qfb_], [pOb])
                kb.op("pe", lambda e, n=n, cs=cs, pO=pO, qf=qf: e.matmul(pO[:, cs], SbB[:, n, :], qf[:, 1, cs], start=False, stop=True), [SbBb, qfb_], [pOb])
            emit_headnorm(cx, pO, pOb, rg_[:, 0:1], rgb, tmpo[:], tmpob, sq1, sq1b, rstd, rstdb, 512)
            os_, osb_ = ost[oi % 2], ostb[oi % 2]
            oi += 1
            kb.op("dve", lambda e, os_=os_, gs=gs: e.tensor_tensor(out=os_[:], in0=tmpo[:], in1=A2[:, gs], op=ALU.mult), [tmpob, A2b], [osb_])
            kb.dma(outT[u, 0, :, gs], os_[:], reads=[osb_], writes=[outb], eng="pool")
        kb.dma(A0[:], fm5[u, 3], writes=[A0b], eng="sync")
        kb.dma(A1[:], fm5[u, 4], writes=[A1b], eng="sync")
        kb.dma(A4[:], tm3[u, 2].rearrange("(n p) d -> p n d", p=128), writes=[A4b], eng="sync")
        kb.dma(nbias[:], nab[u], writes=[nbiasb], eng="sync")
        li = 0
        for pg in range(NCH // 4):
            gs = slice(pg * 512, (pg + 1) * 512)
            pO, pOb = cx.bank()
            pD, pDb = cx.bank()
            for r in range(4):
                p = pg * 4 + r
                kt0 = min(max(p - 2, 0), 59)
                var = {0: 0, 1: 1, 62: 3, 63: 4}.get(p, 2)
                pSa, pSab = cx.bank(exclude=(pOb, pDb))
                pSc, pScb = cx.bank(exclude=(pOb, pDb))
                for a in range(5):
                    tgt = pSa[:, a * 128:(a + 1) * 128] if a < 4 else pSc[:, 0:128]
                    tb_ = pSab if a < 4 else pScb
                    kb.op("pe", lambda e, tgt=tgt, a=a, kt0=kt0, p=p: e.matmul(tgt, A1[:, (kt0 + a) * 128:(kt0 + a + 1) * 128], A0[:, p * 128:(p + 1) * 128],
                                                                         start=True, stop=True), [A0b, A1b], [tb_])
                L, Lb_ = Lb[li % 2], Lbb[li % 2]
                E_, Eb_ = En[li % 2], Enb[li % 2]
                li += 1
                kb.op("dve", lambda e, L=L, pSa=pSa, var=var: e.tensor_tensor(out=L[:, 0:4, :], in0=pSa[:, :].rearrange("p (a q) -> p a q", a=4),
                                                                           in1=nbias[:, var, 0:4, :], op=ALU.add), [pSab, nbiasb], [Lb_])
                kb.op("dve", lambda e, L=L, pSc=pSc, var=var: e.tensor_tensor(out=L[:, 4, :], in0=pSc[:, 0:128], in1=nbias[:, var, 4, :], op=ALU.add),
                      [pScb, nbiasb], [Lb_])
                kb.op("act", lambda e, L=L, E_=E_: e.activation(out=E_[:], in_=L[:], func=AF.Exp), [Lb_], [Eb_])
                cs = slice(r * 128, (r + 1) * 128)
                for a in range(5):
                    kb.op("pe", lambda e, a=a, kt0=kt0, cs=cs, pO=pO, E_=E_: e.matmul(pO[:, cs], A4[:, kt0 + a, :], E_[:, a, :], start=(a == 0), stop=(a == 4)),
                          [A4b, Eb_], [pOb])
                for a in range(5):
                    kb.op("pe", lambda e, a=a, cs=cs, pD=pD, E_=E_: e.matmul(pD[:, cs], cx.ones[:], E_[:, a, :], start=(a == 0), stop=(a == 4)),
                          [cx.onesb, Eb_], [pDb])
            kb.op("dve", lambda e, pD=pD: e.reciprocal(out=rstd[:, :], in_=pD[:, :]), [pDb], [rstdb])
            os_, osb_ = ost[oi % 2], ostb[oi % 2]
            oi += 1
            kb.op("dve", lambda e, os_=os_, pO=pO: e.tensor_tensor(out=os_[:], in0=pO[:, :], in1=rstd[:, :], op=ALU.mult), [pOb, rstdb], [osb_])
            kb.dma(outT[u, 1, :, gs], os_[:], reads=[osb_], writes=[outb], eng="pool")
    return kb.emit([outb])


def retention_itab():
    j = np.arange(128, dtype=np.float32)[:, None]
    i = np.arange(128, dtype=np.float32)[None, :]
    A = np.maximum(i - j, 0.0)
    B = np.maximum(j - i, 0.0)
    r1 = np.broadcast_to(i + 1.0, (128, 128))
    r2 = np.broadcast_to(128.0 - i, (128, 128))
    return np.ascontiguousarray(np.concatenate([A, B, r1, r2, 127.0 - j, j], 1).astype(np.float32))


def na_bias_tables(rpb_h):
    out = np.full((5, 5, 128, 128), -30000.0, np.float32)
    kk = np.arange(128)
    for vi, p in enumerate((0, 1, 30, 62, 63)):
        kt0 = min(max(p - 2, 0), 59)
        rq = 2 * p + kk // 64
        cq = kk % 64
        rs = np.clip(rq - 4, 0, 120)
        cs_ = np.clip(cq - 8, 0, 48)
        for a in range(5):
            rk = 2 * (kt0 + a) + kk // 64
            ck = kk % 64
            okr = (rk[:, None] >= rs[None, :]) & (rk[:, None] < rs[None, :] + 8)
            okc = (ck[:, None] >= cs_[None, :]) & (ck[:, None] < cs_[None, :] + 16)
            dr = np.clip(rk[:, None] - rq[None, :] + 7, 0, 14)
            dcx = np.clip(ck[:, None] - cq[None, :] + 15, 0, 30)
            vals = rpb_h[dr, dcx]
            out[vi, a] = np.where(okr & okc, vals, np.float32(-30000.0))
    return np.ascontiguousarray(out.transpose(2, 0, 1, 3))


SC = float(128 ** -0.5)
EPI_EVEN = [("copy",)] * 8 + [("scale", SC)] * 8 + [("silu",)] * 8 + [("norm", 0, SC)] * 8 + [("norm", 1, 1.0)] * 8
TMS_EVEN = [SC, SC, 1.0, 1.0, 1.0, 1.0]
EPI_ODD = [("norm", 0, SC)] * 16 + [("norm", 1, 1.0)] * 16
TMS_ODD = [1.0] * 4


def _prog(key, fn):
    if key not in _CACHE:
        _CACHE[key] = fn()
    return _CACHE[key]


def _run(nc, in_maps):
    res = run_bass_kernel_spmd(nc, in_maps, core_ids=list(range(NCORES)))
    return res.results


def _lay_f1(W):
    g = W[:, :FH].reshape(KC, 128, FC, 128)
    u = W[:, FH:].reshape(KC, 128, FC, 128)
    return np.ascontiguousarray(np.stack([g, u], 0).transpose(3, 2, 1, 0, 4))


def _lay_tm(W):
    G = W.shape[1] // 512
    return np.ascontiguousarray(W.reshape(KC, 128, G, 512).transpose(2, 1, 0, 3))


def _bc(a, shape):
    return np.ascontiguousarray(np.broadcast_to(a, shape)).astype(np.float32)


def kernel(x, mem, norm_mix_g, norm_xattn_g, norm_mem_g, norm_ffn_g,
           w_in_ab, ret_decay_fwd, ret_decay_bwd, ret_out_g, na_q_g, na_k_g, na_rpb, w_out_ab,
           w_in_c, diff_q_g, diff_k_g, lambda_q1, lambda_k1, lambda_q2, lambda_k2, diff_out_g, w_out_c,
           w_xq, w_xkv, w_xo, xq_g, xk_g, w_ffn_in, w_ffn_out):
    f = lambda a: np.asarray(a, dtype=np.float32)
    x = f(x); mem = f(mem)
    DEPTH = 4
    lay = {}
    for i in range(DEPTH):
        j = i // 2
        if i % 2 == 0:
            W = f(w_in_ab[j])
            lay[("fm", i)] = lay_lhsT(np.concatenate([W[:, 0:1024], W[:, 1024:2048], W[:, 3072:4096], W[:, 4096:5120], W[:, 5120:6144]], 1))
            lay[("tm", i)] = _lay_tm(np.concatenate([W[:, 1024:2048], W[:, 2048:3072], W[:, 6144:7168]], 1))
            lay[("out", i)] = lay_lhsT(f(w_out_ab[j]))
        else:
            W = f(w_in_c[j])
            lay[("fm", i)] = lay_lhsT(W[:, 0:4096])
            lay[("tm", i)] = _lay_tm(W[:, 4096:6144])
            lay[("out", i)] = lay_lhsT(f(w_out_c[j]))
        lay[("xq", i)] = lay_lhsT(f(w_xq[i]))
        lay[("xk", i)] = lay_lhsT(f(w_xkv[i])[:, :512])
        lay[("xv", i)] = lay_rhs(f(w_xkv[i])[:, 512:])
        lay[("xo", i)] = lay_lhsT(f(w_xo[i]))
        lay[("f1", i)] = _lay_f1(f(w_ffn_in[i]))
        lay[("f2", i)] = lay_lhsT(f(w_ffn_out[i]))
    keys = list(lay.keys())
    sizes = [lay[k].size for k in keys]
    total = sum(sizes)
    unit = NCORES * 128 * 4096
    tot_pad = ((total + unit - 1) // unit) * unit
    flat = np.zeros(tot_pad, np.float32)
    off = 0
    for k, n in zip(keys, sizes):
        flat[off:off + n] = lay[k].ravel()
        off += n
    shards = flat.reshape(NCORES, 128, -1)
    outs = run_cast([shards[c] for c in range(NCORES)])
    flat_b = np.concatenate([np.asarray(o).reshape(-1) for o in outs])
    wb = {}
    off = 0
    for k, n in zip(keys, sizes):
        wb[k] = flat_b[off:off + n].reshape(lay[k].shape)
        off += n
    del flat, lay

    xs = x.reshape(NCORES, T, D)
    xT = [np.ascontiguousarray(xs[c].T.reshape(KC, 128, T)) for c in range(NCORES)]
    memT = [np.ascontiguousarray(mem[b].T.reshape(KC, 128, NM)) for b in range(2)]
    ident = np.eye(128, dtype=np.float32).astype(NPBF)
    itab = retention_itab()

    for i in range(DEPTH):
        j = i // 2
        even = (i % 2 == 0)
        if even:
            nc = _prog("P_even", lambda: build_P(EPI_EVEN, TMS_EVEN))
            gains = np.concatenate([lay_gain(f(norm_mix_g[i])), f(na_q_g[j])[:, None], f(na_k_g[j])[:, None]], 1)
        else:
            nc = _prog("P_odd", lambda: build_P(EPI_ODD, TMS_ODD))
            gains = np.concatenate([lay_gain(f(norm_mix_g[i])), f(diff_q_g[j])[:, None], f(diff_k_g[j])[:, None]], 1)
        gains = np.ascontiguousarray(gains.astype(np.float32))
        res = _run(nc, [{"xT": xT[c], "w_fm": wb[("fm", i)], "w_tm": wb[("tm", i)], "gains": gains} for c in range(NCORES)])
        fmT = [np.asarray(r["fmT"]) for r in res]
        tm = [np.asarray(r["tm"]) for r in res]
        mixT = [np.empty((KC, 128, T), NPBF) for _ in range(NCORES)]
        if even:
            nc = _prog("M_even", build_Meven)
            maps = []
            for c in range(NCORES):
                b = c // 4
                fm5 = np.empty((NU, 5, 128, S), NPBF)
                tm3 = np.empty((NU, 3, S, 128), NPBF)
                for u in range(NU):
                    h = (c % 4) * 2 + u
                    for q in range(4):
                        src = b * 4 + q
                        for t_ in range(5):
                            fm5[u, t_, :, q * T:(q + 1) * T] = fmT[src][t_ * 8 + h]
                        for t_ in range(3):
                            tm3[u, t_, q * T:(q + 1) * T, :] = tm[src][:, t_ * 1024 + h * 128:t_ * 1024 + (h + 1) * 128]
                hs = [(c % 4) * 2, (c % 4) * 2 + 1]
                dec = _bc(np.stack([f(ret_decay_fwd[j])[hs], f(ret_decay_bwd[j])[hs]], 1)[None], (128, NU, 2))
                nab = np.stack([na_bias_tables(f(na_rpb[j])[h]) for h in hs])
                maps.append({"fm5": fm5, "tm3": tm3, "dec": dec, "itab": itab, "retg": np.ascontiguousarray(f(ret_out_g[j])[:, None]), "nab": nab})
            res = _run(nc, maps)
            for c in range(NCORES):
                b, q = c // 4, c % 4
                for h in range(8):
                    o = np.asarray(res[b * 4 + h // 2]["mixT"])
                    mixT[c][h] = o[h % 2, 0, :, q * T:(q + 1) * T]
                    mixT[c][8 + h] = o[h % 2, 1, :, q * T:(q + 1) * T]
        else:
            nc = _prog("M_odd", build_Modd)
            lam_init = 0.8 - 0.6 * float(np.exp(-0.3 * i))
            lam4 = _bc(np.stack([f(lambda_q1[j]), f(lambda_k1[j]), f(lambda_q2[j]), f(lambda_k2[j])])[None], (128, 4, 128))
            gout = _bc(f(diff_out_g[j])[None], (128, 256))
            consts = _bc(np.array([lam_init, 1.0 - lam_init], np.float32)[None], (128, 2))
            maps = []
            for c in range(NCORES):
                b = c // 4
                qTa = np.empty((NU, 2, 128, S), NPBF)
                kTa = np.empty((NU, 2, 128, S), NPBF)
                va = np.empty((NU, S, 256), NPBF)
                ats, bts = [], []
                for u in range(NU):
                    h = u * 4 + (c % 4)
                    a_, b_ = alibi_tables(2.0 ** (-(h + 1)))
                    ats.append(a_); bts.append(b_)
                    for q in range(4):
                        src = b * 4 + q
                        for cc in range(2):
                            qTa[u, cc, :, q * T:(q + 1) * T] = fmT[src][h * 2 + cc]
                            kTa[u, cc, :, q * T:(q + 1) * T] = fmT[src][16 + h * 2 + cc]
                        va[u, q * T:(q + 1) * T, :] = tm[src][:, h * 256:(h + 1) * 256]
                maps.append({"qT": qTa, "kT": kTa, "v": va, "atab": np.stack(ats), "btile": np.stack(bts), "ident": ident,
                             "lam4": lam4, "gout": gout, "consts": consts})
            res = _run(nc, maps)
            for c in range(NCORES):
                b, q = c // 4, c % 4
                for h in range(8):
                    o = np.asarray(res[b * 4 + h % 4]["d_tm"])[h // 4, q * T:(q + 1) * T, :]
                    mixT[c][2 * h] = o[:, 0:128].T
                    mixT[c][2 * h + 1] = o[:, 128:256].T
        nc = _prog("C", build_C)
        gains = np.ascontiguousarray(np.concatenate([lay_gain(f(norm_xattn_g[i])), lay_gain(f(norm_ffn_g[i])), lay_gain(f(norm_mem_g[i])),
                                                     f(xq_g[i])[:, None], f(xk_g[i])[:, None]], 1).astype(np.float32))
        res = _run(nc, [{"xT": xT[c], "mixT": mixT[c], "memT": memT[c // 4], "w_out": wb[("out", i)], "w_xq": wb[("xq", i)],
                         "w_xk": wb[("xk", i)], "w_xv": wb[("xv", i)], "w_xo": wb[("xo", i)], "w_f1": wb[("f1", i)],
                         "w_f2": wb[("f2", i)], "gains": gains} for c in range(NCORES)])
        xT = [np.asarray(r["xTo"]) for r in res]
    out = np.stack([xT[c].reshape(D, T).T for c in range(NCORES)]).reshape(2, S, D)
    return np.ascontiguousarray(out.astype(np.float32))
```

```python
import numpy as np
import ml_dtypes
from contextlib import ExitStack
import concourse.bass as bass
import concourse.mybir as mybir
from concourse.bass_utils import run_bass_kernel_spmd

F32 = mybir.dt.float32
BF16 = mybir.dt.bfloat16
AF = mybir.ActivationFunctionType
ALU = mybir.AluOpType
NPBF = ml_dtypes.bfloat16

NCORES = 8
D = 2048
KC = 16
S = 8192
T = 2048
TB = 512
NTB = T // TB
FH = 5632
FC = 44
EPS = 1e-6
SAME_ENGINE_SYNC = True
FAST_RSQRT = True


class Buf:
    __slots__ = ("name", "writers", "readers", "war")

    def __init__(self, name):
        self.name = name
        self.writers = []
        self.readers = []
        self.war = []


class Op:
    __slots__ = ("eng", "idx", "fn", "deps", "dma", "semkey", "count", "signal")


class KB:
    ENGS = ("pe", "act", "dve", "pool", "sync")
    SEMID = 0

    def __init__(self, nc=None):
        self.fused = nc is not None
        self.nc = nc if nc is not None else bass.Bass("TRN2", target_bir_lowering=False)
        if self.fused:
            self.cm = self.nc.cleanup_on_exit()
            self.cm.__enter__()
        self.ops = {e: [] for e in self.ENGS}
        self.stack = ExitStack()
        self.dma_count = {}
        self.nbuf = 0

    def dram(self, name, shape, dtype, kind):
        return self.nc.dram_tensor(name, list(shape), dtype, kind=kind).ap()

    def sb(self, name, shape, dtype):
        return self.stack.enter_context(self.nc.sbuf_tensor(name, list(shape), dtype))

    def ps(self, name, shape, dtype=F32):
        return self.stack.enter_context(self.nc.psum_tensor(name, list(shape), dtype))

    def buf(self, name=None):
        self.nbuf += 1
        return Buf(name or f"b{self.nbuf}")

    def op(self, eng, fn, reads=(), writes=(), dma=False):
        o = Op()
        o.eng = eng
        o.fn = fn
        o.dma = dma
        o.signal = False
        o.count = None
        o.semkey = None
        deps = []
        for b in reads:
            deps.extend(b.writers)
        for b in writes:
            if b.readers:
                b.war = b.readers
                b.readers = []
                b.writers = []
            deps.extend(b.war)
        best = {}
        dmas = []
        for d in deps:
            if d.dma:
                dmas.append(d)
            else:
                if d.eng == eng and not dma and (eng == "pe" or not SAME_ENGINE_SYNC):
                    continue
                if d.eng not in best or d.idx > best[d.eng].idx:
                    best[d.eng] = d
        o.deps = list(best.values()) + list({id(d): d for d in dmas}.values())
        for d in o.deps:
            d.signal = True
        o.idx = len(self.ops[eng])
        self.ops[eng].append(o)
        if dma:
            key = writes[0]
            o.semkey = key
            self.dma_count[id(key)] = self.dma_count.get(id(key), 0) + 16
            o.count = self.dma_count[id(key)]
            o.signal = True
        for b in reads:
            b.readers.append(o)
        for b in writes:
            b.writers.append(o)
        return o

    def dma(self, out, in_, reads=(), writes=(), eng="sync"):
        return self.op(eng, lambda e: e.dma_start(out=out, in_=in_), reads, writes, dma=True)

    def emit(self, final_bufs):
        nc = self.nc
        st = self.stack
        def mksem(name):
            if self.fused:
                KB.SEMID += 1
                return nc.alloc_semaphore(name="%s_%d" % (name, KB.SEMID))
            return st.enter_context(nc.semaphore(name))
        eng_sem = {e: mksem("s_" + e) for e in ("pe", "act", "dve", "pool")}
        dma_sem = {}
        keys = {}
        for e in self.ENGS:
            for o in self.ops[e]:
                if o.dma and id(o.semkey) not in dma_sem:
                    dma_sem[id(o.semkey)] = mksem("d_%d" % len(dma_sem))
                    keys[id(o.semkey)] = o.semkey
        for e in ("pe", "act", "dve", "pool"):
            c = 0
            for o in self.ops[e]:
                if not o.dma and o.signal:
                    c += 1
                    o.count = c
        engobj = {"pe": None, "act": None, "dve": None, "pool": None, "sync": None}

        def run(e, eng):
            waited = {}
            for o in self.ops[e]:
                for d in o.deps:
                    if d.dma:
                        sem = dma_sem[id(d.semkey)]
                        k = ("d", id(d.semkey))
                    else:
                        sem = eng_sem[d.eng]
                        k = ("e", d.eng)
                    if waited.get(k, 0) >= d.count:
                        continue
                    waited[k] = d.count
                    eng.wait_ge(sem, d.count)
                ins = o.fn(eng)
                if o.dma:
                    ins.then_inc(dma_sem[id(o.semkey)], 16)
                elif o.signal:
                    ins.then_inc(eng_sem[e], 1)
            if e == "sync":
                for b in final_bufs:
                    eng.wait_ge(dma_sem[id(b)], self.dma_count[id(b)])

        with nc.Block() as block:
            @block.tensor
            def _(eng):
                run("pe", eng)

            @block.scalar
            def _(eng):
                run("act", eng)

            @block.vector
            def _(eng):
                run("dve", eng)

            @block.gpsimd
            def _(eng):
                run("pool", eng)

            @block.sync
            def _(eng):
                run("sync", eng)
        st.close()
        if self.fused:
            nc.all_engine_barrier()
            self.cm.__exit__(None, None, None)
        return nc


def build_cast(F):
    kb = KB()
    CH = 4096
    nch = F // CH
    assert F % CH == 0
    src = kb.dram("src", [128, F], F32, "ExternalInput")
    dst = kb.dram("dst", [128, F], BF16, "ExternalOutput")
    a = [kb.sb("a%d" % i, [128, CH], F32) for i in range(3)]
    b = [kb.sb("b%d" % i, [128, CH], BF16) for i in range(3)]
    ab = [kb.buf() for _ in range(3)]
    bb = [kb.buf() for _ in range(3)]
    ob = kb.buf("out")
    for i in range(nch):
        s = i % 3
        kb.dma(a[s][:], src[:, i * CH:(i + 1) * CH], writes=[ab[s]], eng="sync")
        e = ("dve", "act", "pool")[i % 3]
        if e == "act":
            kb.op("act", lambda eng, s=s: eng.copy(out=b[s][:], in_=a[s][:]), [ab[s]], [bb[s]])
        else:
            kb.op(e, lambda eng, s=s: eng.tensor_copy(out=b[s][:], in_=a[s][:]), [ab[s]], [bb[s]])
        kb.dma(dst[:, i * CH:(i + 1) * CH], b[s][:], reads=[bb[s]], writes=[ob], eng="sync")
    return kb.emit([ob])


_CACHE = {}


def run_cast(flat_list):
    F = flat_list[0].shape[1]
    key = ("cast", F)
    if key not in _CACHE:
        _CACHE[key] = build_cast(F)
    res = run_bass_kernel_spmd(_CACHE[key], [{"src": f} for f in flat_list], core_ids=list(range(NCORES)))
    return [r["dst"] for r in res.results]


class Ctx:
    def __init__(self, kb, npsum=6):
        self.kb = kb
        self.pbank = [kb.ps("pg%d" % i, [128, 512]) for i in range(npsum)]
        self.pbuf = [kb.buf("pg%d" % i) for i in range(npsum)]
        self.pi = 0
        self.pst = kb.ps("pstat", [128, 512])
        self.pstb = kb.buf("pstat")
        self.ones = kb.sb("ones_bf", [128, 128], BF16)
        self.onesb = kb.buf("ones")
        self.epst = kb.sb("eps_t", [128, 1], F32)
        self.epsb = kb.buf("eps")
        kb.op("pool", lambda e: e.memset(self.ones[:], 1.0), [], [self.onesb])
        kb.op("pool", lambda e: e.memset(self.epst[:], EPS), [], [self.epsb])

    def bank(self, exclude=()):
        while True:
            i = self.pi
            self.pi = (self.pi + 1) % len(self.pbank)
            if not any(self.pbuf[i] is x for x in exclude):
                return self.pbank[i], self.pbuf[i]


def emit_rstd(cx, sq_aps, sq_bufs, nfree, dim, rstd, rstdb):
    kb = cx.kb
    n = len(sq_aps)
    for i, a in enumerate(sq_aps):
        kb.op("pe", lambda e, a=a, i=i: e.matmul(cx.pst[:, 0:nfree], cx.ones[:], a, start=(i == 0), stop=(i == n - 1)),
              [cx.onesb] + list(sq_bufs), [cx.pstb])
    if FAST_RSQRT:
        kb.op("act", lambda e: e.activation(out=rstd, in_=cx.pst[:, 0:nfree], func=AF.Ln, bias=cx.epst[:, 0:1], scale=1.0 / dim),
              [cx.pstb, cx.epsb], [rstdb])
        kb.op("act", lambda e: e.activation(out=rstd, in_=rstd, func=AF.Exp, scale=-0.5), [rstdb], [rstdb])
    else:
        kb.op("act", lambda e: e.activation(out=rstd, in_=cx.pst[:, 0:nfree], func=AF.Sqrt, bias=cx.epst[:, 0:1], scale=1.0 / dim),
              [cx.pstb, cx.epsb], [rstdb])
        kb.op("dve", lambda e: e.reciprocal(out=rstd, in_=rstd), [rstdb], [rstdb])


def emit_norm_block(cx, xb, xbb, g, gb, hT, hTb, sq, sqb, rstd, rstdb, nfree=TB, nk=KC):
    kb = cx.kb
    xl = list(xbb) if isinstance(xbb, (list, tuple)) else [xbb] * nk
    kb.op("act", lambda e: e.activation(out=sq[:, 0:nk, 0:nfree], in_=xb[:, 0:nk, 0:nfree], func=AF.Square), list(set(xl)), [sqb])
    emit_rstd(cx, [sq[:, k, 0:nfree] for k in range(nk)], [sqb], nfree, nk * 128, rstd[:, 0:nfree], rstdb)
    for k in range(nk):
        kb.op("dve", lambda e, k=k: e.scalar_tensor_tensor(out=hT[:, k, 0:nfree], in0=xb[:, k, 0:nfree], scalar=g[:, k:k + 1],
                                                            in1=rstd[:, 0:nfree], op0=ALU.mult, op1=ALU.mult),
              [xl[k], gb, rstdb], [hTb])


def emit_headnorm(cx, pbank, pbuf, gcol, gb, out_ap, outb, sq1, sq1b, rstd, rstdb, nfree):
    kb = cx.kb
    kb.op("act", lambda e: e.activation(out=sq1[:, 0:nfree], in_=pbank[:, 0:nfree], func=AF.Square), [pbuf], [sq1b])
    emit_rstd(cx, [sq1[:, 0:nfree]], [sq1b], nfree, 128, rstd[:, 0:nfree], rstdb)
    kb.op("dve", lambda e: e.scalar_tensor_tensor(out=out_ap, in0=pbank[:, 0:nfree], scalar=gcol, in1=rstd[:, 0:nfree],
                                                  op0=ALU.mult, op1=ALU.mult), [pbuf, gb, rstdb], [outb])


class WPool:
    def __init__(self, kb, name, shape, n):
        self.kb = kb
        self.t = [kb.sb("%s%d" % (name, i), shape, BF16) for i in range(n)]
        self.b = [kb.buf("%s%d" % (name, i)) for i in range(n)]
        self.i = 0

    def load(self, src, eng="sync"):
        i = self.i
        self.i = (self.i + 1) % len(self.t)
        self.kb.dma(self.t[i][:], src, writes=[self.b[i]], eng=eng)
        return self.t[i], self.b[i]


def emit_linear_fm(cx, w, wb, nk, rhs_fn, rhs_bufs, nfree=TB):
    kb = cx.kb
    pb, pbb = cx.bank()
    for k in range(nk):
        kb.op("pe", lambda e, k=k: e.matmul(pb[:, 0:nfree], w[:, k, :], rhs_fn(k), start=(k == 0), stop=(k == nk - 1)),
              [wb] + list(rhs_bufs), [pbb])
    return pb, pbb


NM = 256
XH = 4


def build_C():
    kb = KB()
    xT = kb.dram("xT", [KC, 128, T], F32, "ExternalInput")
    mixT = kb.dram("mixT", [KC, 128, T], BF16, "ExternalInput")
    memT = kb.dram("memT", [KC, 128, NM], F32, "ExternalInput")
    w_out = kb.dram("w_out", [KC, 128, KC, 128], BF16, "ExternalInput")
    w_xq = kb.dram("w_xq", [XH, 128, KC, 128], BF16, "ExternalInput")
    w_xk = kb.dram("w_xk", [XH, 128, KC, 128], BF16, "ExternalInput")
    w_xv = kb.dram("w_xv", [128, KC, 512], BF16, "ExternalInput")
    w_xo = kb.dram("w_xo", [KC, 128, XH, 128], BF16, "ExternalInput")
    w_f1 = kb.dram("w_f1", [FC, 128, KC, 2, 128], BF16, "ExternalInput")
    w_f2 = kb.dram("w_f2", [KC, 128, FC, 128], BF16, "ExternalInput")
    gains = kb.dram("gains", [128, 3 * KC + 2], F32, "ExternalInput")
    xo = kb.dram("xTo", [KC, 128, T], F32, "ExternalOutput")

    cx = Ctx(kb)
    xb = kb.sb("xb", [128, KC, TB], F32); xbb = kb.buf("xb")
    mb = kb.sb("mb", [128, KC, TB], BF16); mbb = kb.buf("mb")
    hT = kb.sb("hT", [128, KC, TB], BF16); hTb = kb.buf("hT")
    aT = kb.sb("aT", [128, FC, TB], BF16); aTb = kb.buf("aT")
    rstd = kb.sb("rstd", [128, TB], F32); rstdb = kb.buf("rstd")
    sq1 = kb.sb("sq1", [128, TB], BF16); sq1b = kb.buf("sq1")
    gt = kb.sb("gt", [128, 3 * KC + 2], F32); gtb = kb.buf("gt")
    gq = kb.sb("gq", [128, 1], F32); gqb = kb.buf("gq")
    kx = kb.sb("kx", [128, XH, NM], BF16); kxb = kb.buf("kx")
    vx = kb.sb("vx", [128, 2, XH * 128], BF16); vxb = kb.buf("vx")
    qx = kb.sb("qx", [128, XH, TB], BF16); qxb = kb.buf("qx")
    E = kb.sb("E", [128, 2, TB], BF16); Eb = kb.buf("E")
    oT = kb.sb("oT", [128, XH, TB], BF16); oTb = kb.buf("oT")
    sg = [kb.sb("sg%d" % i, [128, TB], F32) for i in range(2)]; sgb = [kb.buf() for _ in range(2)]
    wA = WPool(kb, "wA", [128, KC, 128], 2)
    wO = WPool(kb, "wO", [128, XH, 128], 2)
    wF1 = WPool(kb, "wF1", [128, KC, 2, 128], 2)
    wF2 = WPool(kb, "wF2", [128, FC, 128], 2)
    outb = kb.buf("out")

    kb.dma(gt[:], gains[:, :], writes=[gtb], eng="pool")
    kb.op("dve", lambda e: e.tensor_scalar(out=gq[:], in0=gt[:, 3 * KC:3 * KC + 1], scalar1=float(128 ** -0.5), scalar2=None, op0=ALU.mult),
          [gtb], [gqb])

    mf = xb
    for k in range(KC):
        pass
    kb.dma(xb[:, :, 0:NM], memT.rearrange("k p t -> p k t"), writes=[xbb], eng="sync")
    emit_norm_block(cx, xb, xbb, gt[:, 2 * KC:3 * KC], gtb, hT, hTb, aT, aTb, rstd, rstdb, nfree=NM)
    for h in range(XH):
        w, wb = wA.load(w_xk[h])
        pb, pbb = emit_linear_fm(cx, w, wb, KC, lambda k: hT[:, k, 0:NM], [hTb], nfree=NM)
        emit_headnorm(cx, pb, pbb, gt[:, 3 * KC + 1:3 * KC + 2], gtb, kx[:, h, :], kxb, sq1, sq1b, rstd, rstdb, NM)
    wv = aT[:, 16:32, :]
    kb.dma(wv, w_xv[:, :, :], reads=[], writes=[aTb], eng="sync")
    for mt in range(2):
        pb, pbb = cx.bank()
        for k in range(KC):
            kb.op("pe", lambda e, k=k, mt=mt, pb=pb: e.matmul(pb[:, :], hT[:, k, mt * 128:(mt + 1) * 128], aT[:, 16 + k, :],
                                                         start=(k == 0), stop=(k == KC - 1)), [hTb, aTb], [pbb])
        kb.op("act", lambda e, mt=mt, pb=pb: e.copy(out=vx[:, mt, :], in_=pb[:, :]), [pbb], [vxb])

    for tb in range(NTB):
        ts = slice(tb * TB, (tb + 1) * TB)
        kb.dma(xb[:], xT[:, :, ts].rearrange("k p t -> p k t"), writes=[xbb], eng="sync")
        kb.dma(mb[:], mixT[:, :, ts].rearrange("k p t -> p k t"), writes=[mbb], eng="pool")
        for i in range(KC):
            w, wb = wA.load(w_out[i])
            pb, pbb = emit_linear_fm(cx, w, wb, KC, lambda k: mb[:, k, :], [mbb])
            kb.op("dve", lambda e, i=i, pb=pb: e.tensor_tensor(out=xb[:, i, :], in0=xb[:, i, :], in1=pb[:, :], op=ALU.add), [pbb, xbb], [xbb])
        emit_norm_block(cx, xb, xbb, gt[:, 0:KC], gtb, hT, hTb, aT, aTb, rstd, rstdb)
        for h in range(XH):
            w, wb = wA.load(w_xq[h])
            pb, pbb = emit_linear_fm(cx, w, wb, KC, lambda k: hT[:, k, :], [hTb])
            emit_headnorm(cx, pb, pbb, gq[:, 0:1], gqb, qx[:, h, :], qxb, sq1, sq1b, rstd, rstdb, TB)
        for h in range(XH):
            for mt in range(2):
                pb, pbb = cx.bank()
                kb.op("pe", lambda e, h=h, mt=mt, pb=pb: e.matmul(pb[:, :], kx[:, h, mt * 128:(mt + 1) * 128], qx[:, h, :], start=True, stop=True),
                      [kxb, qxb], [pbb])
                kb.op("act", lambda e, mt=mt, pb=pb: e.activation(out=E[:, mt, :], in_=pb[:, :], func=AF.Exp), [pbb], [Eb])
            po, pob = cx.bank()
            pd, pdb = cx.bank()
            for mt in range(2):
                kb.op("pe", lambda e, h=h, mt=mt, po=po: e.matmul(po[:, :], vx[:, mt, h * 128:(h + 1) * 128], E[:, mt, :], start=(mt == 0), stop=(mt == 1)),
                      [vxb, Eb], [pob])
            for mt in range(2):
                kb.op("pe", lambda e, mt=mt, pd=pd: e.matmul(pd[:, :], cx.ones[:], E[:, mt, :], start=(mt == 0), stop=(mt == 1)),
                      [cx.onesb, Eb], [pdb])
            kb.op("dve", lambda e, pd=pd: e.reciprocal(out=rstd[:, :], in_=pd[:, :]), [pdb], [rstdb])
            kb.op("dve", lambda e, h=h, po=po: e.tensor_tensor(out=oT[:, h, :], in0=po[:, :], in1=rstd[:, :], op=ALU.mult), [pob, rstdb], [oTb])
        for i in range(KC):
            w, wb = wO.load(w_xo[i])
            pb, pbb = emit_linear_fm(cx, w, wb, XH, lambda k: oT[:, k, :], [oTb])
            kb.op("dve", lambda e, i=i, pb=pb: e.tensor_tensor(out=xb[:, i, :], in0=xb[:, i, :], in1=pb[:, :], op=ALU.add), [pbb, xbb], [xbb])
        emit_norm_block(cx, xb, xbb, gt[:, KC:2 * KC], gtb, hT, hTb, aT, aTb, rstd, rstdb)
        for j in range(FC):
            w, wb = wF1.load(w_f1[j])
            pg, pgb = cx.bank()
            pu, pub = cx.bank()
            for k in range(KC):
                kb.op("pe", lambda e, k=k, w=w, pg=pg: e.matmul(pg[:, :], w[:, k, 0, :], hT[:, k, :], start=(k == 0), stop=(k == KC - 1)), [wb, hTb], [pgb])
            for k in range(KC):
                kb.op("pe", lambda e, k=k, w=w, pu=pu: e.matmul(pu[:, :], w[:, k, 1, :], hT[:, k, :], start=(k == 0), stop=(k == KC - 1)), [wb, hTb], [pub])
            s = j % 2
            kb.op("act", lambda e, s=s, pg=pg: e.activation(out=sg[s][:], in_=pg[:, :], func=AF.Silu), [pgb], [sgb[s]])
            kb.op("dve", lambda e, s=s, j=j, pu=pu: e.tensor_tensor(out=aT[:, j, :], in0=sg[s][:], in1=pu[:, :], op=ALU.mult), [sgb[s], pub], [aTb])
        for i in range(KC):
            w, wb = wF2.load(w_f2[i])
            pb, pbb = emit_linear_fm(cx, w, wb, FC, lambda k: aT[:, k, :], [aTb])
            kb.op("dve", lambda e, i=i, pb=pb: e.tensor_tensor(out=xb[:, i, :], in0=xb[:, i, :], in1=pb[:, :], op=ALU.add), [pbb, xbb], [xbb])
        kb.dma(xo[:, :, ts].rearrange("k p t -> p k t"), xb[:], reads=[xbb], writes=[outb], eng="sync")
    return kb.emit([outb])


def lay_lhsT(W, mchunk=128):
    Kd, M = W.shape
    return np.ascontiguousarray(W.reshape(Kd // 128, 128, M // 128, 128).transpose(2, 1, 0, 3))


def lay_rhs(W):
    Kd, N = W.shape
    return np.ascontiguousarray(W.reshape(Kd // 128, 128, N).transpose(1, 0, 2))


def lay_gain(g):
    return np.ascontiguousarray(g.reshape(-1, 128).T)


def build_P(fm_epi, tm_scales):
    NF = len(fm_epi)
    NG = len(tm_scales)
    kb = KB()
    xT = kb.dram("xT", [KC, 128, T], F32, "ExternalInput")
    w_fm = kb.dram("w_fm", [NF, 128, KC, 128], BF16, "ExternalInput")
    w_tm = kb.dram("w_tm", [NG, 128, KC, 512], BF16, "ExternalInput")
    gains = kb.dram("gains", [128, KC + 2], F32, "ExternalInput")
    fmo = kb.dram("fmT", [NF, 128, T], BF16, "ExternalOutput")
    tmo = kb.dram("tm", [T, NG * 512], BF16, "ExternalOutput")

    cx = Ctx(kb)
    xb = kb.sb("xb", [128, KC, TB], F32); xbb = kb.buf("xb")
    hT = kb.sb("hT", [128, KC, TB], BF16); hTb = kb.buf("hT")
    sq = kb.sb("sq", [128, KC, TB], BF16); sqb = kb.buf("sq")
    rstd = kb.sb("rstd", [128, TB], F32); rstdb = kb.buf("rstd")
    sq1 = kb.sb("sq1", [128, TB], BF16); sq1b = kb.buf("sq1")
    gt = kb.sb("gt", [128, KC + 2], F32); gtb = kb.buf("gt")
    gs = kb.sb("gs", [128, 2], F32); gsb = kb.buf("gs")
    wF = WPool(kb, "wF", [128, KC, 128], 3)
    wT = WPool(kb, "wT", [128, KC, 512], 2)
    GRP = 8
    of = [kb.sb("of%d" % i, [128, GRP, TB], BF16) for i in range(2)]; ofb = [kb.buf() for _ in range(2)]
    ot = [kb.sb("ot%d" % i, [128, 4, 512], BF16) for i in range(2)]; otb = [kb.buf() for _ in range(2)]
    outb = kb.buf("out")
    outb2 = kb.buf("out2")

    kb.dma(gt[:], gains[:, :], writes=[gtb], eng="pool")
    norm_c = {}
    for ep in fm_epi:
        if ep[0] == "norm":
            norm_c[ep[1]] = ep[2]
    for col, c in norm_c.items():
        kb.op("dve", lambda e, col=col, c=c: e.tensor_scalar(out=gs[:, col:col + 1], in0=gt[:, KC + col:KC + col + 1], scalar1=float(c),
                                                              scalar2=None, op0=ALU.mult), [gtb], [gsb])
    for tb in range(NTB):
        ts = slice(tb * TB, (tb + 1) * TB)
        kb.dma(xb[:], xT[:, :, ts].rearrange("k p t -> p k t"), writes=[xbb], eng="sync")
        emit_norm_block(cx, xb, xbb, gt[:, 0:KC], gtb, hT, hTb, sq, sqb, rstd, rstdb)
        for j in range(NF):
            gi = (j // GRP) % 2
            w, wb = wF.load(w_fm[j])
            pb, pbb = emit_linear_fm(cx, w, wb, KC, lambda k: hT[:, k, :], [hTb])
            dst = of[gi][:, j % GRP, :]
            ep = fm_epi[j]
            if ep[0] == "copy":
                kb.op("act", lambda e, dst=dst, pb=pb: e.copy(out=dst, in_=pb[:, :]), [pbb], [ofb[gi]])
            elif ep[0] == "scale":
                kb.op("act", lambda e, dst=dst, pb=pb, c=ep[1]: e.mul(out=dst, in_=pb[:, :], mul=float(c)), [pbb], [ofb[gi]])
            elif ep[0] == "silu":
                kb.op("act", lambda e, dst=dst, pb=pb: e.activation(out=dst, in_=pb[:, :], func=AF.Silu), [pbb], [ofb[gi]])
            else:
                emit_headnorm(cx, pb, pbb, gs[:, ep[1]:ep[1] + 1], gsb, dst, ofb[gi], sq1, sq1b, rstd, rstdb, TB)
            if j % GRP == GRP - 1 or j == NF - 1:
                j0 = (j // GRP) * GRP
                n = j - j0 + 1
                kb.dma(fmo[j0:j0 + n, :, ts].rearrange("k p t -> p k t"), of[gi][:, 0:n, :], reads=[ofb[gi]], writes=[outb], eng="pool")
        for g in range(NG):
            w, wb = wT.load(w_tm[g])
            oi = g % 2
            for tt in range(4):
                pb, pbb = cx.bank()
                for k in range(KC):
                    kb.op("pe", lambda e, k=k, tt=tt, pb=pb, w=w: e.matmul(pb[:, :], hT[:, k, tt * 128:(tt + 1) * 128], w[:, k, :],
                                                                      start=(k == 0), stop=(k == KC - 1)), [hTb, wb], [pbb])
                kb.op("act", lambda e, tt=tt, pb=pb, oi=oi, c=tm_scales[g]: e.mul(out=ot[oi][:, tt, :], in_=pb[:, :], mul=float(c)), [pbb], [otb[oi]])
            kb.dma(tmo[tb * TB:(tb + 1) * TB, g * 512:(g + 1) * 512].rearrange("(a p) c -> p a c", p=128), ot[oi][:], reads=[otb[oi]],
                   writes=[outb2], eng="pool")
    return kb.emit([outb, outb2])


AX = mybir.AxisListType
NU = 2
QB = 512
NQB = S // QB
NKT = S // 128
MODD_WIN = (9, None)


def build_Modd():
    kb = KB()
    qT = kb.dram("qT", [NU, 2, 128, S], BF16, "ExternalInput")
    kT = kb.dram("kT", [NU, 2, 128, S], BF16, "ExternalInput")
    v = kb.dram("v", [NU, S, 256], BF16, "ExternalInput")
    atab = kb.dram("atab", [NU, 128, 140], F32, "ExternalInput")
    btile = kb.dram("btile", [NU, 128, 3, 128], BF16, "ExternalInput")
    ident_d = kb.dram("ident", [128, 128], BF16, "ExternalInput")
    lam4 = kb.dram("lam4", [128, 4, 128], F32, "ExternalInput")
    gout = kb.dram("gout", [128, 256], F32, "ExternalInput")
    consts = kb.dram("consts", [128, 2], F32, "ExternalInput")
    dout = kb.dram("d_tm", [NU, S, 256], BF16, "ExternalOutput")

    sps = [kb.ps("sps%d" % i, [128, 512]) for i in range(3)]; spsb = [kb.buf() for _ in range(3)]
    acc = [kb.ps("acc%d" % i, [128, 512]) for i in range(4)]; accb = [kb.buf() for _ in range(4)]
    kTs = kb.sb("kTs", [128, 2, S], BF16); kTb = kb.buf("kTs")
    Vx = kb.sb("Vx", [128, NKT, 257], BF16); Vxb = kb.buf("Vx")
    qbl = [kb.sb("qbl%d" % i, [128, 2, QB], BF16) for i in range(2)]; qblb = [kb.buf() for _ in range(2)]
    Es = [kb.sb("E%d" % i, [128, QB], BF16) for i in range(3)]; Esb = [kb.buf() for _ in range(3)]
    O = [kb.sb("O%d" % i, [128, 4, 257], F32) for i in range(2)]; Ob = [kb.buf() for _ in range(2)]
    at = kb.sb("at", [128, 140], F32); atb = kb.buf("at")
    bt = kb.sb("bt", [128, 3, 128], BF16); btb = kb.buf("bt")
    ident = kb.sb("ident_sb", [128, 128], BF16); identb = kb.buf("ident")
    l4 = kb.sb("l4", [128, 4, 128], F32); l4b = kb.buf("l4")
    gfin = kb.sb("gfin", [128, 256], F32); gfinb = kb.buf("gfin")
    cst = kb.sb("cst", [128, 2], F32); cstb = kb.buf("cst")
    epst = kb.sb("epst", [128, 1], F32); epsb = kb.buf("eps")
    sm = kb.sb("sm", [128, 8], F32); smb = kb.buf("sm")
    y = kb.sb("y", [128, 256], F32); yb = kb.buf("y")
    y2 = kb.sb("y2", [128, 256], F32); y2b = kb.buf("y2")
    rr = kb.sb("rr", [128, 4], F32); rrb = kb.buf("rr")
    ost = [kb.sb("ost%d" % i, [128, 4, 256], BF16) for i in range(2)]; ostb = [kb.buf() for _ in range(2)]
    outb = kb.buf("out")

    kb.dma(ident[:], ident_d[:, :], writes=[identb], eng="pool")
    kb.dma(l4[:], lam4[:, :, :], writes=[l4b], eng="pool")
    kb.dma(gfin[:], gout[:, :], writes=[gfinb], eng="pool")
    kb.dma(cst[:], consts[:, :], writes=[cstb], eng="pool")
    kb.op("pool", lambda e: e.memset(epst[:], EPS), [], [epsb])
    kb.op("pool", lambda e: e.memset(Vx[:, :, 256:257], 1.0), [], [Vxb])
    kb.op("dve", lambda e: e.tensor_tensor(out=y[:, 0:128], in0=l4[:, 0, :], in1=l4[:, 1, :], op=ALU.mult), [l4b], [yb])
    kb.op("dve", lambda e: e.tensor_reduce(out=sm[:, 0:1], in_=y[:, 0:128], axis=AX.X, op=ALU.add), [yb], [smb])
    kb.op("dve", lambda e: e.tensor_tensor(out=y[:, 0:128], in0=l4[:, 2, :], in1=l4[:, 3, :], op=ALU.mult), [l4b, smb], [yb])
    kb.op("dve", lambda e: e.tensor_reduce(out=sm[:, 1:2], in_=y[:, 0:128], axis=AX.X, op=ALU.add), [yb], [smb])
    kb.op("act", lambda e: e.activation(out=sm[:, 4:6], in_=sm[:, 0:2], func=AF.Exp), [smb], [smb])
    kb.op("dve", lambda e: e.tensor_tensor(out=sm[:, 2:3], in0=sm[:, 4:5], in1=sm[:, 5:6], op=ALU.subtract), [smb], [smb])
    kb.op("dve", lambda e: e.tensor_tensor(out=sm[:, 2:3], in0=sm[:, 2:3], in1=cst[:, 0:1], op=ALU.add), [smb, cstb], [smb])
    kb.op("dve", lambda e: e.tensor_scalar(out=sm[:, 3:4], in0=sm[:, 2:3], scalar1=-1.0, scalar2=None, op0=ALU.mult), [smb], [smb])
    kb.op("dve", lambda e: e.tensor_scalar(out=gfin[:], in0=gfin[:], scalar1=cst[:, 1:2], scalar2=None, op0=ALU.mult), [gfinb, cstb], [gfinb])

    LOOK = 2
    for u in range(NU):
        kb.dma(kTs[:, 0, :], kT[u, 0], writes=[kTb], eng="sync")
        kb.dma(kTs[:, 1, :], kT[u, 1], writes=[kTb], eng="sync")
        for h4 in range(4):
            kb.dma(Vx[:, h4 * 16:(h4 + 1) * 16, 0:256], v[u, h4 * 2048:(h4 + 1) * 2048, :].rearrange("(t p) e -> p t e", p=128),
                   writes=[Vxb], eng="sync")
        kb.dma(at[:], atab[u], writes=[atb], eng="pool")
        kb.dma(bt[:], btile[u], writes=[btb], eng="pool")
        W = MODD_WIN[u]
        def krange(qb):
            if W is None:
                return 0, NKT
            return max(0, qb * 4 - W), min(NKT, qb * 4 + 4 + W)
        iters = [(qb, c, kt) for qb in range(NQB) for c in range(2) for kt in range(*krange(qb))]
        N = len(iters)

        def s_part(idx):
            qb, c, kt = iters[idx]
            qs = qb % 2
            if c == 0 and kt == krange(qb)[0]:
                kb.dma(qbl[qs][:], qT[u, :, :, qb * QB:(qb + 1) * QB].rearrange("c p t -> p c t"), writes=[qblb[qs]], eng="sync")
            kq = kt // 4
            sp, spb = sps[idx % 3], spsb[idx % 3]
            E, Eb = Es[idx % 3], Esb[idx % 3]
            diag = (kq == qb)
            kb.op("pe", lambda e: e.matmul(sp[:, :], kTs[:, c, kt * 128:(kt + 1) * 128], qbl[qs][:, c, :], start=True, stop=(not diag)),
                  [kTb, qblb[qs]], [spb])
            if diag:
                sbp = kt % 4
                for sb in range(4):
                    dl = sb - sbp
                    ti = 2 if dl == 0 else (0 if dl > 0 else 1)
                    kb.op("pe", lambda e, sb=sb, ti=ti: e.matmul(sp[:, sb * 128:(sb + 1) * 128], ident[:], bt[:, ti, :], start=False, stop=True,
                                                                  skip_group_check=True), [identb, btb], [spb])
                for sb in range(4):
                    dl = abs(sb - sbp)
                    kb.op("act", lambda e, sb=sb, dl=dl: e.activation(out=E[:, sb * 128:(sb + 1) * 128], in_=sp[:, sb * 128:(sb + 1) * 128],
                                                                      func=AF.Exp, bias=at[:, 136 + dl:137 + dl], scale=1.0), [spb, atb], [Eb])
            else:
                col = (qb * 4 - kt) if kq < qb else (64 + kt - qb * 4)
                kb.op("act", lambda e: e.activation(out=E[:, :], in_=sp[:, :], func=AF.Exp, bias=at[:, col:col + 1], scale=1.0), [spb, atb], [Eb])

        def pv_part(idx):
            qb, c, kt = iters[idx]
            kq = kt // 4
            E, Eb = Es[idx % 3], Esb[idx % 3]
            diag = (kq == qb)
            lo, hi = krange(qb)
            if kq < qb:
                first, last, ph = (kt == lo), (kt == qb * 4 - 1), 0
            elif diag:
                first, last, ph = (kt == qb * 4), (kt == qb * 4 + 3), 1
            else:
                first, last, ph = (kt == qb * 4 + 4), (kt == hi - 1), 2
            for sb in range(4):
                kb.op("pe", lambda e, sb=sb: e.matmul(acc[sb][:, 0:257], E[:, sb * 128:(sb + 1) * 128], Vx[:, kt, :], start=first, stop=last),
                      [Eb, Vxb], [accb[sb]])
            if last:
                for sb in range(4):
                    if ph == 0:
                        kb.op("dve", lambda e, sb=sb: e.tensor_scalar(out=O[c][:, sb, :], in0=acc[sb][:, 0:257], scalar1=at[:, 128 + sb:129 + sb],
                                                                     scalar2=None, op0=ALU.mult), [accb[sb], atb], [Ob[c]])
                    elif ph == 1:
                        if lo >= qb * 4:
                            kb.op("dve", lambda e, sb=sb: e.tensor_copy(out=O[c][:, sb, :], in_=acc[sb][:, 0:257]), [accb[sb]], [Ob[c]])
                        else:
                            kb.op("dve", lambda e, sb=sb: e.tensor_tensor(out=O[c][:, sb, :], in0=O[c][:, sb, :], in1=acc[sb][:, 0:257], op=ALU.add),
                                  [accb[sb], Ob[c]], [Ob[c]])
                    else:
                        kb.op("dve", lambda e, sb=sb: e.scalar_tensor_tensor(out=O[c][:, sb, :], in0=acc[sb][:, 0:257], scalar=at[:, 132 + sb:133 + sb],
                                                                            in1=O[c][:, sb, :], op0=ALU.mult, op1=ALU.add), [accb[sb], atb, Ob[c]], [Ob[c]])
            final = (c == 1) and (kt == hi - 1)
            if final:
                combine(qb)

        def combine(qb):
            os_ = qb % 2
            for sb in range(4):
                kb.op("dve", lambda e, sb=sb: e.reciprocal(out=rr[:, 0:1], in_=O[0][:, sb, 256:257]), [Ob[0], rrb], [rrb])
                kb.op("dve", lambda e, sb=sb: e.reciprocal(out=rr[:, 1:2], in_=O[1][:, sb, 256:257]), [Ob[1], rrb], [rrb])
                kb.op("dve", lambda e: e.tensor_tensor(out=rr[:, 1:2], in0=rr[:, 1:2], in1=sm[:, 3:4], op=ALU.mult), [rrb, smb], [rrb])
                kb.op("dve", lambda e, sb=sb: e.tensor_scalar(out=y[:], in0=O[0][:, sb, 0:256], scalar1=rr[:, 0:1], scalar2=None, op0=ALU.mult),
                      [Ob[0], rrb, yb], [yb])
                kb.op("dve", lambda e, sb=sb: e.scalar_tensor_tensor(out=y[:], in0=O[1][:, sb, 0:256], scalar=rr[:, 1:2], in1=y[:], op0=ALU.mult, op1=ALU.add),
                      [Ob[1], rrb, yb], [yb])
                kb.op("dve", lambda e: e.tensor_tensor(out=y2[:], in0=y[:], in1=y[:], op=ALU.mult), [yb, y2b], [y2b])
                kb.op("dve", lambda e: e.tensor_reduce(out=rr[:, 2:3], in_=y2[:], axis=AX.X, op=ALU.add), [y2b, rrb], [rrb])
                kb.op("act", lambda e: e.activation(out=rr[:, 3:4], in_=rr[:, 2:3], func=AF.Sqrt, bias=epst[:, 0:1], scale=1.0 / 256), [rrb, epsb], [rrb])
                kb.op("dve", lambda e: e.reciprocal(out=rr[:, 3:4], in_=rr[:, 3:4]), [rrb], [rrb])
                kb.op("dve", lambda e, sb=sb: e.scalar_tensor_tensor(out=ost[os_][:, sb, :], in0=y[:], scalar=rr[:, 3:4], in1=gfin[:], op0=ALU.mult, op1=ALU.mult),
                      [yb, rrb, gfinb], [ostb[os_]])
            kb.dma(dout[u, qb * QB:(qb + 1) * QB, :].rearrange("(s p) e -> p s e", p=128), ost[os_][:], reads=[ostb[os_]], writes=[outb], eng="pool")

        for idx in range(N + LOOK):
            if idx < N:
                s_part(idx)
            if idx >= LOOK:
                pv_part(idx - LOOK)
    return kb.emit([outb])


def alibi_tables(slope):
    p = np.arange(128, dtype=np.float64)
    at = np.zeros((128, 140), np.float64)
    for dist in range(64):
        at[:, dist] = slope * (p - 128.0 * dist)
        at[:, 64 + dist] = -slope * (p + 128.0 * dist - 511.0)
    for sb in range(4):
        at[:, 128 + sb] = np.exp(-slope * (sb * 128 + p))
        at[:, 132 + sb] = np.exp(-slope * (511 - sb * 128 - p))
        at[:, 136 + sb] = -slope * sb * 128.0
    pk = p[:, None]; pq = p[None, :]
    G = -slope * (pq - pk)
    bt = np.stack([G, -G, -slope * np.abs(pq - pk)], 1)
    return at.astype(np.float32), bt.astype(NPBF)


NCH = S // 128


def build_Meven():
    kb = KB()
    fm5 = kb.dram("fm5", [NU, 5, 128, S], BF16, "ExternalInput")
    tm3 = kb.dram("tm3", [NU, 3, S, 128], BF16, "ExternalInput")
    dec = kb.dram("dec", [128, NU, 2], F32, "ExternalInput")
    itab = kb.dram("itab", [128, 4 * 128 + 2], F32, "ExternalInput")
    retg = kb.dram("retg", [128, 1], F32, "ExternalInput")
    nab = kb.dram("nab", [NU, 128, 5, 5, 128], F32, "ExternalInput")
    outT = kb.dram("mixT", [NU, 2, 128, S], BF16, "ExternalOutput")

    cx = Ctx(kb)
    A0 = kb.sb("A0", [128, S], BF16); A0b = kb.buf("A0")
    A1 = kb.sb("A1", [128, S], BF16); A1b = kb.buf("A1")
    A2 = kb.sb("A2", [128, S], BF16); A2b = kb.buf("A2")
    A3 = kb.sb("A3", [128, NCH, 128], BF16); A3b = kb.buf("A3")
    A4 = kb.sb("A4", [128, NCH, 128], BF16); A4b = kb.buf("A4")
    SfB = kb.sb("SfB", [128, NCH, 128], BF16); SfBb = kb.buf("SfB")
    SbB = kb.sb("SbB", [128, NCH, 128], BF16); SbBb = kb.buf("SbB")
    Sst = kb.sb("Sst", [128, 2, 128], F32); Sstb = [kb.buf("Sf"), kb.buf("Sb")]
    it = kb.sb("it", [128, 4 * 128 + 2], F32); itb = kb.buf("it")
    dc = kb.sb("dc", [128, NU, 2], F32); dcb = kb.buf("dc")
    rg_ = kb.sb("rg_", [128, 1], F32); rgb = kb.buf("rg")
    lgw = kb.sb("lgw", [128, 8], F32); lgwb = kb.buf("lgw")
    sc = kb.sb("sc", [128, 4], F32); scb = kb.buf("sc")
    MT4 = kb.sb("MT4", [128, 4, 128], F32); MT4b = kb.buf("MT4")
    qd4 = kb.sb("qd4", [128, 2, 4, 128], F32); qd4b = kb.buf("qd4")
    tmpm = kb.sb("tmpm", [128, 128], F32); tmpmb = kb.buf("tmpm")
    ksc = [kb.sb("ksc%d" % i, [128, 128], BF16) for i in range(4)]; kscb = [kb.buf() for _ in range(4)]
    PT = [kb.sb("PT%d" % i, [128, 4, 128], BF16) for i in range(2)]; PTb = [kb.buf() for _ in range(2)]
    qfb = [kb.sb("qfb%d" % i, [128, 2, 512], BF16) for i in range(2)]; qfbb = [kb.buf() for _ in range(2)]
    rstd = kb.sb("rstd", [128, 512], F32); rstdb = kb.buf("rstd")
    sq1 = kb.sb("sq1", [128, 512], BF16); sq1b = kb.buf("sq1")
    tmpo = kb.sb("tmpo", [128, 512], F32); tmpob = kb.buf("tmpo")
    ost = [kb.sb("ost%d" % i, [128, 512], BF16) for i in range(2)]; ostb = [kb.buf() for _ in range(2)]
    nbias = kb.sb("nbias", [128, 5, 5, 128], F32); nbiasb = kb.buf("nbias")
    Lb = [kb.sb("Lb%d" % i, [128, 5, 128], F32) for i in range(2)]; Lbb = [kb.buf() for _ in range(2)]
    En = [kb.sb("En%d" % i, [128, 5, 128], BF16) for i in range(2)]; Enb = [kb.buf() for _ in range(2)]
    outb = kb.buf("out")

    kb.dma(it[:], itab[:, :], writes=[itb], eng="pool")
    kb.dma(dc[:], dec[:, :, :], writes=[dcb], eng="pool")
    kb.dma(rg_[:], retg[:, :], writes=[rgb], eng="pool")
    oi = 0
    for u in range(NU):
        kb.dma(A0[:], fm5[u, 0], writes=[A0b], eng="sync")
        kb.dma(A1[:], fm5[u, 1], writes=[A1b], eng="sync")
        kb.dma(A2[:], fm5[u, 2], writes=[A2b], eng="sync")
        kb.dma(A3[:], tm3[u, 0].rearrange("(n p) d -> p n d", p=128), writes=[A3b], eng="sync")
        kb.dma(A4[:], tm3[u, 1].rearrange("(n p) d -> p n d", p=128), writes=[A4b], eng="sync")
        kb.op("act", lambda e, u=u: e.activation(out=lgw[:, 0:2], in_=dc[:, u, :], func=AF.Exp, scale=-1.0), [dcb], [lgwb])
        kb.op("dve", lambda e: e.tensor_scalar(out=lgw[:, 2:4], in0=lgw[:, 0:2], scalar1=-1.0 / 8, scalar2=1.0 / 7, op0=ALU.mult, op1=ALU.add), [lgwb], [lgwb])
        for cc in (6, 5, 4, 3, 2, 1):
            kb.op("dve", lambda e: e.tensor_tensor(out=lgw[:, 2:4], in0=lgw[:, 2:4], in1=lgw[:, 0:2], op=ALU.mult), [lgwb], [lgwb])
            kb.op("dve", lambda e, cc=cc: e.tensor_scalar(out=lgw[:, 2:4], in0=lgw[:, 2:4], scalar1=-1.0, scalar2=1.0 / cc, op0=ALU.mult, op1=ALU.add), [lgwb], [lgwb])
        kb.op("dve", lambda e: e.tensor_tensor(out=lgw[:, 2:4], in0=lgw[:, 2:4], in1=lgw[:, 0:2], op=ALU.mult), [lgwb], [lgwb])
        kb.op("dve", lambda e: e.tensor_scalar(out=lgw[:, 4:6], in0=lgw[:, 2:4], scalar1=-1.0, scalar2=None, op0=ALU.mult), [lgwb], [lgwb])
        lgf = lgw[:, 4:5]
        lgb = lgw[:, 5:6]
        kb.op("dve", lambda e: e.tensor_scalar(out=tmpm[:], in0=it[:, 0:128], scalar1=lgf, scalar2=None, op0=ALU.mult), [itb, lgwb], [tmpmb])
        kb.op("dve", lambda e: e.scalar_tensor_tensor(out=tmpm[:], in0=it[:, 128:256], scalar=lgb, in1=tmpm[:], op0=ALU.mult, op1=ALU.add), [itb, lgwb, tmpmb], [tmpmb])
        for r in range(4):
            kb.op("act", lambda e, r=r: e.activation(out=MT4[:, r, :], in_=tmpm[:], func=AF.Exp), [tmpmb], [MT4b])
            kb.op("act", lambda e, r=r: e.activation(out=qd4[:, 0, r, :], in_=it[:, 256:384], func=AF.Exp, scale=lgf), [itb, lgwb], [qd4b])
            kb.op("act", lambda e, r=r: e.activation(out=qd4[:, 1, r, :], in_=it[:, 384:512], func=AF.Exp, scale=lgb), [itb, lgwb], [qd4b])
        kb.op("act", lambda e: e.activation(out=sc[:, 0:1], in_=it[:, 512:513], func=AF.Exp, scale=lgf), [itb, lgwb], [scb])
        kb.op("act", lambda e: e.activation(out=sc[:, 1:2], in_=it[:, 513:514], func=AF.Exp, scale=lgb), [itb, lgwb], [scb])
        kb.op("act", lambda e: e.activation(out=sc[:, 2:4], in_=lgw[:, 4:6], func=AF.Exp, scale=128.0), [lgwb], [scb])
        ki = 0
        for d_ in range(2):
            kb.op("dve" if d_ == 0 else "pool", lambda e, d_=d_: e.memset(Sst[:, d_, :], 0.0), [], [Sstb[d_]])
        pbs = [None, None]
        for cnt in range(NCH):
            for d_, (SB_, SBb_) in enumerate(((SfB, SfBb), (SbB, SbBb))):
                n = cnt if d_ == 0 else NCH - 1 - cnt
                if cnt % 4 == 0:
                    pbs[d_] = cx.bank(exclude=tuple(x[1] for x in pbs if x is not None))
                pb, pbb = pbs[d_]
                ks, ksb = ksc[ki % 4], kscb[ki % 4]
                ki += 1
                kb.op("pool", lambda e, n=n, ks=ks, d_=d_: e.tensor_scalar(out=ks[:], in0=A3[:, n, :], scalar1=sc[:, d_:d_ + 1], scalar2=None, op0=ALU.mult),
                      [A3b, scb], [ksb])
                col = (cnt % 4) * 128
                kb.op("pe", lambda e, n=n, ks=ks, pb=pb, col=col: e.matmul(pb[:, col:col + 128], ks[:], A4[:, n, :], start=True, stop=True), [ksb, A4b], [pbb])
                kb.op("act", lambda e, n=n, d_=d_, SB_=SB_: e.copy(out=SB_[:, n, :], in_=Sst[:, d_, :]), [Sstb[d_]], [SBb_])
                kb.op("dve", lambda e, d_=d_, pb=pb, col=col: e.scalar_tensor_tensor(out=Sst[:, d_, :], in0=Sst[:, d_, :], scalar=sc[:, 2 + d_:3 + d_],
                                                                                 in1=pb[:, col:col + 128], op0=ALU.mult, op1=ALU.add),
                      [Sstb[d_], scb, pbb], [Sstb[d_]])
        for g in range(NCH // 4):
            gs = slice(g * 512, (g + 1) * 512)
            pS, pSb = cx.bank()
            for r in range(4):
                n = g * 4 + r
                kb.op("pe", lambda e, n=n, r=r, pS=pS: e.matmul(pS[:, r * 128:(r + 1) * 128], A1[:, n * 128:(n + 1) * 128], A0[:, n * 128:(n + 1) * 128],
                                                           start=True, stop=True), [A0b, A1b], [pSb])
            pt, ptb = PT[g % 2], PTb[g % 2]
            kb.op("dve", lambda e, pS=pS, pt=pt: e.tensor_tensor(out=pt[:], in0=pS[:, :].rearrange("p (r i) -> p r i", r=4), in1=MT4[:], op=ALU.mult),
                  [pSb, MT4b], [ptb])
            qf, qfb_ = qfb[g % 2], qfbb[g % 2]
            for d_ in range(2):
                kb.op("pool", lambda e, d_=d_, qf=qf, gs=gs: e.tensor_tensor(out=qf[:, d_, :], in0=A0[:, gs], in1=qd4[:, d_].rearrange("p r i -> p (r i)"), op=ALU.mult),
                      [A0b, qd4b], [qfb_])
            pO, pOb = cx.bank()
            for r in range(4):
                n = g * 4 + r
                cs = slice(r * 128, (r + 1) * 128)
                kb.op("pe", lambda e, n=n, r=r, cs=cs, pO=pO, pt=pt: e.matmul(pO[:, cs], A4[:, n, :], pt[:, r, :], start=True, stop=False), [A4b, ptb], [pOb])
                kb.op("pe", lambda e, n=n, cs=cs, pO=pO, qf=qf: e.matmul(pO[:, cs], SfB[:, n, :], qf[:, 0, cs], start=False, stop=False), [SfBb, qfb_], [pOb])
                kb.op("pe", lambda e, n=n, cs=cs, pO=pO, qf=qf: e.matmul(pO[:, cs], SbB[:, n, :], qf[:, 1, cs], start=False, stop=True), [SbBb, qfb_], [pOb])
            emit_headnorm(cx, pO, pOb, rg_[:, 0:1], rgb, tmpo[:], tmpob, sq1, sq1b, rstd, rstdb, 512)
            os_, osb_ = ost[oi % 2], ostb[oi % 2]
            oi += 1
            kb.op("dve", lambda e, os_=os_, gs=gs: e.tensor_tensor(out=os_[:], in0=tmpo[:], in1=A2[:, gs], op=ALU.mult), [tmpob, A2b], [osb_])
            kb.dma(outT[u, 0, :, gs], os_[:], reads=[osb_], writes=[outb], eng="pool")
        kb.dma(A0[:], fm5[u, 3], writes=[A0b], eng="sync")
        kb.dma(A1[:], fm5[u, 4], writes=[A1b], eng="sync")
        kb.dma(A4[:], tm3[u, 2].rearrange("(n p) d -> p n d", p=128), writes=[A4b], eng="sync")
        kb.dma(nbias[:], nab[u], writes=[nbiasb], eng="sync")
        li = 0
        for pg in range(NCH // 4):
            gs = slice(pg * 512, (pg + 1) * 512)
            pO, pOb = cx.bank()
            pD, pDb = cx.bank()
            for r in range(4):
                p = pg * 4 + r
                kt0 = min(max(p - 2, 0), 59)
                var = {0: 0, 1: 1, 62: 3, 63: 4}.get(p, 2)
                pSa, pSab = cx.bank(exclude=(pOb, pDb))
                pSc, pScb = cx.bank(exclude=(pOb, pDb))
                for a in range(5):
                    tgt = pSa[:, a * 128:(a + 1) * 128] if a < 4 else pSc[:, 0:128]
                    tb_ = pSab if a < 4 else pScb
                    kb.op("pe", lambda e, tgt=tgt, a=a, kt0=kt0, p=p: e.matmul(tgt, A1[:, (kt0 + a) * 128:(kt0 + a + 1) * 128], A0[:, p * 128:(p + 1) * 128],
                                                                         start=True, stop=True), [A0b, A1b], [tb_])
                L, Lb_ = Lb[li % 2], Lbb[li % 2]
                E_, Eb_ = En[li % 2], Enb[li % 2]
                li += 1
                kb.op("dve", lambda e, L=L, pSa=pSa, var=var: e.tensor_tensor(out=L[:, 0:4, :], in0=pSa[:, :].rearrange("p (a q) -> p a q", a=4),
                                                                           in1=nbias[:, var, 0:4, :], op=ALU.add), [pSab, nbiasb], [Lb_])
                kb.op("dve", lambda e, L=L, pSc=pSc, var=var: e.tensor_tensor(out=L[:, 4, :], in0=pSc[:, 0:128], in1=nbias[:, var, 4, :], op=ALU.add),
                      [pScb, nbiasb], [Lb_])
                kb.op("act", lambda e, L=L, E_=E_: e.activation(out=E_[:], in_=L[:], func=AF.Exp), [Lb_], [Eb_])
                cs = slice(r * 128, (r + 1) * 128)
                for a in range(5):
                    kb.op("pe", lambda e, a=a, kt0=kt0, cs=cs, pO=pO, E_=E_: e.matmul(pO[:, cs], A4[:, kt0 + a, :], E_[:, a, :], start=(a == 0), stop=(a == 4)),
                          [A4b, Eb_], [pOb])
                for a in range(5):
                    kb.op("pe", lambda e, a=a, cs=cs, pD=pD, E_=E_: e.matmul(pD[:, cs], cx.ones[:], E_[:, a, :], start=(a == 0), stop=(a == 4)),
                          [cx.onesb, Eb_], [pDb])
            kb.op("dve", lambda e, pD=pD: e.reciprocal(out=rstd[:, :], in_=pD[:, :]), [pDb], [rstdb])
            os_, osb_ = ost[oi % 2], ostb[oi % 2]
            oi += 1
            kb.op("dve", lambda e, os_=os_, pO=pO: e.tensor_tensor(out=os_[:], in0=pO[:, :], in1=rstd[:, :], op=ALU.mult), [pOb, rstdb], [osb_])
            kb.dma(outT[u, 1, :, gs], os_[:], reads=[osb_], writes=[outb], eng="pool")
    return kb.emit([outb])


def retention_itab():
    j = np.arange(128, dtype=np.float32)[:, None]
    i = np.arange(128, dtype=np.float32)[None, :]
    A = np.maximum(i - j, 0.0)
    B = np.maximum(j - i, 0.0)
    r1 = np.broadcast_to(i + 1.0, (128, 128))
    r2 = np.broadcast_to(128.0 - i, (128, 128))
    return np.ascontiguousarray(np.concatenate([A, B, r1, r2, 127.0 - j, j], 1).astype(np.float32))


def na_bias_tables(rpb_h):
    out = np.full((5, 5, 128, 128), -30000.0, np.float32)
    kk = np.arange(128)
    for vi, p in enumerate((0, 1, 30, 62, 63)):
        kt0 = min(max(p - 2, 0), 59)
        rq = 2 * p + kk // 64
        cq = kk % 64
        rs = np.clip(rq - 4, 0, 120)
        cs_ = np.clip(cq - 8, 0, 48)
        for a in range(5):
            rk = 2 * (kt0 + a) + kk // 64
            ck = kk % 64
            okr = (rk[:, None] >= rs[None, :]) & (rk[:, None] < rs[None, :] + 8)
            okc = (ck[:, None] >= cs_[None, :]) & (ck[:, None] < cs_[None, :] + 16)
            dr = np.clip(rk[:, None] - rq[None, :] + 7, 0, 14)
            dcx = np.clip(ck[:, None] - cq[None, :] + 15, 0, 30)
            vals = rpb_h[dr, dcx]
            out[vi, a] = np.where(okr & okc, vals, np.float32(-30000.0))
    return np.ascontiguousarray(out.transpose(2, 0, 1, 3))


SC = float(128 ** -0.5)
EPI_EVEN = [("copy",)] * 8 + [("scale", SC)] * 8 + [("silu",)] * 8 + [("norm", 0, SC)] * 8 + [("norm", 1, 1.0)] * 8
TMS_EVEN = [SC, SC, 1.0, 1.0, 1.0, 1.0]
EPI_ODD = [("norm", 0, SC)] * 16 + [("norm", 1, 1.0)] * 16
TMS_ODD = [1.0] * 4


def _prog(key, fn):
    if key not in _CACHE:
        _CACHE[key] = fn()
    return _CACHE[key]


def _run(nc, in_maps):
    res = run_bass_kernel_spmd(nc, in_maps, core_ids=list(range(NCORES)))
    return res.results


def _lay_f1(W):
    g = W[:, :FH].reshape(KC, 128, FC, 128)
    u = W[:, FH:].reshape(KC, 128, FC, 128)
    return np.ascontiguousarray(np.stack([g, u], 0).transpose(3, 2, 1, 0, 4))


def _lay_tm(W):
    G = W.shape[1] // 512
    return np.ascontiguousarray(W.reshape(KC, 128, G, 512).transpose(2, 1, 0, 3))


def _bc(a, shape):
    return np.ascontiguousarray(np.broadcast_to(a, shape)).astype(np.float32)


def kernel(x, mem, norm_mix_g, norm_xattn_g, norm_mem_g, norm_ffn_g,
           w_in_ab, ret_decay_fwd, ret_decay_bwd, ret_out_g, na_q_g, na_k_g, na_rpb, w_out_ab,
           w_in_c, diff_q_g, diff_k_g, lambda_q1, lambda_k1, lambda_q2, lambda_k2, diff_out_g, w_out_c,
           w_xq, w_xkv, w_xo, xq_g, xk_g, w_ffn_in, w_ffn_out):
    f = lambda a: np.asarray(a, dtype=np.float32)
    x = f(x); mem = f(mem)
    DEPTH = 4
    lay = {}
    for i in range(DEPTH):
        j = i // 2
        if i % 2 == 0:
            W = f(w_in_ab[j])
            lay[("fm", i)] = lay_lhsT(np.concatenate([W[:, 0:1024], W[:, 1024:2048], W[:, 3072:4096], W[:, 4096:5120], W[:, 5120:6144]], 1))
            lay[("tm", i)] = _lay_tm(np.concatenate([W[:, 1024:2048], W[:, 2048:3072], W[:, 6144:7168]], 1))
            lay[("out", i)] = lay_lhsT(f(w_out_ab[j]))
        else:
            W = f(w_in_c[j])
            lay[("fm", i)] = lay_lhsT(W[:, 0:4096])
            lay[("tm", i)] = _lay_tm(W[:, 4096:6144])
            lay[("out", i)] = lay_lhsT(f(w_out_c[j]))
        lay[("xq", i)] = lay_lhsT(f(w_xq[i]))
        lay[("xk", i)] = lay_lhsT(f(w_xkv[i])[:, :512])
        lay[("xv", i)] = lay_rhs(f(w_xkv[i])[:, 512:])
        lay[("xo", i)] = lay_lhsT(f(w_xo[i]))
        lay[("f1", i)] = _lay_f1(f(w_ffn_in[i]))
        lay[("f2", i)] = lay_lhsT(f(w_ffn_out[i]))
    keys = list(lay.keys())
    sizes = [lay[k].size for k in keys]
    total = sum(sizes)
    unit = NCORES * 128 * 4096
    tot_pad = ((total + unit - 1) // unit) * unit
    flat = np.zeros(tot_pad, np.float32)
    off = 0
    for k, n in zip(keys, sizes):
        flat[off:off + n] = lay[k].ravel()
        off += n
    shards = flat.reshape(NCORES, 128, -1)
    outs = run_cast([shards[c] for c in range(NCORES)])
    flat_b = np.concatenate([np.asarray(o).reshape(-1) for o in outs])
    wb = {}
    off = 0
    for k, n in zip(keys, sizes):
        wb[k] = flat_b[off:off + n].reshape(lay[k].shape)
        off += n
    del flat, lay

    xs = x.reshape(NCORES, T, D)
    xT = [np.ascontiguousarray(xs[c].T.reshape(KC, 128, T)) for c in range(NCORES)]
    memT = [np.ascontiguousarray(mem[b].T.reshape(KC, 128, NM)) for b in range(2)]
    ident = np.eye(128, dtype=np.float32).astype(NPBF)
    itab = retention_itab()

    for i in range(DEPTH):
        j = i // 2
        even = (i % 2 == 0)
        if even:
            nc = _prog("P_even", lambda: build_P(EPI_EVEN, TMS_EVEN))
            gains = np.concatenate([lay_gain(f(norm_mix_g[i])), f(na_q_g[j])[:, None], f(na_k_g[j])[:, None]], 1)
        else:
            nc = _prog("P_odd", lambda: build_P(EPI_ODD, TMS_ODD))
            gains = np.concatenate([lay_gain(f(norm_mix_g[i])), f(diff_q_g[j])[:, None], f(diff_k_g[j])[:, None]], 1)
        gains = np.ascontiguousarray(gains.astype(np.float32))
        res = _run(nc, [{"xT": xT[c], "w_fm": wb[("fm", i)], "w_tm": wb[("tm", i)], "gains": gains} for c in range(NCORES)])
        fmT = [np.asarray(r["fmT"]) for r in res]
        tm = [np.asarray(r["tm"]) for r in res]
        mixT = [np.empty((KC, 128, T), NPBF) for _ in range(NCORES)]
        if even:
            nc = _prog("M_even", build_Meven)
            maps = []
            for c in range(NCORES):
                b = c // 4
                fm5 = np.empty((NU, 5, 128, S), NPBF)
                tm3 = np.empty((NU, 3, S, 128), NPBF)
                for u in range(NU):
                    h = (c % 4) * 2 + u
                    for q in range(4):
                        src = b * 4 + q
                        for t_ in range(5):
                            fm5[u, t_, :, q * T:(q + 1) * T] = fmT[src][t_ * 8 + h]
                        for t_ in range(3):
                            tm3[u, t_, q * T:(q + 1) * T, :] = tm[src][:, t_ * 1024 + h * 128:t_ * 1024 + (h + 1) * 128]
                hs = [(c % 4) * 2, (c % 4) * 2 + 1]
                dec = _bc(np.stack([f(ret_decay_fwd[j])[hs], f(ret_decay_bwd[j])[hs]], 1)[None], (128, NU, 2))
                nab = np.stack([na_bias_tables(f(na_rpb[j])[h]) for h in hs])
                maps.append({"fm5": fm5, "tm3": tm3, "dec": dec, "itab": itab, "retg": np.ascontiguousarray(f(ret_out_g[j])[:, None]), "nab": nab})
            res = _run(nc, maps)
            for c in range(NCORES):
                b, q = c // 4, c % 4
                for h in range(8):
                    o = np.asarray(res[b * 4 + h // 2]["mixT"])
                    mixT[c][h] = o[h % 2, 0, :, q * T:(q + 1) * T]
                    mixT[c][8 + h] = o[h % 2, 1, :, q * T:(q + 1) * T]
        else:
            nc = _prog("M_odd", build_Modd)
            lam_init = 0.8 - 0.6 * float(np.exp(-0.3 * i))
            lam4 = _bc(np.stack([f(lambda_q1[j]), f(lambda_k1[j]), f(lambda_q2[j]), f(lambda_k2[j])])[None], (128, 4, 128))
            gout = _bc(f(diff_out_g[j])[None], (128, 256))
            consts = _bc(np.array([lam_init, 1.0 - lam_init], np.float32)[None], (128, 2))
            maps = []
            for c in range(NCORES):
                b = c // 4
                qTa = np.empty((NU, 2, 128, S), NPBF)
                kTa = np.empty((NU, 2, 128, S), NPBF)
                va = np.empty((NU, S, 256), NPBF)
                ats, bts = [], []
                for u in range(NU):
                    h = u * 4 + (c % 4)
                    a_, b_ = alibi_tables(2.0 ** (-(h + 1)))
                    ats.append(a_); bts.append(b_)
                    for q in range(4):
                        src = b * 4 + q
                        for cc in range(2):
                            qTa[u, cc, :, q * T:(q + 1) * T] = fmT[src][h * 2 + cc]
                            kTa[u, cc, :, q * T:(q + 1) * T] = fmT[src][16 + h * 2 + cc]
                        va[u, q * T:(q + 1) * T, :] = tm[src][:, h * 256:(h + 1) * 256]
                maps.append({"qT": qTa, "kT": kTa, "v": va, "atab": np.stack(ats), "btile": np.stack(bts), "ident": ident,
                             "lam4": lam4, "gout": gout, "consts": consts})
            res = _run(nc, maps)
            for c in range(NCORES):
                b, q = c // 4, c % 4
                for h in range(8):
                    o = np.asarray(res[b * 4 + h % 4]["d_tm"])[h // 4, q * T:(q + 1) * T, :]
                    mixT[c][2 * h] = o[:, 0:128].T
                    mixT[c][2 * h + 1] = o[:, 128:256].T
        nc = _prog("C", build_C)
        gains = np.ascontiguousarray(np.concatenate([lay_gain(f(norm_xattn_g[i])), lay_gain(f(norm_ffn_g[i])), lay_gain(f(norm_mem_g[i])),
                                                     f(xq_g[i])[:, None], f(xk_g[i])[:, None]], 1).astype(np.float32))
        res = _run(nc, [{"xT": xT[c], "mixT": mixT[c], "memT": memT[c // 4], "w_out": wb[("out", i)], "w_xq": wb[("xq", i)],
                         "w_xk": wb[("xk", i)], "w_xv": wb[("xv", i)], "w_xo": wb[("xo", i)], "w_f1": wb[("f1", i)],
                         "w_f2": wb[("f2", i)], "gains": gains} for c in range(NCORES)])
        xT = [np.asarray(r["xTo"]) for r in res]
    out = np.stack([xT[c].reshape(D, T).T for c in range(NCORES)]).reshape(2, S, D)
    return np.ascontiguousarray(out.astype(np.float32))
```

```python
import numpy as np
import ml_dtypes
from contextlib import ExitStack
import concourse.bass as bass
import concourse.mybir as mybir
from concourse.bass_utils import run_bass_kernel_spmd

F32 = mybir.dt.float32
BF16 = mybir.dt.bfloat16
AF = mybir.ActivationFunctionType
ALU = mybir.AluOpType
NPBF = ml_dtypes.bfloat16

NCORES = 8
D = 2048
KC = 16
S = 8192
T = 2048
TB = 512
NTB = T // TB
FH = 5632
FC = 44
EPS = 1e-6
SAME_ENGINE_SYNC = True
FAST_RSQRT = True


class Buf:
    __slots__ = ("name", "writers", "readers", "war")

    def __init__(self, name):
        self.name = name
        self.writers = []
        self.readers = []
        self.war = []


class Op:
    __slots__ = ("eng", "idx", "fn", "deps", "dma", "semkey", "count", "signal")


class KB:
    ENGS = ("pe", "act", "dve", "pool", "sync")
    SEMID = 0

    def __init__(self, nc=None):
        self.fused = nc is not None
        self.nc = nc if nc is not None else bass.Bass("TRN2", target_bir_lowering=False)
        if self.fused:
            self.cm = self.nc.cleanup_on_exit()
            self.cm.__enter__()
        self.ops = {e: [] for e in self.ENGS}
        self.stack = ExitStack()
        self.dma_count = {}
        self.nbuf = 0

    def dram(self, name, shape, dtype, kind):
        return self.nc.dram_tensor(name, list(shape), dtype, kind=kind).ap()

    def sb(self, name, shape, dtype):
        return self.stack.enter_context(self.nc.sbuf_tensor(name, list(shape), dtype))

    def ps(self, name, shape, dtype=F32):
        return self.stack.enter_context(self.nc.psum_tensor(name, list(shape), dtype))

    def buf(self, name=None):
        self.nbuf += 1
        return Buf(name or f"b{self.nbuf}")

    def op(self, eng, fn, reads=(), writes=(), dma=False):
        o = Op()
        o.eng = eng
        o.fn = fn
        o.dma = dma
        o.signal = False
        o.count = None
        o.semkey = None
        deps = []
        for b in reads:
            deps.extend(b.writers)
        for b in writes:
            if b.readers:
                b.war = b.readers
                b.readers = []
                b.writers = []
            deps.extend(b.war)
        best = {}
        dmas = []
        for d in deps:
            if d.dma:
                dmas.append(d)
            else:
                if d.eng == eng and not dma and (eng == "pe" or not SAME_ENGINE_SYNC):
                    continue
                if d.eng not in best or d.idx > best[d.eng].idx:
                    best[d.eng] = d
        o.deps = list(best.values()) + list({id(d): d for d in dmas}.values())
        for d in o.deps:
            d.signal = True
        o.idx = len(self.ops[eng])
        self.ops[eng].append(o)
        if dma:
            key = writes[0]
            o.semkey = key
            self.dma_count[id(key)] = self.dma_count.get(id(key), 0) + 16
            o.count = self.dma_count[id(key)]
            o.signal = True
        for b in reads:
            b.readers.append(o)
        for b in writes:
            b.writers.append(o)
        return o

    def dma(self, out, in_, reads=(), writes=(), eng="sync"):
        return self.op(eng, lambda e: e.dma_start(out=out, in_=in_), reads, writes, dma=True)

    def emit(self, final_bufs):
        nc = self.nc
        st = self.stack
        def mksem(name):
            if self.fused:
                KB.SEMID += 1
                return nc.alloc_semaphore(name="%s_%d" % (name, KB.SEMID))
            return st.enter_context(nc.semaphore(name))
        eng_sem = {e: mksem("s_" + e) for e in ("pe", "act", "dve", "pool")}
        dma_sem = {}
        keys = {}
        for e in self.ENGS:
            for o in self.ops[e]:
                if o.dma and id(o.semkey) not in dma_sem:
                    dma_sem[id(o.semkey)] = mksem("d_%d" % len(dma_sem))
                    keys[id(o.semkey)] = o.semkey
        for e in ("pe", "act", "dve", "pool"):
            c = 0
            for o in self.ops[e]:
                if not o.dma and o.signal:
                    c += 1
                    o.count = c
        engobj = {"pe": None, "act": None, "dve": None, "pool": None, "sync": None}

        def run(e, eng):
            waited = {}
            for o in self.ops[e]:
                for d in o.deps:
                    if d.dma:
                        sem = dma_sem[id(d.semkey)]
                        k = ("d", id(d.semkey))
                    else:
                        sem = eng_sem[d.eng]
                        k = ("e", d.eng)
                    if waited.get(k, 0) >= d.count:
                        continue
                    waited[k] = d.count
                    eng.wait_ge(sem, d.count)
                ins = o.fn(eng)
                if o.dma:
                    ins.then_inc(dma_sem[id(o.semkey)], 16)
                elif o.signal:
                    ins.then_inc(eng_sem[e], 1)
            if e == "sync":
                for b in final_bufs:
                    eng.wait_ge(dma_sem[id(b)], self.dma_count[id(b)])

        with nc.Block() as block:
            @block.tensor
            def _(eng):
                run("pe", eng)

            @block.scalar
            def _(eng):
                run("act", eng)

            @block.vector
            def _(eng):
                run("dve", eng)

            @block.gpsimd
            def _(eng):
                run("pool", eng)

            @block.sync
            def _(eng):
                run("sync", eng)
        st.close()
        if self.fused:
            nc.all_engine_barrier()
            self.cm.__exit__(None, None, None)
        return nc


def build_cast(F):
    kb = KB()
    CH = 4096
    nch = F // CH
    assert F % CH == 0
    src = kb.dram("src", [128, F], F32, "ExternalInput")
    dst = kb.dram("dst", [128, F], BF16, "ExternalOutput")
    a = [kb.sb("a%d" % i, [128, CH], F32) for i in range(3)]
    b = [kb.sb("b%d" % i, [128, CH], BF16) for i in range(3)]
    ab = [kb.buf() for _ in range(3)]
    bb = [kb.buf() for _ in range(3)]
    obs = [kb.buf("out%d" % i) for i in range(3)]
    for i in range(nch):
        s = i % 3
        kb.dma(a[s][:], src[:, i * CH:(i + 1) * CH], writes=[ab[s]], eng="sync")
        e = ("dve", "act", "pool")[i % 3]
        if e == "act":
            kb.op("act", lambda eng, s=s: eng.copy(out=b[s][:], in_=a[s][:]), [ab[s]], [bb[s]])
        else:
            kb.op(e, lambda eng, s=s: eng.tensor_copy(out=b[s][:], in_=a[s][:]), [ab[s]], [bb[s]])
        kb.dma(dst[:, i * CH:(i + 1) * CH], b[s][:], reads=[bb[s]], writes=[obs[s]], eng="sync")
    return kb.emit(obs)


_CACHE = {}


def run_cast(flat_list):
    F = flat_list[0].shape[1]
    key = ("cast", F)
    if key not in _CACHE:
        _CACHE[key] = build_cast(F)
    res = run_bass_kernel_spmd(_CACHE[key], [{"src": f} for f in flat_list], core_ids=list(range(NCORES)))
    return [r["dst"] for r in res.results]


class Ctx:
    def __init__(self, kb, npsum=6):
        self.kb = kb
        self.pbank = [kb.ps("pg%d" % i, [128, 512]) for i in range(npsum)]
        self.pbuf = [kb.buf("pg%d" % i) for i in range(npsum)]
        self.pi = 0
        self.pst = kb.ps("pstat", [128, 512])
        self.pstb = kb.buf("pstat")
        self.ones = kb.sb("ones_bf", [128, 128], BF16)
        self.onesb = kb.buf("ones")
        self.epst = kb.sb("eps_t", [128, 1], F32)
        self.epsb = kb.buf("eps")
        kb.op("pool", lambda e: e.memset(self.ones[:], 1.0), [], [self.onesb])
        kb.op("pool", lambda e: e.memset(self.epst[:], EPS), [], [self.epsb])

    def bank(self, exclude=()):
        while True:
            i = self.pi
            self.pi = (self.pi + 1) % len(self.pbank)
            if not any(self.pbuf[i] is x for x in exclude):
                return self.pbank[i], self.pbuf[i]


def emit_rstd(cx, sq_aps, sq_bufs, nfree, dim, rstd, rstdb):
    kb = cx.kb
    n = len(sq_aps)
    for i, a in enumerate(sq_aps):
        kb.op("pe", lambda e, a=a, i=i: e.matmul(cx.pst[:, 0:nfree], cx.ones[:], a, start=(i == 0), stop=(i == n - 1)),
              [cx.onesb] + list(sq_bufs), [cx.pstb])
    if FAST_RSQRT:
        kb.op("act", lambda e: e.activation(out=rstd, in_=cx.pst[:, 0:nfree], func=AF.Ln, bias=cx.epst[:, 0:1], scale=1.0 / dim),
              [cx.pstb, cx.epsb], [rstdb])
        kb.op("act", lambda e: e.activation(out=rstd, in_=rstd, func=AF.Exp, scale=-0.5), [rstdb], [rstdb])
    else:
        kb.op("act", lambda e: e.activation(out=rstd, in_=cx.pst[:, 0:nfree], func=AF.Sqrt, bias=cx.epst[:, 0:1], scale=1.0 / dim),
              [cx.pstb, cx.epsb], [rstdb])
        kb.op("dve", lambda e: e.reciprocal(out=rstd, in_=rstd), [rstdb], [rstdb])


def emit_norm_block(cx, xb, xbb, g, gb, hT, hTb, sq, sqb, rstd, rstdb, nfree=TB, nk=KC):
    kb = cx.kb
    xl = list(xbb) if isinstance(xbb, (list, tuple)) else [xbb] * nk
    kb.op("act", lambda e: e.activation(out=sq[:, 0:nk, 0:nfree], in_=xb[:, 0:nk, 0:nfree], func=AF.Square), list(set(xl)), [sqb])
    emit_rstd(cx, [sq[:, k, 0:nfree] for k in range(nk)], [sqb], nfree, nk * 128, rstd[:, 0:nfree], rstdb)
    for k in range(nk):
        kb.op("dve", lambda e, k=k: e.scalar_tensor_tensor(out=hT[:, k, 0:nfree], in0=xb[:, k, 0:nfree], scalar=g[:, k:k + 1],
                                                            in1=rstd[:, 0:nfree], op0=ALU.mult, op1=ALU.mult),
              [xl[k], gb, rstdb], [hTb])


def emit_headnorm(cx, pbank, pbuf, gcol, gb, out_ap, outb, sq1, sq1b, rstd, rstdb, nfree):
    kb = cx.kb
    kb.op("act", lambda e: e.activation(out=sq1[:, 0:nfree], in_=pbank[:, 0:nfree], func=AF.Square), [pbuf], [sq1b])
    emit_rstd(cx, [sq1[:, 0:nfree]], [sq1b], nfree, 128, rstd[:, 0:nfree], rstdb)
    kb.op("dve", lambda e: e.scalar_tensor_tensor(out=out_ap, in0=pbank[:, 0:nfree], scalar=gcol, in1=rstd[:, 0:nfree],
                                                  op0=ALU.mult, op1=ALU.mult), [pbuf, gb, rstdb], [outb])


class WPool:
    def __init__(self, kb, name, shape, n):
        self.kb = kb
        self.t = [kb.sb("%s%d" % (name, i), shape, BF16) for i in range(n)]
        self.b = [kb.buf("%s%d" % (name, i)) for i in range(n)]
        self.i = 0

    def load(self, src, eng="sync"):
        i = self.i
        self.i = (self.i + 1) % len(self.t)
        self.kb.dma(self.t[i][:], src, writes=[self.b[i]], eng=eng)
        return self.t[i], self.b[i]


def emit_linear_fm(cx, w, wb, nk, rhs_fn, rhs_bufs, nfree=TB):
    kb = cx.kb
    pb, pbb = cx.bank()
    for k in range(nk):
        kb.op("pe", lambda e, k=k: e.matmul(pb[:, 0:nfree], w[:, k, :], rhs_fn(k), start=(k == 0), stop=(k == nk - 1)),
              [wb] + list(rhs_bufs), [pbb])
    return pb, pbb


NM = 256
XH = 4


def build_C():
    kb = KB()
    xT = kb.dram("xT", [KC, 128, T], F32, "ExternalInput")
    mixT = kb.dram("mixT", [KC, 128, T], BF16, "ExternalInput")
    memT = kb.dram("memT", [KC, 128, NM], F32, "ExternalInput")
    w_out = kb.dram("w_out", [KC, 128, KC, 128], BF16, "ExternalInput")
    w_xq = kb.dram("w_xq", [XH, 128, KC, 128], BF16, "ExternalInput")
    w_xk = kb.dram("w_xk", [XH, 128, KC, 128], BF16, "ExternalInput")
    w_xv = kb.dram("w_xv", [128, KC, 512], BF16, "ExternalInput")
    w_xo = kb.dram("w_xo", [KC, 128, XH, 128], BF16, "ExternalInput")
    w_f1 = kb.dram("w_f1", [FC, 128, KC, 2, 128], BF16, "ExternalInput")
    w_f2 = kb.dram("w_f2", [KC, 128, FC, 128], BF16, "ExternalInput")
    gains = kb.dram("gains", [128, 3 * KC + 2], F32, "ExternalInput")
    xo = kb.dram("xTo", [KC, 128, T], F32, "ExternalOutput")

    cx = Ctx(kb)
    xb = kb.sb("xb", [128, KC, TB], F32); xbb = kb.buf("xb")
    mb = kb.sb("mb", [128, KC, TB], BF16); mbb = kb.buf("mb")
    hT = kb.sb("hT", [128, KC, TB], BF16); hTb = kb.buf("hT")
    aT = kb.sb("aT", [128, FC, TB], BF16); aTb = kb.buf("aT")
    rstd = kb.sb("rstd", [128, TB], F32); rstdb = kb.buf("rstd")
    sq1 = kb.sb("sq1", [128, TB], BF16); sq1b = kb.buf("sq1")
    gt = kb.sb("gt", [128, 3 * KC + 2], F32); gtb = kb.buf("gt")
    gq = kb.sb("gq", [128, 1], F32); gqb = kb.buf("gq")
    kx = kb.sb("kx", [128, XH, NM], BF16); kxb = kb.buf("kx")
    vx = kb.sb("vx", [128, 2, XH * 128], BF16); vxb = kb.buf("vx")
    qx = kb.sb("qx", [128, XH, TB], BF16); qxb = kb.buf("qx")
    E = kb.sb("E", [128, 2, TB], BF16); Eb = kb.buf("E")
    oT = kb.sb("oT", [128, XH, TB], BF16); oTb = kb.buf("oT")
    sg = [kb.sb("sg%d" % i, [128, TB], F32) for i in range(2)]; sgb = [kb.buf() for _ in range(2)]
    wA = WPool(kb, "wA", [128, KC, 128], 2)
    wO = WPool(kb, "wO", [128, XH, 128], 2)
    wF1 = WPool(kb, "wF1", [128, KC, 2, 128], 2)
    wF2 = WPool(kb, "wF2", [128, FC, 128], 2)
    outb = kb.buf("out")

    kb.dma(gt[:], gains[:, :], writes=[gtb], eng="pool")
    kb.op("dve", lambda e: e.tensor_scalar(out=gq[:], in0=gt[:, 3 * KC:3 * KC + 1], scalar1=float(128 ** -0.5), scalar2=None, op0=ALU.mult),
          [gtb], [gqb])

    mf = xb
    for k in range(KC):
        pass
    kb.dma(xb[:, :, 0:NM], memT.rearrange("k p t -> p k t"), writes=[xbb], eng="sync")
    emit_norm_block(cx, xb, xbb, gt[:, 2 * KC:3 * KC], gtb, hT, hTb, aT, aTb, rstd, rstdb, nfree=NM)
    for h in range(XH):
        w, wb = wA.load(w_xk[h])
        pb, pbb = emit_linear_fm(cx, w, wb, KC, lambda k: hT[:, k, 0:NM], [hTb], nfree=NM)
        emit_headnorm(cx, pb, pbb, gt[:, 3 * KC + 1:3 * KC + 2], gtb, kx[:, h, :], kxb, sq1, sq1b, rstd, rstdb, NM)
    wv = aT[:, 16:32, :]
    kb.dma(wv, w_xv[:, :, :], reads=[], writes=[aTb], eng="sync")
    for mt in range(2):
        pb, pbb = cx.bank()
        for k in range(KC):
            kb.op("pe", lambda e, k=k, mt=mt, pb=pb: e.matmul(pb[:, :], hT[:, k, mt * 128:(mt + 1) * 128], aT[:, 16 + k, :],
                                                         start=(k == 0), stop=(k == KC - 1)), [hTb, aTb], [pbb])
        kb.op("act", lambda e, mt=mt, pb=pb: e.copy(out=vx[:, mt, :], in_=pb[:, :]), [pbb], [vxb])

    for tb in range(NTB):
        ts = slice(tb * TB, (tb + 1) * TB)
        kb.dma(xb[:], xT[:, :, ts].rearrange("k p t -> p k t"), writes=[xbb], eng="sync")
        kb.dma(mb[:], mixT[:, :, ts].rearrange("k p t -> p k t"), writes=[mbb], eng="pool")
        for i in range(KC):
            w, wb = wA.load(w_out[i])
            pb, pbb = emit_linear_fm(cx, w, wb, KC, lambda k: mb[:, k, :], [mbb])
            kb.op("dve", lambda e, i=i, pb=pb: e.tensor_tensor(out=xb[:, i, :], in0=xb[:, i, :], in1=pb[:, :], op=ALU.add), [pbb, xbb], [xbb])
        emit_norm_block(cx, xb, xbb, gt[:, 0:KC], gtb, hT, hTb, aT, aTb, rstd, rstdb)
        for h in range(XH):
            w, wb = wA.load(w_xq[h])
            pb, pbb = emit_linear_fm(cx, w, wb, KC, lambda k: hT[:, k, :], [hTb])
            emit_headnorm(cx, pb, pbb, gq[:, 0:1], gqb, qx[:, h, :], qxb, sq1, sq1b, rstd, rstdb, TB)
        for h in range(XH):
            for mt in range(2):
                pb, pbb = cx.bank()
                kb.op("pe", lambda e, h=h, mt=mt, pb=pb: e.matmul(pb[:, :], kx[:, h, mt * 128:(mt + 1) * 128], qx[:, h, :], start=True, stop=True),
                      [kxb, qxb], [pbb])
                kb.op("act", lambda e, mt=mt, pb=pb: e.activation(out=E[:, mt, :], in_=pb[:, :], func=AF.Exp), [pbb], [Eb])
            po, pob = cx.bank()
            pd, pdb = cx.bank()
            for mt in range(2):
                kb.op("pe", lambda e, h=h, mt=mt, po=po: e.matmul(po[:, :], vx[:, mt, h * 128:(h + 1) * 128], E[:, mt, :], start=(mt == 0), stop=(mt == 1)),
                      [vxb, Eb], [pob])
            for mt in range(2):
                kb.op("pe", lambda e, mt=mt, pd=pd: e.matmul(pd[:, :], cx.ones[:], E[:, mt, :], start=(mt == 0), stop=(mt == 1)),
                      [cx.onesb, Eb], [pdb])
            kb.op("dve", lambda e, pd=pd: e.reciprocal(out=rstd[:, :], in_=pd[:, :]), [pdb], [rstdb])
            kb.op("dve", lambda e, h=h, po=po: e.tensor_tensor(out=oT[:, h, :], in0=po[:, :], in1=rstd[:, :], op=ALU.mult), [pob, rstdb], [oTb])
        for i in range(KC):
            w, wb = wO.load(w_xo[i])
            pb, pbb = emit_linear_fm(cx, w, wb, XH, lambda k: oT[:, k, :], [oTb])
            kb.op("dve", lambda e, i=i, pb=pb: e.tensor_tensor(out=xb[:, i, :], in0=xb[:, i, :], in1=pb[:, :], op=ALU.add), [pbb, xbb], [xbb])
        emit_norm_block(cx, xb, xbb, gt[:, KC:2 * KC], gtb, hT, hTb, aT, aTb, rstd, rstdb)
        for j in range(FC):
            w, wb = wF1.load(w_f1[j])
            pg, pgb = cx.bank()
            pu, pub = cx.bank()
            for k in range(KC):
                kb.op("pe", lambda e, k=k, w=w, pg=pg: e.matmul(pg[:, :], w[:, k, 0, :], hT[:, k, :], start=(k == 0), stop=(k == KC - 1)), [wb, hTb], [pgb])
            for k in range(KC):
                kb.op("pe", lambda e, k=k, w=w, pu=pu: e.matmul(pu[:, :], w[:, k, 1, :], hT[:, k, :], start=(k == 0), stop=(k == KC - 1)), [wb, hTb], [pub])
            s = j % 2
            kb.op("act", lambda e, s=s, pg=pg: e.activation(out=sg[s][:], in_=pg[:, :], func=AF.Silu), [pgb], [sgb[s]])
            kb.op("dve", lambda e, s=s, j=j, pu=pu: e.tensor_tensor(out=aT[:, j, :], in0=sg[s][:], in1=pu[:, :], op=ALU.mult), [sgb[s], pub], [aTb])
        for i in range(KC):
            w, wb = wF2.load(w_f2[i])
            pb, pbb = emit_linear_fm(cx, w, wb, FC, lambda k: aT[:, k, :], [aTb])
            kb.op("dve", lambda e, i=i, pb=pb: e.tensor_tensor(out=xb[:, i, :], in0=xb[:, i, :], in1=pb[:, :], op=ALU.add), [pbb, xbb], [xbb])
        kb.dma(xo[:, :, ts].rearrange("k p t -> p k t"), xb[:], reads=[xbb], writes=[outb], eng="sync")
    return kb.emit([outb])


def lay_lhsT(W, mchunk=128):
    Kd, M = W.shape
    return np.ascontiguousarray(W.reshape(Kd // 128, 128, M // 128, 128).transpose(2, 1, 0, 3))


def lay_rhs(W):
    Kd, N = W.shape
    return np.ascontiguousarray(W.reshape(Kd // 128, 128, N).transpose(1, 0, 2))


def lay_gain(g):
    return np.ascontiguousarray(g.reshape(-1, 128).T)


def build_P(fm_epi, tm_scales):
    NF = len(fm_epi)
    NG = len(tm_scales)
    kb = KB()
    xT = kb.dram("xT", [KC, 128, T], F32, "ExternalInput")
    w_fm = kb.dram("w_fm", [NF, 128, KC, 128], BF16, "ExternalInput")
    w_tm = kb.dram("w_tm", [NG, 128, KC, 512], BF16, "ExternalInput")
    gains = kb.dram("gains", [128, KC + 2], F32, "ExternalInput")
    fmo = kb.dram("fmT", [NF, 128, T], BF16, "ExternalOutput")
    tmo = kb.dram("tm", [T, NG * 512], BF16, "ExternalOutput")

    cx = Ctx(kb)
    xb = kb.sb("xb", [128, KC, TB], F32); xbb = kb.buf("xb")
    hT = kb.sb("hT", [128, KC, TB], BF16); hTb = kb.buf("hT")
    sq = kb.sb("sq", [128, KC, TB], BF16); sqb = kb.buf("sq")
    rstd = kb.sb("rstd", [128, TB], F32); rstdb = kb.buf("rstd")
    sq1 = kb.sb("sq1", [128, TB], BF16); sq1b = kb.buf("sq1")
    gt = kb.sb("gt", [128, KC + 2], F32); gtb = kb.buf("gt")
    gs = kb.sb("gs", [128, 2], F32); gsb = kb.buf("gs")
    wF = WPool(kb, "wF", [128, KC, 128], 3)
    wT = WPool(kb, "wT", [128, KC, 512], 2)
    GRP = 8
    of = [kb.sb("of%d" % i, [128, GRP, TB], BF16) for i in range(2)]; ofb = [kb.buf() for _ in range(2)]
    ot = [kb.sb("ot%d" % i, [128, 4, 512], BF16) for i in range(2)]; otb = [kb.buf() for _ in range(2)]
    outb = [kb.buf("outf%d" % i) for i in range(2)]
    outb2 = [kb.buf("outt%d" % i) for i in range(2)]

    kb.dma(gt[:], gains[:, :], writes=[gtb], eng="pool")
    norm_c = {}
    for ep in fm_epi:
        if ep[0] == "norm":
            norm_c[ep[1]] = ep[2]
    for col, c in norm_c.items():
        kb.op("dve", lambda e, col=col, c=c: e.tensor_scalar(out=gs[:, col:col + 1], in0=gt[:, KC + col:KC + col + 1], scalar1=float(c),
                                                              scalar2=None, op0=ALU.mult), [gtb], [gsb])
    for tb in range(NTB):
        ts = slice(tb * TB, (tb + 1) * TB)
        kb.dma(xb[:], xT[:, :, ts].rearrange("k p t -> p k t"), writes=[xbb], eng="sync")
        emit_norm_block(cx, xb, xbb, gt[:, 0:KC], gtb, hT, hTb, sq, sqb, rstd, rstdb)
        for j in range(NF):
            gi = (j // GRP) % 2
            w, wb = wF.load(w_fm[j])
            pb, pbb = emit_linear_fm(cx, w, wb, KC, lambda k: hT[:, k, :], [hTb])
            dst = of[gi][:, j % GRP, :]
            ep = fm_epi[j]
            if ep[0] == "copy":
                kb.op("act", lambda e, dst=dst, pb=pb: e.copy(out=dst, in_=pb[:, :]), [pbb], [ofb[gi]])
            elif ep[0] == "scale":
                kb.op("act", lambda e, dst=dst, pb=pb, c=ep[1]: e.mul(out=dst, in_=pb[:, :], mul=float(c)), [pbb], [ofb[gi]])
            elif ep[0] == "silu":
                kb.op("act", lambda e, dst=dst, pb=pb: e.activation(out=dst, in_=pb[:, :], func=AF.Silu), [pbb], [ofb[gi]])
            else:
                emit_headnorm(cx, pb, pbb, gs[:, ep[1]:ep[1] + 1], gsb, dst, ofb[gi], sq1, sq1b, rstd, rstdb, TB)
            if j % GRP == GRP - 1 or j == NF - 1:
                j0 = (j // GRP) * GRP
                n = j - j0 + 1
                kb.dma(fmo[j0:j0 + n, :, ts].rearrange("k p t -> p k t"), of[gi][:, 0:n, :], reads=[ofb[gi]], writes=[outb[gi]], eng="pool")
        for g in range(NG):
            w, wb = wT.load(w_tm[g])
            oi = g % 2
            for tt in range(4):
                pb, pbb = cx.bank()
                for k in range(KC):
                    kb.op("pe", lambda e, k=k, tt=tt, pb=pb, w=w: e.matmul(pb[:, :], hT[:, k, tt * 128:(tt + 1) * 128], w[:, k, :],
                                                                      start=(k == 0), stop=(k == KC - 1)), [hTb, wb], [pbb])
                kb.op("act", lambda e, tt=tt, pb=pb, oi=oi, c=tm_scales[g]: e.mul(out=ot[oi][:, tt, :], in_=pb[:, :], mul=float(c)), [pbb], [otb[oi]])
            kb.dma(tmo[tb * TB:(tb + 1) * TB, g * 512:(g + 1) * 512].rearrange("(a p) c -> p a c", p=128), ot[oi][:], reads=[otb[oi]],
                   writes=[outb2[oi]], eng="pool")
    return kb.emit(outb + outb2)


AX = mybir.AxisListType
NU = 2
QB = 512
NQB = S // QB
NKT = S // 128
MODD_WIN = (9, None)


def build_Modd():
    kb = KB()
    qT = kb.dram("qT", [NU, 2, 128, S], BF16, "ExternalInput")
    kT = kb.dram("kT", [NU, 2, 128, S], BF16, "ExternalInput")
    v = kb.dram("v", [NU, S, 256], BF16, "ExternalInput")
    atab = kb.dram("atab", [NU, 128, 140], F32, "ExternalInput")
    btile = kb.dram("btile", [NU, 128, 3, 128], BF16, "ExternalInput")
    ident_d = kb.dram("ident", [128, 128], BF16, "ExternalInput")
    lam4 = kb.dram("lam4", [128, 4, 128], F32, "ExternalInput")
    gout = kb.dram("gout", [128, 256], F32, "ExternalInput")
    consts = kb.dram("consts", [128, 2], F32, "ExternalInput")
    dout = kb.dram("d_tm", [NU, S, 256], BF16, "ExternalOutput")

    sps = [kb.ps("sps%d" % i, [128, 512]) for i in range(3)]; spsb = [kb.buf() for _ in range(3)]
    acc = [kb.ps("acc%d" % i, [128, 512]) for i in range(4)]; accb = [kb.buf() for _ in range(4)]
    kTs = kb.sb("kTs", [128, 2, S], BF16); kTb = kb.buf("kTs")
    Vx = kb.sb("Vx", [128, NKT, 257], BF16); Vxb = kb.buf("Vx")
    qbl = [kb.sb("qbl%d" % i, [128, 2, QB], BF16) for i in range(2)]; qblb = [kb.buf() for _ in range(2)]
    Es = [kb.sb("E%d" % i, [128, QB], BF16) for i in range(3)]; Esb = [kb.buf() for _ in range(3)]
    O = [kb.sb("O%d" % i, [128, 4, 257], F32) for i in range(2)]; Ob = [kb.buf() for _ in range(2)]
    at = kb.sb("at", [128, 140], F32); atb = kb.buf("at")
    bt = kb.sb("bt", [128, 3, 128], BF16); btb = kb.buf("bt")
    ident = kb.sb("ident_sb", [128, 128], BF16); identb = kb.buf("ident")
    l4 = kb.sb("l4", [128, 4, 128], F32); l4b = kb.buf("l4")
    gfin = kb.sb("gfin", [128, 256], F32); gfinb = kb.buf("gfin")
    cst = kb.sb("cst", [128, 2], F32); cstb = kb.buf("cst")
    epst = kb.sb("epst", [128, 1], F32); epsb = kb.buf("eps")
    sm = kb.sb("sm", [128, 8], F32); smb = kb.buf("sm")
    y = kb.sb("y", [128, 256], F32); yb = kb.buf("y")
    y2 = kb.sb("y2", [128, 256], F32); y2b = kb.buf("y2")
    rr = kb.sb("rr", [128, 4], F32); rrb = kb.buf("rr")
    ost = [kb.sb("ost%d" % i, [128, 4, 256], BF16) for i in range(2)]; ostb = [kb.buf() for _ in range(2)]
    outb = [kb.buf("out%d" % i) for i in range(2)]

    kb.dma(ident[:], ident_d[:, :], writes=[identb], eng="pool")
    kb.dma(l4[:], lam4[:, :, :], writes=[l4b], eng="pool")
    kb.dma(gfin[:], gout[:, :], writes=[gfinb], eng="pool")
    kb.dma(cst[:], consts[:, :], writes=[cstb], eng="pool")
    kb.op("pool", lambda e: e.memset(epst[:], EPS), [], [epsb])
    kb.op("pool", lambda e: e.memset(Vx[:, :, 256:257], 1.0), [], [Vxb])
    kb.op("dve", lambda e: e.tensor_tensor(out=y[:, 0:128], in0=l4[:, 0, :], in1=l4[:, 1, :], op=ALU.mult), [l4b], [yb])
    kb.op("dve", lambda e: e.tensor_reduce(out=sm[:, 0:1], in_=y[:, 0:128], axis=AX.X, op=ALU.add), [yb], [smb])
    kb.op("dve", lambda e: e.tensor_tensor(out=y[:, 0:128], in0=l4[:, 2, :], in1=l4[:, 3, :], op=ALU.mult), [l4b, smb], [yb])
    kb.op("dve", lambda e: e.tensor_reduce(out=sm[:, 1:2], in_=y[:, 0:128], axis=AX.X, op=ALU.add), [yb], [smb])
    kb.op("act", lambda e: e.activation(out=sm[:, 4:6], in_=sm[:, 0:2], func=AF.Exp), [smb], [smb])
    kb.op("dve", lambda e: e.tensor_tensor(out=sm[:, 2:3], in0=sm[:, 4:5], in1=sm[:, 5:6], op=ALU.subtract), [smb], [smb])
    kb.op("dve", lambda e: e.tensor_tensor(out=sm[:, 2:3], in0=sm[:, 2:3], in1=cst[:, 0:1], op=ALU.add), [smb, cstb], [smb])
    kb.op("dve", lambda e: e.tensor_scalar(out=sm[:, 3:4], in0=sm[:, 2:3], scalar1=-1.0, scalar2=None, op0=ALU.mult), [smb], [smb])
    kb.op("dve", lambda e: e.tensor_scalar(out=gfin[:], in0=gfin[:], scalar1=cst[:, 1:2], scalar2=None, op0=ALU.mult), [gfinb, cstb], [gfinb])

    LOOK = 2
    for u in range(NU):
        kb.dma(kTs[:, 0, :], kT[u, 0], writes=[kTb], eng="sync")
        kb.dma(kTs[:, 1, :], kT[u, 1], writes=[kTb], eng="sync")
        for h4 in range(4):
            kb.dma(Vx[:, h4 * 16:(h4 + 1) * 16, 0:256], v[u, h4 * 2048:(h4 + 1) * 2048, :].rearrange("(t p) e -> p t e", p=128),
                   writes=[Vxb], eng="sync")
        kb.dma(at[:], atab[u], writes=[atb], eng="pool")
        kb.dma(bt[:], btile[u], writes=[btb], eng="pool")
        W = MODD_WIN[u]
        def krange(qb):
            if W is None:
                return 0, NKT
            return max(0, qb * 4 - W), min(NKT, qb * 4 + 4 + W)
        iters = [(qb, c, kt) for qb in range(NQB) for c in range(2) for kt in range(*krange(qb))]
        N = len(iters)

        def s_part(idx):
            qb, c, kt = iters[idx]
            qs = qb % 2
            if c == 0 and kt == krange(qb)[0]:
                kb.dma(qbl[qs][:], qT[u, :, :, qb * QB:(qb + 1) * QB].rearrange("c p t -> p c t"), writes=[qblb[qs]], eng="sync")
            kq = kt // 4
            sp, spb = sps[idx % 3], spsb[idx % 3]
            E, Eb = Es[idx % 3], Esb[idx % 3]
            diag = (kq == qb)
            kb.op("pe", lambda e: e.matmul(sp[:, :], kTs[:, c, kt * 128:(kt + 1) * 128], qbl[qs][:, c, :], start=True, stop=(not diag)),
                  [kTb, qblb[qs]], [spb])
            if diag:
                sbp = kt % 4
                for sb in range(4):
                    dl = sb - sbp
                    ti = 2 if dl == 0 else (0 if dl > 0 else 1)
                    kb.op("pe", lambda e, sb=sb, ti=ti: e.matmul(sp[:, sb * 128:(sb + 1) * 128], ident[:], bt[:, ti, :], start=False, stop=True,
                                                                  skip_group_check=True), [identb, btb], [spb])
                for sb in range(4):
                    dl = abs(sb - sbp)
                    kb.op("act", lambda e, sb=sb, dl=dl: e.activation(out=E[:, sb * 128:(sb + 1) * 128], in_=sp[:, sb * 128:(sb + 1) * 128],
                                                                      func=AF.Exp, bias=at[:, 136 + dl:137 + dl], scale=1.0), [spb, atb], [Eb])
            else:
                col = (qb * 4 - kt) if kq < qb else (64 + kt - qb * 4)
                kb.op("act", lambda e: e.activation(out=E[:, :], in_=sp[:, :], func=AF.Exp, bias=at[:, col:col + 1], scale=1.0), [spb, atb], [Eb])

        def pv_part(idx):
            qb, c, kt = iters[idx]
            kq = kt // 4
            E, Eb = Es[idx % 3], Esb[idx % 3]
            diag = (kq == qb)
            lo, hi = krange(qb)
            if kq < qb:
                first, last, ph = (kt == lo), (kt == qb * 4 - 1), 0
            elif diag:
                first, last, ph = (kt == qb * 4), (kt == qb * 4 + 3), 1
            else:
                first, last, ph = (kt == qb * 4 + 4), (kt == hi - 1), 2
            for sb in range(4):
                kb.op("pe", lambda e, sb=sb: e.matmul(acc[sb][:, 0:257], E[:, sb * 128:(sb + 1) * 128], Vx[:, kt, :], start=first, stop=last),
                      [Eb, Vxb], [accb[sb]])
            if last:
                for sb in range(4):
                    if ph == 0:
                        kb.op("dve", lambda e, sb=sb: e.tensor_scalar(out=O[c][:, sb, :], in0=acc[sb][:, 0:257], scalar1=at[:, 128 + sb:129 + sb],
                                                                     scalar2=None, op0=ALU.mult), [accb[sb], atb], [Ob[c]])
                    elif ph == 1:
                        if lo >= qb * 4:
                            kb.op("dve", lambda e, sb=sb: e.tensor_copy(out=O[c][:, sb, :], in_=acc[sb][:, 0:257]), [accb[sb]], [Ob[c]])
                        else:
                            kb.op("dve", lambda e, sb=sb: e.tensor_tensor(out=O[c][:, sb, :], in0=O[c][:, sb, :], in1=acc[sb][:, 0:257], op=ALU.add),
                                  [accb[sb], Ob[c]], [Ob[c]])
                    else:
                        kb.op("dve", lambda e, sb=sb: e.scalar_tensor_tensor(out=O[c][:, sb, :], in0=acc[sb][:, 0:257], scalar=at[:, 132 + sb:133 + sb],
                                                                            in1=O[c][:, sb, :], op0=ALU.mult, op1=ALU.add), [accb[sb], atb, Ob[c]], [Ob[c]])
            final = (c == 1) and (kt == hi - 1)
            if final:
                combine(qb)

        def combine(qb):
            os_ = qb % 2
            for sb in range(4):
                kb.op("dve", lambda e, sb=sb: e.reciprocal(out=rr[:, 0:1], in_=O[0][:, sb, 256:257]), [Ob[0], rrb], [rrb])
                kb.op("dve", lambda e, sb=sb: e.reciprocal(out=rr[:, 1:2], in_=O[1][:, sb, 256:257]), [Ob[1], rrb], [rrb])
                kb.op("dve", lambda e: e.tensor_tensor(out=rr[:, 1:2], in0=rr[:, 1:2], in1=sm[:, 3:4], op=ALU.mult), [rrb, smb], [rrb])
                kb.op("dve", lambda e, sb=sb: e.tensor_scalar(out=y[:], in0=O[0][:, sb, 0:256], scalar1=rr[:, 0:1], scalar2=None, op0=ALU.mult),
                      [Ob[0], rrb, yb], [yb])
                kb.op("dve", lambda e, sb=sb: e.scalar_tensor_tensor(out=y[:], in0=O[1][:, sb, 0:256], scalar=rr[:, 1:2], in1=y[:], op0=ALU.mult, op1=ALU.add),
                      [Ob[1], rrb, yb], [yb])
                kb.op("dve", lambda e: e.tensor_tensor(out=y2[:], in0=y[:], in1=y[:], op=ALU.mult), [yb, y2b], [y2b])
                kb.op("dve", lambda e: e.tensor_reduce(out=rr[:, 2:3], in_=y2[:], axis=AX.X, op=ALU.add), [y2b, rrb], [rrb])
                kb.op("act", lambda e: e.activation(out=rr[:, 3:4], in_=rr[:, 2:3], func=AF.Sqrt, bias=epst[:, 0:1], scale=1.0 / 256), [rrb, epsb], [rrb])
                kb.op("dve", lambda e: e.reciprocal(out=rr[:, 3:4], in_=rr[:, 3:4]), [rrb], [rrb])
                kb.op("dve", lambda e, sb=sb: e.scalar_tensor_tensor(out=ost[os_][:, sb, :], in0=y[:], scalar=rr[:, 3:4], in1=gfin[:], op0=ALU.mult, op1=ALU.mult),
                      [yb, rrb, gfinb], [ostb[os_]])
            kb.dma(dout[u, qb * QB:(qb + 1) * QB, :].rearrange("(s p) e -> p s e", p=128), ost[os_][:], reads=[ostb[os_]], writes=[outb[os_]], eng="pool")

        for idx in range(N + LOOK):
            if idx < N:
                s_part(idx)
            if idx >= LOOK:
                pv_part(idx - LOOK)
    return kb.emit(outb)


def alibi_tables(slope):
    p = np.arange(128, dtype=np.float64)
    at = np.zeros((128, 140), np.float64)
    for dist in range(64):
        at[:, dist] = slope * (p - 128.0 * dist)
        at[:, 64 + dist] = -slope * (p + 128.0 * dist - 511.0)
    for sb in range(4):
        at[:, 128 + sb] = np.exp(-slope * (sb * 128 + p))
        at[:, 132 + sb] = np.exp(-slope * (511 - sb * 128 - p))
        at[:, 136 + sb] = -slope * sb * 128.0
    pk = p[:, None]; pq = p[None, :]
    G = -slope * (pq - pk)
    bt = np.stack([G, -G, -slope * np.abs(pq - pk)], 1)
    return at.astype(np.float32), bt.astype(NPBF)


NCH = S // 128


def build_Meven():
    kb = KB()
    fm5 = kb.dram("fm5", [NU, 5, 128, S], BF16, "ExternalInput")
    tm3 = kb.dram("tm3", [NU, 3, S, 128], BF16, "ExternalInput")
    dec = kb.dram("dec", [128, NU, 2], F32, "ExternalInput")
    itab = kb.dram("itab", [128, 4 * 128 + 2], F32, "ExternalInput")
    retg = kb.dram("retg", [128, 1], F32, "ExternalInput")
    nab = kb.dram("nab", [NU, 128, 5, 5, 128], F32, "ExternalInput")
    outT = kb.dram("mixT", [NU, 2, 128, S], BF16, "ExternalOutput")

    cx = Ctx(kb)
    A0 = kb.sb("A0", [128, S], BF16); A0b = kb.buf("A0")
    A1 = kb.sb("A1", [128, S], BF16); A1b = kb.buf("A1")
    A2 = kb.sb("A2", [128, S], BF16); A2b = kb.buf("A2")
    A3 = kb.sb("A3", [128, NCH, 128], BF16); A3b = kb.buf("A3")
    A4 = kb.sb("A4", [128, NCH, 128], BF16); A4b = kb.buf("A4")
    SfB = kb.sb("SfB", [128, NCH, 128], BF16); SfBb = kb.buf("SfB")
    SbB = kb.sb("SbB", [128, NCH, 128], BF16); SbBb = kb.buf("SbB")
    Sst = kb.sb("Sst", [128, 2, 128], F32); Sstb = [kb.buf("Sf"), kb.buf("Sb")]
    it = kb.sb("it", [128, 4 * 128 + 2], F32); itb = kb.buf("it")
    dc = kb.sb("dc", [128, NU, 2], F32); dcb = kb.buf("dc")
    rg_ = kb.sb("rg_", [128, 1], F32); rgb = kb.buf("rg")
    lgw = kb.sb("lgw", [128, 8], F32); lgwb = kb.buf("lgw")
    sc = kb.sb("sc", [128, 4], F32); scb = kb.buf("sc")
    MT4 = kb.sb("MT4", [128, 4, 128], F32); MT4b = kb.buf("MT4")
    qd4 = kb.sb("qd4", [128, 2, 4, 128], F32); qd4b = kb.buf("qd4")
    tmpm = kb.sb("tmpm", [128, 128], F32); tmpmb = kb.buf("tmpm")
    ksc = [kb.sb("ksc%d" % i, [128, 128], BF16) for i in range(4)]; kscb = [kb.buf() for _ in range(4)]
    PT = [kb.sb("PT%d" % i, [128, 4, 128], BF16) for i in range(2)]; PTb = [kb.buf() for _ in range(2)]
    qfb = [kb.sb("qfb%d" % i, [128, 2, 512], BF16) for i in range(2)]; qfbb = [kb.buf() for _ in range(2)]
    rstd = kb.sb("rstd", [128, 512], F32); rstdb = kb.buf("rstd")
    sq1 = kb.sb("sq1", [128, 512], BF16); sq1b = kb.buf("sq1")
    tmpo = kb.sb("tmpo", [128, 512], F32); tmpob = kb.buf("tmpo")
    ost = [kb.sb("ost%d" % i, [128, 512], BF16) for i in range(2)]; ostb = [kb.buf() for _ in range(2)]
    nbias = kb.sb("nbias", [128, 5, 5, 128], F32); nbiasb = kb.buf("nbias")
    Lb = [kb.sb("Lb%d" % i, [128, 5, 128], F32) for i in range(2)]; Lbb = [kb.buf() for _ in range(2)]
    En = [kb.sb("En%d" % i, [128, 5, 128], BF16) for i in range(2)]; Enb = [kb.buf() for _ in range(2)]
    outb = [kb.buf("out%d" % i) for i in range(2)]

    kb.dma(it[:], itab[:, :], writes=[itb], eng="pool")
    kb.dma(dc[:], dec[:, :, :], writes=[dcb], eng="pool")
    kb.dma(rg_[:], retg[:, :], writes=[rgb], eng="pool")
    oi = 0
    for u in range(NU):
        kb.dma(A0[:], fm5[u, 0], writes=[A0b], eng="sync")
        kb.dma(A1[:], fm5[u, 1], writes=[A1b], eng="sync")
        kb.dma(A2[:], fm5[u, 2], writes=[A2b], eng="sync")
        kb.dma(A3[:], tm3[u, 0].rearrange("(n p) d -> p n d", p=128), writes=[A3b], eng="sync")
        kb.dma(A4[:], tm3[u, 1].rearrange("(n p) d -> p n d", p=128), writes=[A4b], eng="sync")
        kb.op("act", lambda e, u=u: e.activation(out=lgw[:, 0:2], in_=dc[:, u, :], func=AF.Exp, scale=-1.0), [dcb], [lgwb])
        kb.op("dve", lambda e: e.tensor_scalar(out=lgw[:, 2:4], in0=lgw[:, 0:2], scalar1=-1.0 / 8, scalar2=1.0 / 7, op0=ALU.mult, op1=ALU.add), [lgwb], [lgwb])
        for cc in (6, 5, 4, 3, 2, 1):
            kb.op("dve", lambda e: e.tensor_tensor(out=lgw[:, 2:4], in0=lgw[:, 2:4], in1=lgw[:, 0:2], op=ALU.mult), [lgwb], [lgwb])
            kb.op("dve", lambda e, cc=cc: e.tensor_scalar(out=lgw[:, 2:4], in0=lgw[:, 2:4], scalar1=-1.0, scalar2=1.0 / cc, op0=ALU.mult, op1=ALU.add), [lgwb], [lgwb])
        kb.op("dve", lambda e: e.tensor_tensor(out=lgw[:, 2:4], in0=lgw[:, 2:4], in1=lgw[:, 0:2], op=ALU.mult), [lgwb], [lgwb])
        kb.op("dve", lambda e: e.tensor_scalar(out=lgw[:, 4:6], in0=lgw[:, 2:4], scalar1=-1.0, scalar2=None, op0=ALU.mult), [lgwb], [lgwb])
        lgf = lgw[:, 4:5]
        lgb = lgw[:, 5:6]
        kb.op("dve", lambda e: e.tensor_scalar(out=tmpm[:], in0=it[:, 0:128], scalar1=lgf, scalar2=None, op0=ALU.mult), [itb, lgwb], [tmpmb])
        kb.op("dve", lambda e: e.scalar_tensor_tensor(out=tmpm[:], in0=it[:, 128:256], scalar=lgb, in1=tmpm[:], op0=ALU.mult, op1=ALU.add), [itb, lgwb, tmpmb], [tmpmb])
        for r in range(4):
            kb.op("act", lambda e, r=r: e.activation(out=MT4[:, r, :], in_=tmpm[:], func=AF.Exp), [tmpmb], [MT4b])
            kb.op("act", lambda e, r=r: e.activation(out=qd4[:, 0, r, :], in_=it[:, 256:384], func=AF.Exp, scale=lgf), [itb, lgwb], [qd4b])
            kb.op("act", lambda e, r=r: e.activation(out=qd4[:, 1, r, :], in_=it[:, 384:512], func=AF.Exp, scale=lgb), [itb, lgwb], [qd4b])
        kb.op("act", lambda e: e.activation(out=sc[:, 0:1], in_=it[:, 512:513], func=AF.Exp, scale=lgf), [itb, lgwb], [scb])
        kb.op("act", lambda e: e.activation(out=sc[:, 1:2], in_=it[:, 513:514], func=AF.Exp, scale=lgb), [itb, lgwb], [scb])
        kb.op("act", lambda e: e.activation(out=sc[:, 2:4], in_=lgw[:, 4:6], func=AF.Exp, scale=128.0), [lgwb], [scb])
        ki = 0
        for d_ in range(2):
            kb.op("dve" if d_ == 0 else "pool", lambda e, d_=d_: e.memset(Sst[:, d_, :], 0.0), [], [Sstb[d_]])
        pbs = [None, None]
        for cnt in range(NCH):
            for d_, (SB_, SBb_) in enumerate(((SfB, SfBb), (SbB, SbBb))):
                n = cnt if d_ == 0 else NCH - 1 - cnt
                if cnt % 4 == 0:
                    pbs[d_] = cx.bank(exclude=tuple(x[1] for x in pbs if x is not None))
                pb, pbb = pbs[d_]
                ks, ksb = ksc[ki % 4], kscb[ki % 4]
                ki += 1
                kb.op("pool", lambda e, n=n, ks=ks, d_=d_: e.tensor_scalar(out=ks[:], in0=A3[:, n, :], scalar1=sc[:, d_:d_ + 1], scalar2=None, op0=ALU.mult),
                      [A3b, scb], [ksb])
                col = (cnt % 4) * 128
                kb.op("pe", lambda e, n=n, ks=ks, pb=pb, col=col: e.matmul(pb[:, col:col + 128], ks[:], A4[:, n, :], start=True, stop=True), [ksb, A4b], [pbb])
                kb.op("act", lambda e, n=n, d_=d_, SB_=SB_: e.copy(out=SB_[:, n, :], in_=Sst[:, d_, :]), [Sstb[d_]], [SBb_])
                kb.op("dve", lambda e, d_=d_, pb=pb, col=col: e.scalar_tensor_tensor(out=Sst[:, d_, :], in0=Sst[:, d_, :], scalar=sc[:, 2 + d_:3 + d_],
                                                                                 in1=pb[:, col:col + 128], op0=ALU.mult, op1=ALU.add),
                      [Sstb[d_], scb, pbb], [Sstb[d_]])
        for g in range(NCH // 4):
            gs = slice(g * 512, (g + 1) * 512)
            pS, pSb = cx.bank()
            for r in range(4):
                n = g * 4 + r
                kb.op("pe", lambda e, n=n, r=r, pS=pS: e.matmul(pS[:, r * 128:(r + 1) * 128], A1[:, n * 128:(n + 1) * 128], A0[:, n * 128:(n + 1) * 128],
                                                           start=True, stop=True), [A0b, A1b], [pSb])
            pt, ptb = PT[g % 2], PTb[g % 2]
            kb.op("dve", lambda e, pS=pS, pt=pt: e.tensor_tensor(out=pt[:], in0=pS[:, :].rearrange("p (r i) -> p r i", r=4), in1=MT4[:], op=ALU.mult),
                  [pSb, MT4b], [ptb])
            qf, qfb_ = qfb[g % 2], qfbb[g % 2]
            for d_ in range(2):
                kb.op("pool", lambda e, d_=d_, qf=qf, gs=gs: e.tensor_tensor(out=qf[:, d_, :], in0=A0[:, gs], in1=qd4[:, d_].rearrange("p r i -> p (r i)"), op=ALU.mult),
                      [A0b, qd4b], [qfb_])
            pO, pOb = cx.bank()
            for r in range(4):
                n = g * 4 + r
                cs = slice(r * 128, (r + 1) * 128)
                kb.op("pe", lambda e, n=n, r=r, cs=cs, pO=pO, pt=pt: e.matmul(pO[:, cs], A4[:, n, :], pt[:, r, :], start=True, stop=False), [A4b, ptb], [pOb])
                kb.op("pe", lambda e, n=n, cs=cs, pO=pO, qf=qf: e.matmul(pO[:, cs], SfB[:, n, :], qf[:, 0, cs], start=False, stop=False), [SfBb, qfb_], [pOb])
                kb.op("pe", lambda e, n=n, cs=cs, pO=pO, qf=qf: e.matmul(pO[:, cs], SbB[:, n, :], qf[:, 1, cs], start=False, stop=True), [SbBb, qfb_], [pOb])
            emit_headnorm(cx, pO, pOb, rg_[:, 0:1], rgb, tmpo[:], tmpob, sq1, sq1b, rstd, rstdb, 512)
            os_, osb_, ob_ = ost[oi % 2], ostb[oi % 2], outb[oi % 2]
            oi += 1
            kb.op("dve", lambda e, os_=os_, gs=gs: e.tensor_tensor(out=os_[:], in0=tmpo[:], in1=A2[:, gs], op=ALU.mult), [tmpob, A2b], [osb_])
            kb.dma(outT[u, 0, :, gs], os_[:], reads=[osb_], writes=[ob_], eng="pool")
        kb.dma(A0[:], fm5[u, 3], writes=[A0b], eng="sync")
        kb.dma(A1[:], fm5[u, 4], writes=[A1b], eng="sync")
        kb.dma(A4[:], tm3[u, 2].rearrange("(n p) d -> p n d", p=128), writes=[A4b], eng="sync")
        kb.dma(nbias[:], nab[u], writes=[nbiasb], eng="sync")
        li = 0
        for pg in range(NCH // 4):
            gs = slice(pg * 512, (pg + 1) * 512)
            pO, pOb = cx.bank()
            pD, pDb = cx.bank()
            for r in range(4):
                p = pg * 4 + r
                kt0 = min(max(p - 2, 0), 59)
                var = {0: 0, 1: 1, 62: 3, 63: 4}.get(p, 2)
                pSa, pSab = cx.bank(exclude=(pOb, pDb))
                pSc, pScb = cx.bank(exclude=(pOb, pDb))
                for a in range(5):
                    tgt = pSa[:, a * 128:(a + 1) * 128] if a < 4 else pSc[:, 0:128]
                    tb_ = pSab if a < 4 else pScb
                    kb.op("pe", lambda e, tgt=tgt, a=a, kt0=kt0, p=p: e.matmul(tgt, A1[:, (kt0 + a) * 128:(kt0 + a + 1) * 128], A0[:, p * 128:(p + 1) * 128],
                                                                         start=True, stop=True), [A0b, A1b], [tb_])
                L, Lb_ = Lb[li % 2], Lbb[li % 2]
                E_, Eb_ = En[li % 2], Enb[li % 2]
                li += 1
                kb.op("dve", lambda e, L=L, pSa=pSa, var=var: e.tensor_tensor(out=L[:, 0:4, :], in0=pSa[:, :].rearrange("p (a q) -> p a q", a=4),
                                                                           in1=nbias[:, var, 0:4, :], op=ALU.add), [pSab, nbiasb], [Lb_])
                kb.op("dve", lambda e, L=L, pSc=pSc, var=var: e.tensor_tensor(out=L[:, 4, :], in0=pSc[:, 0:128], in1=nbias[:, var, 4, :], op=ALU.add),
                      [pScb, nbiasb], [Lb_])
                kb.op("act", lambda e, L=L, E_=E_: e.activation(out=E_[:], in_=L[:], func=AF.Exp), [Lb_], [Eb_])
                cs = slice(r * 128, (r + 1) * 128)
                for a in range(5):
                    kb.op("pe", lambda e, a=a, kt0=kt0, cs=cs, pO=pO, E_=E_: e.matmul(pO[:, cs], A4[:, kt0 + a, :], E_[:, a, :], start=(a == 0), stop=(a == 4)),
                          [A4b, Eb_], [pOb])
                for a in range(5):
                    kb.op("pe", lambda e, a=a, cs=cs, pD=pD, E_=E_: e.matmul(pD[:, cs], cx.ones[:], E_[:, a, :], start=(a == 0), stop=(a == 4)),
                          [cx.onesb, Eb_], [pDb])
            kb.op("dve", lambda e, pD=pD: e.reciprocal(out=rstd[:, :], in_=pD[:, :]), [pDb], [rstdb])
            os_, osb_, ob_ = ost[oi % 2], ostb[oi % 2], outb[oi % 2]
            oi += 1
            kb.op("dve", lambda e, os_=os_, pO=pO: e.tensor_tensor(out=os_[:], in0=pO[:, :], in1=rstd[:, :], op=ALU.mult), [pOb, rstdb], [osb_])
            kb.dma(outT[u, 1, :, gs], os_[:], reads=[osb_], writes=[ob_], eng="pool")
    return kb.emit(outb)


def retention_itab():
    j = np.arange(128, dtype=np.float32)[:, None]
    i = np.arange(128, dtype=np.float32)[None, :]
    A = np.maximum(i - j, 0.0)
    B = np.maximum(j - i, 0.0)
    r1 = np.broadcast_to(i + 1.0, (128, 128))
    r2 = np.broadcast_to(128.0 - i, (128, 128))
    return np.ascontiguousarray(np.concatenate([A, B, r1, r2, 127.0 - j, j], 1).astype(np.float32))


def na_bias_tables(rpb_h):
    out = np.full((5, 5, 128, 128), -30000.0, np.float32)
    kk = np.arange(128)
    for vi, p in enumerate((0, 1, 30, 62, 63)):
        kt0 = min(max(p - 2, 0), 59)
        rq = 2 * p + kk // 64
        cq = kk % 64
        rs = np.clip(rq - 4, 0, 120)
        cs_ = np.clip(cq - 8, 0, 48)
        for a in range(5):
            rk = 2 * (kt0 + a) + kk // 64
            ck = kk % 64
            okr = (rk[:, None] >= rs[None, :]) & (rk[:, None] < rs[None, :] + 8)
            okc = (ck[:, None] >= cs_[None, :]) & (ck[:, None] < cs_[None, :] + 16)
            dr = np.clip(rk[:, None] - rq[None, :] + 7, 0, 14)
            dcx = np.clip(ck[:, None] - cq[None, :] + 15, 0, 30)
            vals = rpb_h[dr, dcx]
            out[vi, a] = np.where(okr & okc, vals, np.float32(-30000.0))
    return np.ascontiguousarray(out.transpose(2, 0, 1, 3))


SC = float(128 ** -0.5)
EPI_EVEN = [("copy",)] * 8 + [("scale", SC)] * 8 + [("silu",)] * 8 + [("norm", 0, SC)] * 8 + [("norm", 1, 1.0)] * 8
TMS_EVEN = [SC, SC, 1.0, 1.0, 1.0, 1.0]
EPI_ODD = [("norm", 0, SC)] * 16 + [("norm", 1, 1.0)] * 16
TMS_ODD = [1.0] * 4


def _prog(key, fn):
    if key not in _CACHE:
        _CACHE[key] = fn()
    return _CACHE[key]


def _run(nc, in_maps):
    res = run_bass_kernel_spmd(nc, in_maps, core_ids=list(range(NCORES)))
    return res.results


def _lay_f1(W):
    g = W[:, :FH].reshape(KC, 128, FC, 128)
    u = W[:, FH:].reshape(KC, 128, FC, 128)
    return np.ascontiguousarray(np.stack([g, u], 0).transpose(3, 2, 1, 0, 4))


def _lay_tm(W):
    G = W.shape[1] // 512
    return np.ascontiguousarray(W.reshape(KC, 128, G, 512).transpose(2, 1, 0, 3))


def _bc(a, shape):
    return np.ascontiguousarray(np.broadcast_to(a, shape)).astype(np.float32)


def kernel(x, mem, norm_mix_g, norm_xattn_g, norm_mem_g, norm_ffn_g,
           w_in_ab, ret_decay_fwd, ret_decay_bwd, ret_out_g, na_q_g, na_k_g, na_rpb, w_out_ab,
           w_in_c, diff_q_g, diff_k_g, lambda_q1, lambda_k1, lambda_q2, lambda_k2, diff_out_g, w_out_c,
           w_xq, w_xkv, w_xo, xq_g, xk_g, w_ffn_in, w_ffn_out):
    f = lambda a: np.asarray(a, dtype=np.float32)
    x = f(x); mem = f(mem)
    DEPTH = 4
    lay = {}
    for i in range(DEPTH):
        j = i // 2
        if i % 2 == 0:
            W = f(w_in_ab[j])
            lay[("fm", i)] = lay_lhsT(np.concatenate([W[:, 0:1024], W[:, 1024:2048], W[:, 3072:4096], W[:, 4096:5120], W[:, 5120:6144]], 1))
            lay[("tm", i)] = _lay_tm(np.concatenate([W[:, 1024:2048], W[:, 2048:3072], W[:, 6144:7168]], 1))
            lay[("out", i)] = lay_lhsT(f(w_out_ab[j]))
        else:
            W = f(w_in_c[j])
            lay[("fm", i)] = lay_lhsT(W[:, 0:4096])
            lay[("tm", i)] = _lay_tm(W[:, 4096:6144])
            lay[("out", i)] = lay_lhsT(f(w_out_c[j]))
        lay[("xq", i)] = lay_lhsT(f(w_xq[i]))
        lay[("xk", i)] = lay_lhsT(f(w_xkv[i])[:, :512])
        lay[("xv", i)] = lay_rhs(f(w_xkv[i])[:, 512:])
        lay[("xo", i)] = lay_lhsT(f(w_xo[i]))
        lay[("f1", i)] = _lay_f1(f(w_ffn_in[i]))
        lay[("f2", i)] = lay_lhsT(f(w_ffn_out[i]))
    keys = list(lay.keys())
    sizes = [lay[k].size for k in keys]
    total = sum(sizes)
    unit = NCORES * 128 * 4096
    tot_pad = ((total + unit - 1) // unit) * unit
    flat = np.zeros(tot_pad, np.float32)
    off = 0
    for k, n in zip(keys, sizes):
        flat[off:off + n] = lay[k].ravel()
        off += n
    shards = flat.reshape(NCORES, 128, -1)
    outs = run_cast([shards[c] for c in range(NCORES)])
    flat_b = np.concatenate([np.asarray(o).reshape(-1) for o in outs])
    wb = {}
    off = 0
    for k, n in zip(keys, sizes):
        wb[k] = flat_b[off:off + n].reshape(lay[k].shape)
        off += n
    del flat, lay

    xs = x.reshape(NCORES, T, D)
    xT = [np.ascontiguousarray(xs[c].T.reshape(KC, 128, T)) for c in range(NCORES)]
    memT = [np.ascontiguousarray(mem[b].T.reshape(KC, 128, NM)) for b in range(2)]
    ident = np.eye(128, dtype=np.float32).astype(NPBF)
    itab = retention_itab()

    for i in range(DEPTH):
        j = i // 2
        even = (i % 2 == 0)
        if even:
            nc = _prog("P_even", lambda: build_P(EPI_EVEN, TMS_EVEN))
            gains = np.concatenate([lay_gain(f(norm_mix_g[i])), f(na_q_g[j])[:, None], f(na_k_g[j])[:, None]], 1)
        else:
            nc = _prog("P_odd", lambda: build_P(EPI_ODD, TMS_ODD))
            gains = np.concatenate([lay_gain(f(norm_mix_g[i])), f(diff_q_g[j])[:, None], f(diff_k_g[j])[:, None]], 1)
        gains = np.ascontiguousarray(gains.astype(np.float32))
        res = _run(nc, [{"xT": xT[c], "w_fm": wb[("fm", i)], "w_tm": wb[("tm", i)], "gains": gains} for c in range(NCORES)])
        fmT = [np.asarray(r["fmT"]) for r in res]
        tm = [np.asarray(r["tm"]) for r in res]
        mixT = [np.empty((KC, 128, T), NPBF) for _ in range(NCORES)]
        if even:
            nc = _prog("M_even", build_Meven)
            maps = []
            for c in range(NCORES):
                b = c // 4
                fm5 = np.empty((NU, 5, 128, S), NPBF)
                tm3 = np.empty((NU, 3, S, 128), NPBF)
                for u in range(NU):
                    h = (c % 4) * 2 + u
                    for q in range(4):
                        src = b * 4 + q
                        for t_ in range(5):
                            fm5[u, t_, :, q * T:(q + 1) * T] = fmT[src][t_ * 8 + h]
                        for t_ in range(3):
                            tm3[u, t_, q * T:(q + 1) * T, :] = tm[src][:, t_ * 1024 + h * 128:t_ * 1024 + (h + 1) * 128]
                hs = [(c % 4) * 2, (c % 4) * 2 + 1]
                dec = _bc(np.stack([f(ret_decay_fwd[j])[hs], f(ret_decay_bwd[j])[hs]], 1)[None], (128, NU, 2))
                nab = np.stack([na_bias_tables(f(na_rpb[j])[h]) for h in hs])
                maps.append({"fm5": fm5, "tm3": tm3, "dec": dec, "itab": itab, "retg": np.ascontiguousarray(f(ret_out_g[j])[:, None]), "nab": nab})
            res = _run(nc, maps)
            for c in range(NCORES):
                b, q = c // 4, c % 4
                for h in range(8):
                    o = np.asarray(res[b * 4 + h // 2]["mixT"])
                    mixT[c][h] = o[h % 2, 0, :, q * T:(q + 1) * T]
                    mixT[c][8 + h] = o[h % 2, 1, :, q * T:(q + 1) * T]
        else:
            nc = _prog("M_odd", build_Modd)
            lam_init = 0.8 - 0.6 * float(np.exp(-0.3 * i))
            lam4 = _bc(np.stack([f(lambda_q1[j]), f(lambda_k1[j]), f(lambda_q2[j]), f(lambda_k2[j])])[None], (128, 4, 128))
            gout = _bc(f(diff_out_g[j])[None], (128, 256))
            consts = _bc(np.array([lam_init, 1.0 - lam_init], np.float32)[None], (128, 2))
            maps = []
            for c in range(NCORES):
                b = c // 4
                qTa = np.empty((NU, 2, 128, S), NPBF)
                kTa = np.empty((NU, 2, 128, S), NPBF)
                va = np.empty((NU, S, 256), NPBF)
                ats, bts = [], []
                for u in range(NU):
                    h = u * 4 + (c % 4)
                    a_, b_ = alibi_tables(2.0 ** (-(h + 1)))
                    ats.append(a_); bts.append(b_)
                    for q in range(4):
                        src = b * 4 + q
                        for cc in range(2):
                            qTa[u, cc, :, q * T:(q + 1) * T] = fmT[src][h * 2 + cc]
                            kTa[u, cc, :, q * T:(q + 1) * T] = fmT[src][16 + h * 2 + cc]
                        va[u, q * T:(q + 1) * T, :] = tm[src][:, h * 256:(h + 1) * 256]
                maps.append({"qT": qTa, "kT": kTa, "v": va, "atab": np.stack(ats), "btile": np.stack(bts), "ident": ident,
                             "lam4": lam4, "gout": gout, "consts": consts})
            res = _run(nc, maps)
            for c in range(NCORES):
                b, q = c // 4, c % 4
                for h in range(8):
                    o = np.asarray(res[b * 4 + h % 4]["d_tm"])[h // 4, q * T:(q + 1) * T, :]
                    mixT[c][2 * h] = o[:, 0:128].T
                    mixT[c][2 * h + 1] = o[:, 128:256].T
        nc = _prog("C", build_C)
        gains = np.ascontiguousarray(np.concatenate([lay_gain(f(norm_xattn_g[i])), lay_gain(f(norm_ffn_g[i])), lay_gain(f(norm_mem_g[i])),
                                                     f(xq_g[i])[:, None], f(xk_g[i])[:, None]], 1).astype(np.float32))
        res = _run(nc, [{"xT": xT[c], "mixT": mixT[c], "memT": memT[c // 4], "w_out": wb[("out", i)], "w_xq": wb[("xq", i)],
                         "w_xk": wb[("xk", i)], "w_xv": wb[("xv", i)], "w_xo": wb[("xo", i)], "w_f1": wb[("f1", i)],
                         "w_f2": wb[("f2", i)], "gains": gains} for c in range(NCORES)])
        xT = [np.asarray(r["xTo"]) for r in res]
    out = np.stack([xT[c].reshape(D, T).T for c in range(NCORES)]).reshape(2, S, D)
    return np.ascontiguousarray(out.astype(np.float32))
```
